# Optimizing a Trainium2 kernel written in Bass

```python
import jax, jax.numpy as jnp
from jax import lax
import numpy as np

D_MODEL = 1024
BATCH = 16
SEQ = 256
DEPTH = 2
DEC_BATCH = 8
DEC_SEQ = 1024
PAST_LEN = 512

GRID_W = 64
N_HEADS = 16
HEAD_DIM = D_MODEL // N_HEADS
ATTN_SCALE = HEAD_DIM ** -0.5
WIN_ROWS = 8
WIN_COLS = 16
Q_BLOCK = 128
D_LRU = D_MODEL
LRU_BLOCKS = 8
LRU_BLOCK_W = D_LRU // LRU_BLOCKS
CONV_W = 4
LRU_C = 8.0
D_FF = -(-(8 * D_MODEL) // (3 * 256)) * 256
N_MOD = 6
N_ATTN = (DEPTH + 1) // 2
N_LRU = DEPTH // 2
RMS_EPS = 1e-6
NEG_INF = -1e30

kernel_name = 'hybrid_na_rglru_flow_step'


def _rms_norm(x, g):
    xf = x.astype(jnp.float32)
    y = xf * lax.rsqrt(jnp.mean(xf * xf, axis=-1, keepdims=True) + RMS_EPS)
    return (y * g.astype(jnp.float32)).astype(x.dtype)


def _adaln(cond, w_mod, b_mod):
    m = jax.nn.silu(cond) @ w_mod + b_mod
    return jnp.split(m, N_MOD, axis=-1)


def _modulate(x, g, shift, scale):
    return _rms_norm(x, g) * (1 + scale[:, None, :]) + shift[:, None, :]


def _swiglu(x, w_gu, w_down):
    gate, up = jnp.split(x @ w_gu, 2, axis=-1)
    return (jax.nn.silu(gate) * up) @ w_down


def _qkv(xm, w_qkv):
    B, T, _ = xm.shape
    q, k, v = jnp.split(xm @ w_qkv, 3, axis=-1)
    shp = (B, T, N_HEADS, HEAD_DIM)
    return q.reshape(shp), k.reshape(shp), v.reshape(shp)


def _context_self_attention(q, k, v):
    B, L, H, hd = q.shape
    nb = L // Q_BLOCK
    qb = jnp.moveaxis(q.reshape(B, nb, Q_BLOCK, H, hd), 1, 0)

    def block(qblk):
        s = jnp.einsum('bqhd,bkhd->bhqk', qblk * ATTN_SCALE, k).astype(jnp.float32)
        p = jax.nn.softmax(s, axis=-1).astype(v.dtype)
        return jnp.einsum('bhqk,bkhd->bqhd', p, v)

    o = lax.map(block, qb)
    return jnp.moveaxis(o, 0, 1).reshape(B, L, H * hd)


def _neighbourhood_attention(q, k, v, ck, cv, rpb):
    B, T, H, hd = q.shape
    rows = T // GRID_W
    kh = min(WIN_ROWS, rows)
    r = jnp.arange(rows)
    row_start = jnp.clip(r - kh // 2, 0, rows - kh)
    row_idx = row_start[:, None] + jnp.arange(kh)[None, :]
    col = jnp.arange(GRID_W)
    col_start = jnp.clip(col - WIN_COLS // 2, 0, GRID_W - WIN_COLS)
    in_win = (col[None, :] >= col_start[:, None]) & (col[None, :] < col_start[:, None] + WIN_COLS)
    dr = row_idx - r[:, None] + (WIN_ROWS - 1)
    dc = jnp.clip(col[None, :] - col[:, None], -(WIN_COLS - 1), WIN_COLS - 1) + (WIN_COLS - 1)
    bias = rpb.astype(jnp.float32)[:, dr[:, None, :, None], dc[None, :, None, :]]
    bias = jnp.where(in_win[None, None, :, None, :], bias, NEG_INF)
    qg = q.reshape(B, rows, GRID_W, H, hd) * ATTN_SCALE
    kg = k.reshape(B, rows, GRID_W, H, hd)[:, row_idx]
    vg = v.reshape(B, rows, GRID_W, H, hd)[:, row_idx]
    s_lat = jnp.einsum('brqhd,brjkhd->bhrqjk', qg, kg).astype(jnp.float32) + bias[None]
    s_ctx = jnp.einsum('brqhd,bchd->bhrqc', qg, ck).astype(jnp.float32)
    n_lat = kh * GRID_W
    s = jnp.concatenate([s_lat.reshape(B, H, rows, GRID_W, n_lat), s_ctx], axis=-1)
    p = jax.nn.softmax(s, axis=-1).astype(v.dtype)
    p_lat = p[..., :n_lat].reshape(B, H, rows, GRID_W, kh, GRID_W)
    p_ctx = p[..., n_lat:]
    o = (jnp.einsum('bhrqjk,brjkhd->brqhd', p_lat, vg)
         + jnp.einsum('bhrqc,bchd->brqhd', p_ctx, cv))
    return o.reshape(B, T, H * hd)


def _centred_depthwise_conv(x, w, b):
    T = x.shape[1]
    left = CONV_W // 2
    xp = jnp.pad(x, ((0, 0), (left, CONV_W - 1 - left), (0, 0)))
    y = b
    for j in range(CONV_W):
        y = y + xp[:, j:j + T] * w[j]
    return y


def _rglru_coeffs(xc, w_a, b_a, w_i, b_i, lam):
    B, T, D = xc.shape
    xb = xc.reshape(B, T, LRU_BLOCKS, LRU_BLOCK_W)
    r = jax.nn.sigmoid(jnp.einsum('btnk,nkj->btnj', xb, w_a).reshape(B, T, D) + b_a)
    i = jax.nn.sigmoid(jnp.einsum('btnk,nkj->btnj', xb, w_i).reshape(B, T, D) + b_i)
    log_a = -LRU_C * r * jax.nn.softplus(-lam.astype(jnp.float32))
    a = jnp.exp(log_a)
    b = jnp.sqrt(-jnp.expm1(2.0 * log_a)) * (i * xc)
    return a.astype(jnp.float32), b.astype(jnp.float32)


def _linear_scan(a, b, h0, reverse):
    def step(h, ab):
        a_t, b_t = ab
        h = a_t * h + b_t
        return h, h

    h_last, hs = lax.scan(step, h0, (jnp.swapaxes(a, 0, 1), jnp.swapaxes(b, 0, 1)), reverse=reverse)
    return jnp.swapaxes(hs, 0, 1), h_last


def _lru_mixer(xm, h0, w_in, conv_w, conv_b, w_a, b_a, w_i, b_i, lam, w_out):
    gate, xr = jnp.split(xm @ w_in, 2, axis=-1)
    xc = _centred_depthwise_conv(xr, conv_w, conv_b).astype(jnp.float32)
    h0 = h0.astype(jnp.float32)
    a_f, b_f = _rglru_coeffs(xc, w_a[0], b_a[0], w_i[0], b_i[0], lam[0])
    a_b, b_b = _rglru_coeffs(xc, w_a[1], b_a[1], w_i[1], b_i[1], lam[1])
    hs_f, hT_f = _linear_scan(a_f, b_f, h0[:, 0], reverse=False)
    hs_b, hT_b = _linear_scan(a_b, b_b, h0[:, 1], reverse=True)
    y = (hs_f + hs_b).astype(xm.dtype) * jax.nn.gelu(gate)
    return y @ w_out, jnp.stack([hT_f, hT_b], axis=1)


def setup_inputs(seed: int = 0) -> dict:
    key = jax.random.key(seed)
    ks = jax.random.split(key, 32)
    f32 = jnp.float32

    def nrm(k, shape, scale):
        return jax.random.normal(k, shape, f32) * scale

    D = D_MODEL
    u = jax.random.uniform(ks[20], (N_LRU, 2, D_LRU), f32, 0.9, 0.999)
    a0 = u ** (1.0 / LRU_C)
    return {
        'x_prompt': nrm(ks[0], (BATCH, SEQ, D), 1.0),
        'x_sample': nrm(ks[1], (DEC_BATCH, DEC_SEQ, D), 1.0),
        'c': nrm(ks[2], (DEC_BATCH, D), 1.0),
        'cache_k': nrm(ks[3], (DEC_BATCH, N_ATTN, PAST_LEN, N_HEADS, HEAD_DIM), 1.0),
        'cache_v': nrm(ks[4], (DEC_BATCH, N_ATTN, PAST_LEN, N_HEADS, HEAD_DIM), 1.0),
        'state_h': nrm(ks[5], (DEC_BATCH, N_LRU, 2, D_LRU), 0.5),
        'c_ctx': nrm(ks[6], (D,), 1.0),
        'norm_g': 1.0 + nrm(ks[7], (DEPTH, 2, D), 0.02),
        'w_mod': nrm(ks[8], (DEPTH, D, N_MOD * D), 0.5 * D ** -0.5),
        'b_mod': nrm(ks[9], (DEPTH, N_MOD * D), 0.02),
        'attn_w_qkv': nrm(ks[10], (N_ATTN, D, 3 * D), D ** -0.5),
        'attn_w_o': nrm(ks[11], (N_ATTN, D, D), D ** -0.5),
        'attn_rpb': nrm(ks[12], (N_ATTN, N_HEADS, 2 * WIN_ROWS - 1, 2 * WIN_COLS - 1), 0.1),
        'lru_w_in': nrm(ks[13], (N_LRU, D, 2 * D_LRU), D ** -0.5),
        'lru_conv_w': nrm(ks[14], (N_LRU, CONV_W, D_LRU), CONV_W ** -0.5),
        'lru_conv_b': nrm(ks[15], (N_LRU, D_LRU), 0.02),
        'lru_w_a': nrm(ks[16], (N_LRU, 2, LRU_BLOCKS, LRU_BLOCK_W, LRU_BLOCK_W), LRU_BLOCK_W ** -0.5),
        'lru_b_a': nrm(ks[17], (N_LRU, 2, D_LRU), 0.02),
        'lru_w_i': nrm(ks[18], (N_LRU, 2, LRU_BLOCKS, LRU_BLOCK_W, LRU_BLOCK_W), LRU_BLOCK_W ** -0.5),
        'lru_b_i': nrm(ks[19], (N_LRU, 2, D_LRU), 0.02),
        'lru_lam': jnp.log(a0) - jnp.log1p(-a0),
        'lru_w_out': nrm(ks[21], (N_LRU, D_LRU, D), D_LRU ** -0.5),
        'ffn_w_gu': nrm(ks[22], (DEPTH, D, 2 * D_FF), D ** -0.5),
        'ffn_w_down': nrm(ks[23], (DEPTH, D_FF, D), D_FF ** -0.5),
        'final_g': 1.0 + nrm(ks[24], (D,), 0.02),
    }


def reference(x_prompt, x_sample, c, cache_k, cache_v, state_h, c_ctx, norm_g, w_mod, b_mod,
              attn_w_qkv, attn_w_o, attn_rpb, lru_w_in, lru_conv_w, lru_conv_b, lru_w_a, lru_b_a,
              lru_w_i, lru_b_i, lru_lam, lru_w_out, ffn_w_gu, ffn_w_down, final_g):
    ctx = x_prompt
    lat = x_sample
    new_k, new_v, new_h = [], [], []
    for i in range(DEPTH):
        j = i // 2
        sh1_c, sc1_c, g1_c, sh2_c, sc2_c, g2_c = _adaln(c_ctx[None, :], w_mod[i], b_mod[i])
        sh1_l, sc1_l, g1_l, sh2_l, sc2_l, g2_l = _adaln(c, w_mod[i], b_mod[i])
        xm_c = _modulate(ctx, norm_g[i, 0], sh1_c, sc1_c)
        xm_l = _modulate(lat, norm_g[i, 0], sh1_l, sc1_l)
        if i % 2 == 0:
            q_c, k_c, v_c = _qkv(xm_c, attn_w_qkv[j])
            o_c = _context_self_attention(q_c, k_c, v_c) @ attn_w_o[j]
            q_l, k_l, v_l = _qkv(xm_l, attn_w_qkv[j])
            o_l = _neighbourhood_attention(q_l, k_l, v_l, cache_k[:, j], cache_v[:, j], attn_rpb[j]) @ attn_w_o[j]
            new_k.append(k_c)
            new_v.append(v_c)
        else:
            lru_p = (lru_w_in[j], lru_conv_w[j], lru_conv_b[j], lru_w_a[j], lru_b_a[j],
                     lru_w_i[j], lru_b_i[j], lru_lam[j], lru_w_out[j])
            h0_c = jnp.zeros((ctx.shape[0], 2, D_LRU), jnp.float32)
            o_c, h_c = _lru_mixer(xm_c, h0_c, *lru_p)
            o_l, _ = _lru_mixer(xm_l, state_h[:, j], *lru_p)
            new_h.append(h_c)
        ctx = ctx + g1_c[:, None, :] * o_c
        lat = lat + g1_l[:, None, :] * o_l
        f_c = _swiglu(_modulate(ctx, norm_g[i, 1], sh2_c, sc2_c), ffn_w_gu[i], ffn_w_down[i])
        f_l = _swiglu(_modulate(lat, norm_g[i, 1], sh2_l, sc2_l), ffn_w_gu[i], ffn_w_down[i])
        ctx = ctx + g2_c[:, None, :] * f_c
        lat = lat + g2_l[:, None, :] * f_l
    y_prompt = _rms_norm(ctx, final_g)
    y_sample = _rms_norm(lat, final_g)
    return (y_prompt, y_sample, jnp.stack(new_k, axis=1), jnp.stack(new_v, axis=1), jnp.stack(new_h, axis=1))
```

```python
import numpy as np
from contextlib import ExitStack
import concourse.bass as bass
import concourse.mybir as mybir
from concourse.bass_utils import run_bass_kernel_spmd

F32 = mybir.dt.float32
BF16 = mybir.dt.bfloat16
AF = mybir.ActivationFunctionType
ALU = mybir.AluOpType

D = 1024
KC = 8
NT = 1536
DFF = 2816
FC = 22
NCORES = 8
EPS = 1e-6


class Sched:
    ENG = ("pe", "act", "dve", "pool", "sp")

    def __init__(self, nc, es):
        self.nc = nc
        self.ops = {e: [] for e in self.ENG}
        self.cnt = {e: 0 for e in self.ENG}
        self.esem = {e: es.enter_context(nc.semaphore("s_" + e)) for e in self.ENG}
        self.dsem = {q: [es.enter_context(nc.semaphore("d_%s%d" % (q, i))) for i in range(n)]
                     for q, n in (("sp", 24), ("pool", 12))}
        self.dtgt = {}
        self.drr = {"sp": 0, "pool": 0}
        self.lastw = {}
        self.rd = {}
        self.seen = {e: {} for e in self.ENG}
        self.pending = {e: {} for e in self.ENG}
        self.out_tokens = []

    def _deps(self, eng, idx, reads, writes, strict=False):
        waits = dict(self.pending[eng])
        self.pending[eng] = {}
        for sk in list(waits):
            if self.seen[eng].get(sk, 0) >= waits[sk]:
                del waits[sk]

        def need(tok, raw):
            sk, v, pe, pidx = tok
            if pe == eng and eng != "pool":
                if eng == "pe":
                    return
                if raw == "war":
                    return
            if self.seen[eng].get(sk, 0) >= v:
                return
            if waits.get(sk, 0) < v:
                waits[sk] = v

        for r in reads:
            t = self.lastw.get(r)
            if t:
                need(t, True)
        for w in writes:
            t = self.lastw.get(w)
            if t:
                need(t, "waw")
            for t in self.rd.get(w, {}).values():
                need(t, "war")
        for sk, v in waits.items():
            self.seen[eng][sk] = v
        return waits

    def _commit(self, tok, reads, writes):
        key = tok[2] if tok[2] else tok[0]
        for r in reads:
            self.rd.setdefault(r, {})[key] = tok
        for w in writes:
            self.lastw[w] = tok
            self.rd[w] = {}

    def op(self, eng, fn, reads=(), writes=()):
        idx = self.cnt[eng]
        waits = self._deps(eng, idx, reads, writes)
        self.cnt[eng] += 1
        tok = (("e", eng), idx + 1, eng, idx)
        self._commit(tok, reads, writes)
        self.ops[eng].append((list(waits.items()), fn, None))
        return tok

    def dma(self, q, fn, reads=(), writes=(), out=False):
        idx = self.cnt[q]
        waits = self._deps(q, idx, reads, writes, strict=True)
        k = self.drr[q]
        self.drr[q] += 1
        sk = ("d", q, k % len(self.dsem[q]))
        prev = self.dtgt.get(sk, 0)
        if prev and self.seen[q].get(sk, 0) < prev:
            waits[sk] = prev
            self.seen[q][sk] = prev
        self.dtgt[sk] = prev + 16
        tok = (sk, prev + 16, None, None)
        self._commit(tok, reads, writes)
        self.ops[q].append((list(waits.items()), fn, sk))
        if out:
            self.out_tokens.append(tok)
        return tok

    def barrier(self):
        for e in self.ENG:
            p = self.pending[e]
            for f in self.ENG:
                if f != e and self.cnt[f] > 0:
                    sk = ("e", f)
                    p[sk] = max(p.get(sk, 0), self.cnt[f])
            for sk, v in self.dtgt.items():
                p[sk] = max(p.get(sk, 0), v)

    def finish(self):
        waits = {}
        for sk, v in self.dtgt.items():
            waits[sk] = v
        for f in self.ENG:
            if f != "sp" and self.cnt[f] > 0:
                waits[("e", f)] = self.cnt[f]
        self.ops["sp"].append((list(waits.items()), None, None))

    def sem(self, sk):
        return self.esem[sk[1]] if sk[0] == "e" else self.dsem[sk[1]][sk[2]]

    def emit(self, eng, e):
        for waits, fn, dsk in self.ops[eng]:
            attach = None
            if eng != "pe" and fn is not None and waits:
                attach = waits[-1]
                waits = waits[:-1]
            for sk, v in waits:
                e.wait_ge(self.sem(sk), v)
            if fn is None:
                continue
            if eng == "pe":
                px = _PEProxy(e, self.sem(attach[0]), attach[1]) if attach is not None else e
                ins = fn(px)
            else:
                ins = fn(e)
                if attach is not None:
                    ins._wait_ge(self.sem(attach[0]), attach[1])
            if dsk is None:
                ins.then_inc(self.esem[eng], 1)
            else:
                ins.then_inc(self.sem(dsk), 16)


class _PEProxy:
    def __init__(self, e, sem, val):
        self.e, self.sem, self.val = e, sem, val

    def _first(self, ins):
        if self.sem is not None:
            ins._wait_ge(self.sem, self.val)
            self.sem = None
        return ins

    def matmul(self, *a, **k):
        return self._first(self.e.matmul(*a, **k))

    def transpose(self, *a, **k):
        return self._first(self.e.transpose(*a, **k))


def _prod(s):
    r = 1
    for v in s:
        r *= v
    return r


def carve(base, off, dtype, shape):
    esz = 4 if dtype == F32 else 2
    nb = _prod(shape[1:]) * esz
    assert off % 4 == 0 and nb % 4 == 0
    sl = base[:, off // 4:(off + nb) // 4]
    if dtype != F32:
        sl = sl.bitcast(dtype)
    if len(shape) == 2:
        return sl
    names = "abcd"[:len(shape) - 1]
    pat = "p (%s) -> p %s" % (" ".join(names), " ".join(names))
    kw = {names[i]: shape[i + 1] for i in range(1, len(names))}
    return sl.rearrange(pat, **kw)


FULL = dict(attn=True, ffn0=True, lru=True, ffn1=True)


def build(cfg=FULL):
    nc = bass.Bass("TRN2", target_bir_lowering=False)

    def din(name, shape, dt=F32):
        return nc.dram_tensor(name, shape, dt, kind="ExternalInput").ap()

    def dout(name, shape):
        return nc.dram_tensor(name, shape, F32, kind="ExternalOutput").ap()

    x_d = din("x", [NT, D])
    smalls_d = din("smalls", [256, 128])
    ident_d = din("ident", [128, 128])
    cm_d = din("cm", [128, 64])
    ck_d = din("ck", [512, D])
    cv_d = din("cv", [512, D])
    rpb_d = din("rpb", [16, 15, 31])
    wmod_d = din("w_mod", [2, D, 6144])
    wqkv_d = din("w_qkv", [D, 3072])
    wo_d = din("w_o", [D, D])
    win_d = din("w_in", [D, 2048])
    wa_d = din("w_a", [2, 8, 128, 128])
    wi_d = din("w_i", [2, 8, 128, 128])
    wout_d = din("w_out", [D, D])
    wgu_d = din("w_gu", [2, D, 2 * DFF])
    wdown_d = din("w_down", [2, DFF, D])
    y_d = dout("y", [NT, D])
    nk_d = dout("nk", [512, D])
    nv_d = dout("nv", [512, D])
    nh_d = dout("nh", [32, 128])
    epd_t = nc.dram_tensor("epd", [16 * 15 * 127 + 256], BF16, kind="Internal")
    epd = epd_t.ap()

    with ExitStack() as es:
        S = Sched(nc, es)

        def sb(name, shape, dt):
            return es.enter_context(nc.sbuf_tensor("sb_" + name, shape, dt))

        X = sb("X", [128, KC, NT], F32)
        WR = sb("WR", [128, 3, 4096], BF16)
        VT = sb("VT", [128, 256], F32)
        MOD = sb("MOD", [128, 2, 48, 2], F32)
        GS = sb("GS", [128, 2, 2, KC, 2], F32)
        ident = sb("ident", [128, 128], F32)
        ones = sb("ones", [128, 128], BF16)
        cmb = sb("cmb", [128, 64], BF16)
        scT = sb("scT", [128, KC, 2], BF16)
        nls = sb("nls", [128, 16], F32)
        ltmp = sb("ltmp", [128, 4, 16], F32)
        NH = sb("NH", [128, 32], F32)
        Dg = sb("Dg", [128, 4, 128], F32)
        ARENA_B = 122880
        STRIP = 10240
        arena = sb("arena", [128, (ARENA_B + STRIP) // 4], F32)
        rstd = [carve(arena, ARENA_B + i * 2048, F32, [128, 512]) for i in range(3)]
        tmpf = [carve(arena, ARENA_B + 6144 + i * 2048, F32, [128, 512]) for i in range(2)]
        xsq8 = carve(arena, 98304, BF16, [128, KC, 512])
        banks = [es.enter_context(nc.psum_tensor("bk%d" % i, [128, 512], F32)) for i in range(8)]

        state = {"bank": 0, "ev": 0, "xsq": 0, "tmp": 0, "kst": 0, "pt": 0}

        def bank(pool=(0, 1, 2, 3, 4, 5)):
            i = pool[state["bank"] % len(pool)]
            state["bank"] += 1
            return i

        def evac_eng():
            state["ev"] += 1
            return "act" if state["ev"] % 2 else "dve"

        def copy_op(eng, out, in_, reads, writes, scale=None):
            if eng == "act":
                if scale is None:
                    S.op("act", lambda e, o=out, i=in_: e.activation(out=o, in_=i, func=AF.Copy), reads, writes)
                else:
                    S.op("act", lambda e, o=out, i=in_, s=scale: e.activation(out=o, in_=i, func=AF.Copy, scale=s),
                         reads, writes)
            else:
                if scale is None:
                    S.op(eng, lambda e, o=out, i=in_: e.tensor_copy(out=o, in_=i), reads, writes)
                else:
                    S.op(eng, lambda e, o=out, i=in_, s=scale: e.tensor_scalar(out=o, in0=i, scalar1=s, scalar2=None,
                                                                               op0=ALU.mult), reads, writes)

        def B(i):
            return ("bank", i)

        def vt(col, n=1):
            return VT[:, col:col + n]

        COND = [0, 1, 1]

        plan = []

        def add_k8(w2d, c0):
            plan.append(("k8", (w2d, c0)))

        for_layers = []
        def mod_slabs(l, qs):
            for q in qs:
                plan.append(("k8", (wmod_d[l], q * 512)))

        mod_slabs(0, range(0, 4))
        if cfg["attn"]:
            for q in range(6):
                plan.append(("k8", (wqkv_d, q * 512)))
            mod_slabs(0, range(4, 6))
            for q in range(2):
                plan.append(("k8", (wo_d, q * 512)))
        else:
            mod_slabs(0, range(4, 6))
        mod_slabs(0, range(6, 10))
        if cfg["ffn0"]:
            for j in range(11):
                plan.append(("pair", (wgu_d[0], j * 256, DFF)))
            mod_slabs(0, range(10, 12))
            for m in range(8):
                plan.append(("down", (wdown_d[0], m * 128)))
        else:
            mod_slabs(0, range(10, 12))
        mod_slabs(1, range(0, 4))
        if cfg["lru"]:
            for j in range(4):
                plan.append(("pair", (win_d, j * 256, 1024)))
            mod_slabs(1, range(4, 6))
            for q in range(2):
                plan.append(("k8", (wout_d, q * 512)))
        else:
            mod_slabs(1, range(4, 6))
        mod_slabs(1, range(6, 10))
        if cfg["ffn1"]:
            for j in range(11):
                plan.append(("pair", (wgu_d[1], j * 256, DFF)))
            mod_slabs(1, range(10, 12))
            for m in range(8):
                plan.append(("down", (wdown_d[1], m * 128)))
        else:
            mod_slabs(1, range(10, 12))

        wstate = {"loaded": 0, "next": 0}

        def w_load(j):
            kind, a = plan[j]
            s = j % 3
            res = [("ws", s)]
            if kind == "k8":
                w2d, c0 = a
                src = w2d[:, c0:c0 + 512].rearrange("(kc p) n -> p kc n", p=128)
                dst = WR[:, s, :].rearrange("p (kc n) -> p kc n", n=512)
                S.dma("pool", lambda e, o=dst, i=src: e.dma_start(out=o, in_=i), writes=res)
            elif kind == "pair":
                w2d, c0, off = a
                dst = WR[:, s, :].rearrange("p (kc t n) -> p kc t n", t=2, n=256)
                for t in range(2):
                    src = w2d[:, t * off + c0:t * off + c0 + 256].rearrange("(kc p) n -> p kc n", p=128)
                    S.dma("pool", lambda e, o=dst[:, :, t, :], i=src: e.dma_start(out=o, in_=i), writes=res)
            elif kind == "down":
                w2d, c0 = a
                src = w2d[:, c0:c0 + 128].rearrange("(kc p) n -> p kc n", p=128)
                dst = WR[:, s, 0:FC * 128].rearrange("p (kc n) -> p kc n", n=128)
                S.dma("pool", lambda e, o=dst, i=src: e.dma_start(out=o, in_=i), writes=res)
            elif kind == "lrug":
                dst = WR[:, s, :].rearrange("p (w d n j) -> p w d n j", w=2, d=2, n=8)
                for wi_, wd in enumerate((wa_d, wi_d)):
                    for d in range(2):
                        src = wd[d].rearrange("n k j -> k n j")
                        S.dma("pool", lambda e, o=dst[:, wi_, d, :, :], i=src: e.dma_start(out=o, in_=i), writes=res)

        def w_get(kind, ahead=3):
            i = wstate["next"]
            wstate["next"] += 1
            assert plan[i][0] == kind, (i, plan[i][0], kind)
            while wstate["loaded"] < min(i + ahead, len(plan)):
                w_load(wstate["loaded"])
                wstate["loaded"] += 1
            return i % 3

        while wstate["loaded"] < min(3, len(plan)):
            w_load(wstate["loaded"])
            wstate["loaded"] += 1

        S0 = carve(arena, 0, F32, [128, 2, 128])
        cmf = carve(arena, 1024, F32, [128, 64])
        S.dma("sp", lambda e: e.dma_start(out=ident[:], in_=ident_d), writes=["ident"])
        S.dma("sp", lambda e: e.dma_start(out=S0, in_=smalls_d.rearrange("(t r) c -> r t c", t=2)), writes=["S0"])
        S.op("pool", lambda e: e.memset(ones[:], 1.0), writes=["ones"])
        S.dma("sp", lambda e: e.dma_start(out=cmf, in_=cm_d), writes=["cmf"])
        S.op("pool", lambda e: e.tensor_copy(out=cmb[:], in_=cmf), reads=["cmf"], writes=["cmb"])

        def f_sm(e):
            e.transpose(out=banks[6][:, 0:128], in_=S0[:, 0, :], identity=ident[:])
            return e.transpose(out=banks[6][:, 128:256], in_=S0[:, 1, :], identity=ident[:])

        S.op("pe", f_sm, reads=["ident", "S0"], writes=[B(6)])
        S.op("dve", lambda e: e.tensor_copy(out=VT[:], in_=banks[6][:, 0:256]), reads=[B(6)], writes=["VT"])
        C_BMOD, C_NG, C_FG, C_CW, C_CB, C_BA, C_BI, C_LAM, C_CP, C_H0 = 0, 96, 128, 136, 168, 176, 192, 208, 224, 240
        for cnd in range(2):
            S.op("act", lambda e, c=cnd: e.activation(out=scT[:, :, c], in_=VT[:, C_CP + c * 8:C_CP + c * 8 + 8],
                                                      func=AF.Silu), reads=["VT"], writes=["scT"])

        def do_mod(l, qs):
            for q in qs:
                s = w_get("k8")
                wv = WR[:, s, :].rearrange("p (kc n) -> p kc n", n=512)

                def f(e, wv=wv, q=q):
                    ins = None
                    for jj in range(4):
                        j = 4 * q + jj
                        for kc in range(KC):
                            ins = e.matmul(banks[7][:, 2 * j:2 * j + 2], lhsT=wv[:, kc, jj * 128:(jj + 1) * 128],
                                           rhs=scT[:, kc, :], start=(kc == 0), stop=(kc == KC - 1))
                    return ins

                S.op("pe", f, reads=[("ws", s), "scT"], writes=[B(7)])
                pv = banks[7][:, 0:96].rearrange("p (j c) -> p j c", c=2)
                for cnd in range(2):
                    S.op("dve", lambda e, q=q, c=cnd, l=l: e.tensor_tensor(
                        out=MOD[:, l, 4 * q:4 * q + 4, c], in0=pv[:, 4 * q:4 * q + 4, c],
                        in1=VT[:, C_BMOD + l * 48 + 4 * q:C_BMOD + l * 48 + 4 * q + 4], op=ALU.add),
                        reads=[B(7), "VT"], writes=[("mod", l, q)])

        def mod_vec(l, which, kc, cnd):
            return MOD[:, l, which * 8 + kc, cnd:cnd + 1]

        def mod_res(l, which):
            return [("mod", l, 2 * which), ("mod", l, 2 * which + 1)]

        def make_gs(l, s):
            which = 1 if s == 0 else 4
            for cnd in range(2):
                S.op("dve", lambda e, c=cnd: e.tensor_scalar(out=GS[:, l, s, :, c], in0=MOD[:, l, which * 8:which * 8 + 8, c],
                                                             scalar1=1.0, scalar2=None, op0=ALU.add),
                     reads=mod_res(l, which), writes=[("gs", l, s, cnd)])
                S.op("dve", lambda e, c=cnd: e.tensor_tensor(out=GS[:, l, s, :, c], in0=GS[:, l, s, :, c],
                                                             in1=VT[:, C_NG + l * 16 + s * 8:C_NG + l * 16 + s * 8 + 8],
                                                             op=ALU.mult),
                     reads=[("gs", l, s, cnd), "VT"], writes=[("gs", l, s, cnd)])

        def stats(tt):
            cs = slice(tt * 512, (tt + 1) * 512)
            S.op("act", lambda e: e.activation(out=xsq8[:, 0:4, :], in_=X[:, 0:4, cs], func=AF.Square),
                 reads=[("X", kc, tt) for kc in range(4)], writes=[("xsq", 0)])
            S.op("dve", lambda e: e.tensor_tensor(out=xsq8[:, 4:8, :], in0=X[:, 4:8, cs], in1=X[:, 4:8, cs], op=ALU.mult),
                 reads=[("X", kc, tt) for kc in range(4, 8)], writes=[("xsq", 1)])

            def f(e):
                ins = None
                for kc in range(KC):
                    ins = e.matmul(banks[6][:, :], lhsT=ones[:], rhs=xsq8[:, kc, :], start=(kc == 0), stop=(kc == KC - 1))
                return ins

            S.op("pe", f, reads=[("xsq", 0), ("xsq", 1), "ones"], writes=[B(6)])
            r = tt % 3
            S.op("act", lambda e, r=r: e.activation(out=rstd[r], in_=banks[6][:, :], func=AF.Ln, scale=1.0 / D, bias=EPS),
                 reads=[B(6)], writes=[("rstd", r)])
            S.op("act", lambda e, r=r: e.activation(out=rstd[r], in_=rstd[r], func=AF.Exp, scale=-0.5),
                 reads=[("rstd", r)], writes=[("rstd", r)])
            return r

        xm = carve(arena, 0, BF16, [128, KC, NT])

        def norm_mod(l, s):
            sh = 0 if s == 0 else 3
            make_gs(l, s)

            def modulate(tt, r):
                cnd = COND[tt]
                cs = slice(tt * 512, (tt + 1) * 512)
                for kc in range(KC):
                    b = state["tmp"] % 2
                    state["tmp"] += 1
                    S.op("dve", lambda e, b=b, kc=kc, r=r, cs=cs: e.tensor_tensor(out=tmpf[b], in0=X[:, kc, cs],
                                                                                 in1=rstd[r], op=ALU.mult),
                         reads=[("X", kc, tt), ("rstd", r)], writes=[("tmpf", b)])
                    S.op("act", lambda e, b=b, kc=kc, cs=cs, cnd=cnd: e.activation(
                        out=xm[:, kc, cs], in_=tmpf[b], func=AF.Identity,
                        scale=GS[:, l, s, kc, cnd:cnd + 1], bias=mod_vec(l, sh, kc, cnd)),
                        reads=[("tmpf", b), ("gs", l, s, cnd)] + mod_res(l, sh), writes=[("xm", kc, tt)])

            r0 = stats(0)
            r1 = stats(1)
            modulate(0, r0)
            r2 = stats(2)
            modulate(1, r1)
            modulate(2, r2)

        def residual(l, which, bk, m, tt):
            cnd = COND[tt]
            cs = slice(tt * 512, (tt + 1) * 512)
            S.op("dve", lambda e: e.scalar_tensor_tensor(out=X[:, m, cs], in0=banks[bk][:, :],
                                                         scalar=mod_vec(l, which, m, cnd), in1=X[:, m, cs],
                                                         op0=ALU.mult, op1=ALU.add),
                 reads=[B(bk), ("X", m, tt)] + mod_res(l, which), writes=[("X", m, tt)])

        do_mod(0, [0, 1, 2])

        xst = [carve(arena, 8192 + i * 16384, F32, [128, 4, D]) for i in range(2)]
        for tt in range(3):
            st = xst[tt % 2]
            for i in range(4):
                g = tt * 4 + i
                S.dma("sp", lambda e, st=st, i=i, g=g: e.dma_start(out=st[:, i, :], in_=x_d[g * 128:(g + 1) * 128, :]),
                      writes=[("xst", tt % 2, i)])
            for kc in range(KC):
                bk = bank()

                def f(e, st=st, kc=kc, bk=bk):
                    ins = None
                    for i in range(4):
                        ins = e.transpose(out=banks[bk][:, i * 128:(i + 1) * 128], in_=st[:, i, kc * 128:(kc + 1) * 128],
                                          identity=ident[:])
                    return ins

                S.op("pe", f, reads=[("xst", tt % 2, i) for i in range(4)] + ["ident"], writes=[B(bk)])
                copy_op(evac_eng(), X[:, kc, tt * 512:(tt + 1) * 512], banks[bk][:, :], [B(bk)], [("X", kc, tt)])
        S.barrier()

        def ffn(l):
            norm_mod(l, 1)
            hbuf = carve(arena, 24576, BF16, [128, FC, NT])
            sgs = [carve(arena, 24576 + 67584 + i * 2048, F32, [128, 512]) for i in range(3)]
            sgi = [0]
            for jp in range(11):
                s = w_get("pair")
                wv = WR[:, s, :].rearrange("p (kc t n) -> p kc t n", t=2, n=256)
                for jj in range(2):
                    j = 2 * jp + jj
                    for tt in range(3):
                        cs = slice(tt * 512, (tt + 1) * 512)
                        bg = bank()
                        bu = bank()
                        for t, bk in ((0, bg), (1, bu)):
                            def f(e, t=t, bk=bk, jj=jj, cs=cs, wv=wv):
                                ins = None
                                for kc in range(KC):
                                    ins = e.matmul(banks[bk][:, :], lhsT=wv[:, kc, t, jj * 128:(jj + 1) * 128],
                                                   rhs=xm[:, kc, cs], start=(kc == 0), stop=(kc == KC - 1))
                                return ins

                            S.op("pe", f, reads=[("ws", s)] + [("xm", kc, tt) for kc in range(KC)], writes=[B(bk)])
                        g = sgi[0] % 3
                        sgi[0] += 1
                        S.op("act", lambda e, g=g, bg=bg: e.activation(out=sgs[g], in_=banks[bg][:, :], func=AF.Silu),
                             reads=[B(bg)], writes=[("sg", g)])
                        S.op("dve", lambda e, g=g, bu=bu, j=j, cs=cs: e.tensor_tensor(out=hbuf[:, j, cs], in0=banks[bu][:, :],
                                                                                     in1=sgs[g], op=ALU.mult),
                             reads=[B(bu), ("sg", g)], writes=[("h", j, tt)])
            if (l == 0 and True) or l == 1:
                do_mod(l, range(10, 12))
            for m in range(KC):
                s = w_get("down")
                wv = WR[:, s, 0:FC * 128].rearrange("p (kc n) -> p kc n", n=128)
                for tt in range(3):
                    cs = slice(tt * 512, (tt + 1) * 512)
                    bk = bank()

                    def f(e, bk=bk, cs=cs, wv=wv):
                        ins = None
                        for j in range(FC):
                            ins = e.matmul(banks[bk][:, :], lhsT=wv[:, j, :], rhs=hbuf[:, j, cs],
                                           start=(j == 0), stop=(j == FC - 1))
                        return ins

                    S.op("pe", f, reads=[("ws", s)] + [("h", j, tt) for j in range(FC)], writes=[B(bk)])
                    residual(l, 5, bk, m, tt)
            S.barrier()

        def attention():
            l = 0
            norm_mod(0, 0)
            QA = carve(arena, 24576, BF16, [128, KC, NT])
            KT = carve(arena, 49152, BF16, [128, KC, NT])
            VA = carve(arena, 73728, BF16, [128, 12, 8, 192])
            A0 = 0
            Pt = [carve(arena, A0 + 18432 + i * 1024, BF16, [128, 512]) for i in range(6)]
            Ct = [carve(arena, A0 + 4096 + i * 2816, BF16, [128, 22, 64]) for i in range(2)]
            rden = [carve(arena, A0 + 9728 + i * 2048, F32, [128, 512]) for i in range(2)]
            QZ = [[carve(arena, A0 + 13824 + (hp_ * 2 + i) * 1024, BF16, [128, 512]) for i in range(2)] for hp_ in range(2)]
            R1 = carve(arena, 110592 + 128, F32, [128, 2, 31])
            EPs = carve(arena, 110592 + 128 + 248, BF16, [128, 2, 128])
            kst = carve(arena, 110592 + 1024, F32, [128, 2816])
            if cfg.get("a_vam", True):
                S.op("pool", lambda e: e.memset(VA[:, :, :, 64:128], 1.0),
                     writes=[("vaug", t) for t in range(12)] + [("xsq", 0), ("xsq", 1)])
            ETAB = cfg.get("a_etab", True)
            if ETAB:
                S.dma("sp", lambda e: e.dma_start(out=R1[0:120], in_=rpb_d.rearrange("(hg h) r c -> (h r) hg c", hg=2)),
                      writes=["R1"])
                S.op("pool", lambda e: e.memset(EPs[0:120], 0.0), writes=["EPs"])
                S.op("act", lambda e: e.activation(out=EPs[0:120, :, 48:79], in_=R1[0:120, :, :], func=AF.Exp),
                     reads=["R1", "EPs"], writes=["EPs"])
                epd_v = epd[0:16 * 15 * 127].rearrange("(hg q j) -> q hg j", hg=2, j=127)
                S.dma("sp", lambda e: e.dma_start(out=epd_v, in_=EPs[0:120, :, 0:127]), reads=["EPs"], writes=["epd"])
            def prep_ct(h):
                i = h % 2
                if not ETAB:
                    return
                for a, s0 in ((0, 4), (1, 3)):
                    src = bass.AP(tensor=epd_t, offset=h * 15 * 127, ap=[[1, 64], [127, 15], [1, 64]])
                    S.dma("sp", lambda e, i=i, a=a, s0=s0, src=src: e.dma_start(
                        out=Ct[i][a * 64:(a + 1) * 64, s0:s0 + 15, :], in_=src), reads=["epd"], writes=[("ct", i)])
                S.op("pool", lambda e, i=i: e.tensor_tensor(out=Ct[i][:, 3:19, :], in0=Ct[i][:, 3:19, :],
                                                            in1=cmb[:].unsqueeze(1).to_broadcast([128, 16, 64]),
                                                            op=ALU.mult),
                     reads=[("ct", i), "cmb"], writes=[("ct", i)])

            for q in range(6):
                s = w_get("k8")
                wv = WR[:, s, :].rearrange("p (kc n) -> p kc n", n=512)
                xr_ = [("xm", kc, tt) for kc in range(KC) for tt in range(3)]
                if q < 4:
                    dst, nm, scl = (QA, "qa", 0.125) if q < 2 else (KT, "kt", None)
                    for mm in range(4):
                        c = (q % 2) * 4 + mm
                        for tt in range(3):
                            cs = slice(tt * 512, (tt + 1) * 512)
                            bk = bank()

                            def f(e, bk=bk, cs=cs, wv=wv, mm=mm):
                                ins = None
                                for kc in range(KC):
                                    ins = e.matmul(banks[bk][:, :], lhsT=wv[:, kc, mm * 128:(mm + 1) * 128],
                                                   rhs=xm[:, kc, cs], start=(kc == 0), stop=(kc == KC - 1))
                                return ins

                            S.op("pe", f, reads=[("ws", s)] + [("xm", kc, tt) for kc in range(KC)], writes=[B(bk)])
                            if nm == "qa":
                                wr = [("qa", c, 0, tt), ("qa", c, 1, tt)]
                            else:
                                wr = [("kt", c, tt)]
                            copy_op(evac_eng(), dst[:, c, cs], banks[bk][:, :], [B(bk)], wr, scale=scl)
                if q in (2, 3, 4, 5) and cfg.get("a_tok", True):
                    isv = q >= 4
                    half = q % 2
                    for g in range(12 if isv else 4):
                        bk = bank()
                        ts_ = slice(g * 128, (g + 1) * 128)
                        tt = g // 4

                        def f(e, bk=bk, ts_=ts_, wv=wv):
                            ins = None
                            for kc in range(KC):
                                ins = e.matmul(banks[bk][:, :], lhsT=xm[:, kc, ts_], rhs=wv[:, kc, :],
                                               start=(kc == 0), stop=(kc == KC - 1))
                            return ins

                        S.op("pe", f, reads=[("ws", s)] + [("xm", kc, tt) for kc in range(KC)], writes=[B(bk)])
                        TM = cfg.get("a_tokm", 3)
                        eng = evac_eng()
                        if isv and TM >= 2:
                            vtile = 8 + g if g < 4 else g - 4
                            pv4 = banks[bk][:, :].rearrange("p (a t d) -> p a t d", t=2, d=64)
                            for t in range(2):
                                copy_op(eng, VA[:, vtile, half * 4:half * 4 + 4, t * 128:t * 128 + 64], pv4[:, :, t, :],
                                        [B(bk)], [("vaug", vtile)])
                        if g < 4 and TM >= 3:
                            so = (state["kst"] % 5) * 512
                            state["kst"] += 1
                            stg = kst[:, so:so + 512]
                            copy_op(eng, stg, banks[bk][:, :], [B(bk)], [("kst", so)])
                            od = nv_d if isv else nk_d
                            if TM >= 4 or TM == 3 and cfg.get("a_tokm", 3) == 3 and not cfg.get("a_nodma", False):
                              S.dma("sp", lambda e, od=od, g=g, half=half, stg=stg: e.dma_start(
                                out=od[g * 128:(g + 1) * 128, half * 512:(half + 1) * 512], in_=stg),
                                reads=[("kst", so)], out=True)
            S.barrier()
            for i in range(2):
                S.op("pool", lambda e, i=i: e.memset(Ct[i], 0.0), writes=[("ct", i)])
            for hp_ in range(2):
                for i in range(2):
                    S.op("pool", lambda e, hp_=hp_, i=i: e.memset(QZ[hp_][i], 0.0), writes=[("qz", hp_, i)])

            def prep_qz(c, hp, i, tt, q0, n, eng="dve"):
                ps_ = slice(hp * 64, hp * 64 + 64)
                S.op(eng, lambda e: e.tensor_copy(out=QZ[hp][i][ps_, 0:n], in_=QA[ps_, c, q0:q0 + n]),
                     reads=[("qa", c, hp, tt)], writes=[("qz", hp, i)])
            do_mod(0, range(4, 6))

            def run_items(items):
                n_it = len(items)
                import os
                DEPTH = int(os.environ.get('ADEPTH', '4'))
                for n in range(n_it + DEPTH):
                    if n < n_it:
                        items[n]["S"]()
                    if n >= DEPTH:
                        items[n - DEPTH]["PV"]()

            obank = [0]

            items = []
            for s_ in range(2):
                for h in range(16):
                    c, hp = h // 2, h % 2
                    ps = slice(hp * 64, hp * 64 + 64)
                    dps = slice(64, 128) if hp == 0 else slice(0, 64)
                    q0 = s_ * 256
                    it = {}

                    def fS(c=c, ps=ps, q0=q0, s_=s_, it=it):
                        bk = bank((0, 1, 2, 3, 6, 7))
                        p = state["pt"] % 6
                        state["pt"] += 1
                        it["p"] = p

                        hp_ = 0 if ps.start == 0 else 1
                        prep_qz(c, hp_, s_, 0, q0, 256, eng="pool")

                        def f(e):
                            ins = None
                            for j in range(2):
                                ins = e.matmul(banks[bk][:, j * 256:(j + 1) * 256],
                                               lhsT=KT[:, c, q0 + j * 128:q0 + (j + 1) * 128],
                                               rhs=QZ[hp_][s_][:, 0:256], start=True, stop=True)
                            return ins

                        S.op("pe", f, reads=[("kt", c, 0), ("qz", hp_, s_)], writes=[B(bk)])
                        S.op("act", lambda e: e.activation(out=Pt[p], in_=banks[bk][:, :], func=AF.Exp),
                             reads=[B(bk)], writes=[("pt", p)])

                    def fPV(c=c, hp=hp, ps=ps, dps=dps, q0=q0, s_=s_, it=it):
                        p = it["p"]
                        ob = 4 + obank[0] % 2
                        obank[0] += 1

                        def f(e):
                            ins = None
                            for j in range(2):
                                ins = e.matmul(banks[ob][:, 0:256], lhsT=VA[:, 8 + s_ * 2 + j, c, hp * 64:hp * 64 + 128],
                                               rhs=Pt[p][:, j * 256:(j + 1) * 256], start=(j == 0), stop=(j == 1))
                            return ins

                        S.op("pe", f, reads=[("pt", p), ("vaug", 8 + s_ * 2), ("vaug", 9 + s_ * 2)], writes=[B(ob)])
                        r = ob - 4
                        S.op("act", lambda e: e.activation(out=rden[r][dps, 0:256], in_=banks[ob][dps, 0:256], func=AF.Ln),
                             reads=[B(ob)], writes=[("rden", r)])
                        S.op("act", lambda e: e.activation(out=rden[r][dps, 0:256], in_=rden[r][dps, 0:256], func=AF.Exp,
                                                           scale=-1.0),
                             reads=[("rden", r)], writes=[("rden", r)])
                        S.op("dve", lambda e: e.tensor_tensor(out=QA[ps, c, q0:q0 + 256], in0=banks[ob][ps, 0:256],
                                                              in1=rden[r][dps, 0:256], op=ALU.mult),
                             reads=[B(ob), ("rden", r)], writes=[("qa", c, hp, 0)])

                    it["S"] = fS
                    it["PV"] = fPV
                    items.append(it)
            if cfg.get("a_ctx", True):
                run_items(items)

            for ct in range(4 if cfg.get("a_cache", True) else 0):
                for t in range(2):
                    src = cv_d[ct * 128:(ct + 1) * 128, :].rearrange("p (a t d) -> p a t d", t=2, d=64)[:, :, t, :]
                    S.dma("pool", lambda e, ct=ct, t=t, src=src: e.dma_start(
                        out=VA[:, 8 + ct, :, t * 128:t * 128 + 64], in_=src), writes=[("vaug", 8 + ct)])
            kstg = kst[:, 0:2048].rearrange("p (a b) -> p a b", a=2)
            for ct in range(4 if cfg.get("a_cache", True) else 0):
                buf = ct % 2
                S.dma("sp", lambda e, ct=ct, buf=buf: e.dma_start(out=kstg[:, buf, :], in_=ck_d[ct * 128:(ct + 1) * 128, :]),
                      writes=[("kstg", buf)] + [("kst", so) for so in range(0, 2560, 512)])
                for hh in range(2):
                    bk = bank((0, 1, 2, 3))

                    def f(e, bk=bk, hh=hh, buf=buf):
                        ins = None
                        for q in range(4):
                            c = hh * 4 + q
                            ins = e.transpose(out=banks[bk][:, q * 128:(q + 1) * 128],
                                              in_=kstg[:, buf, c * 128:(c + 1) * 128], identity=ident[:])
                        return ins

                    S.op("pe", f, reads=[("kstg", buf), "ident"], writes=[B(bk)])
                    copy_op(evac_eng(), KT[:, hh * 4:hh * 4 + 4, ct * 128:(ct + 1) * 128],
                            banks[bk][:, :].rearrange("p (a b) -> p a b", b=128), [B(bk)],
                            [("kt", hh * 4 + q, 0) for q in range(4)])

            PB = [carve(arena, 112640 + i * 1024, BF16, [128, 512]) for i in range(10)] + \
                 [carve(arena, A0 + i * 1024, BF16, [128, 512]) for i in range(2)]
            for i in range(12):
                S.op("pool", lambda e, i=i: e.memset(PB[i], 0.0),
                     writes=[("pb", i), ("kstg", 0), ("kstg", 1)] + [("kst", so) for so in range(0, 2560, 512)])

            def rs_(r):
                return min(max(r - 4, 0), 8)

            def valid(rk, r):
                return rs_(r) <= rk <= rs_(r) + 7

            lat_tiles = {0: list(range(0, 6)), 1: list(range(2, 8))}
            items = []
            prep_ct(0)
            prep_qz(0, 0, 0, 1, 512, 512)
            for h in range(16):
                c, hp = h // 2, h % 2
                ps = slice(hp * 64, hp * 64 + 64)
                dps = slice(64, 128) if hp == 0 else slice(0, 64)
                for qt in range(2):
                    tt = 1 + qt
                    q0 = 512 + qt * 512
                    tiles = [("c", ct) for ct in range(4)] + [("l", kt) for kt in lat_tiles[qt]]
                    for n, (kind, t) in enumerate(tiles):
                        it = {}
                        first = (n == 0)
                        last = (n == len(tiles) - 1)

                        def fS(c=c, hp=hp, ps=ps, q0=q0, tt=tt, qt=qt, kind=kind, t=t, it=it, h=h, first=first):
                            if first:
                                nh_, nq_ = (h, 1) if qt == 0 else (h + 1, 0)
                                if nh_ < 16:
                                    prep_qz(nh_ // 2, nh_ % 2, nq_, 1 + nq_, 512 + nq_ * 512, 512)
                                if qt == 1 and h + 1 < 16:
                                    prep_ct(h + 1)
                            bk = bank((0, 1, 2, 3, 6, 7))
                            p = state["pt"] % 6
                            state["pt"] += 1
                            it["p"] = p
                            k0 = t * 128 if kind == "c" else 512 + t * 128
                            kres = ("kt", c, 0) if kind == "c" else ("kt", c, 1 + t // 4)
                            if kind == "l":
                                vu = [b for b in range(8) if valid(2 * t, 8 * qt + b) or valid(2 * t + 1, 8 * qt + b)]
                                cols = slice(vu[0] * 64, (vu[-1] + 1) * 64)
                                assert vu == list(range(vu[0], vu[-1] + 1))
                            else:
                                cols = slice(0, 512)
                            it["cols"] = cols
                            S.op("pe", lambda e: e.matmul(banks[bk][:, cols], lhsT=KT[:, c, k0:k0 + 128],
                                                          rhs=QZ[hp][qt][:, cols], start=True, stop=True),
                                 reads=[kres, ("qz", hp, qt)], writes=[B(bk)])
                            S.op("act", lambda e: e.activation(out=Pt[p][:, cols], in_=banks[bk][:, cols], func=AF.Exp),
                                 reads=[B(bk)], writes=[("pt", p)])
                            if kind == "l":
                                Dd = 2 * t - 8 * qt
                                s_hi = Dd + 11
                                ci = h % 2
                                pbi = qt * 6 + lat_tiles[qt].index(t)
                                it["pb"] = pbi
                                vb = [[b for b in range(8) if valid(2 * t + a, 8 * qt + b)] for a in range(2)]
                                if vb[0] == vb[1]:
                                    jobs = [(slice(0, 128), vb[0])]
                                else:
                                    jobs = [(slice(a * 64, (a + 1) * 64), vb[a]) for a in range(2)]
                                for psl, vbl in jobs:
                                    if not vbl:
                                        continue
                                    b_lo, b_hi = vbl[0], vbl[-1] + 1
                                    assert vbl == list(range(b_lo, b_hi))
                                    hi_ = s_hi - b_lo
                                    lo_ = s_hi - b_hi
                                    ev_ = Ct[ci][psl, hi_:(lo_ if lo_ >= 0 else None):-1, ::-1]
                                    o3 = PB[pbi][psl, b_lo * 64:b_hi * 64].rearrange("p (b c) -> p b c", c=64)
                                    i3 = Pt[p][psl, b_lo * 64:b_hi * 64].rearrange("p (b c) -> p b c", c=64)
                                    S.op("dve", lambda e, o3=o3, i3=i3, ev_=ev_: e.tensor_tensor(out=o3, in0=i3, in1=ev_, op=ALU.mult),
                                         reads=[("pt", p), ("ct", ci)], writes=[("pb", pbi)])

                        def fPV(c=c, hp=hp, ps=ps, dps=dps, q0=q0, tt=tt, kind=kind, t=t, it=it, first=first, last=last):
                            p = it["p"]
                            if first:
                                obank[0] += 1
                            ob = 4 + obank[0] % 2
                            vtile = 8 + t if kind == "c" else t
                            cols = it["cols"]
                            if kind == "l":
                                rhs_, rres = PB[it["pb"]][:, cols], ("pb", it["pb"])
                            else:
                                rhs_, rres = Pt[p], ("pt", p)
                            S.op("pe", lambda e: e.matmul(banks[ob][:, cols], lhsT=VA[:, vtile, c, hp * 64:hp * 64 + 128],
                                                          rhs=rhs_, start=first, stop=last),
                                 reads=[rres, ("vaug", vtile)], writes=[B(ob)])
                            if last:
                                r = ob - 4
                                S.op("act", lambda e: e.activation(out=rden[r][dps, :], in_=banks[ob][dps, :], func=AF.Ln),
                                     reads=[B(ob)], writes=[("rden", r)])
                                S.op("act", lambda e: e.activation(out=rden[r][dps, :], in_=rden[r][dps, :], func=AF.Exp,
                                                                   scale=-1.0),
                                     reads=[("rden", r)], writes=[("rden", r)])
                                S.op("dve", lambda e: e.tensor_tensor(out=QA[ps, c, q0:q0 + 512], in0=banks[ob][ps, :],
                                                                      in1=rden[r][dps, :], op=ALU.mult),
                                     reads=[B(ob), ("rden", r)], writes=[("qa", c, hp, tt)])

                        it["S"] = fS
                        it["PV"] = fPV
                        items.append(it)
            if cfg.get("a_lat", True):
                run_items(items)

            S.barrier()
            slabs = [w_get("k8"), w_get("k8", ahead=2)]
            for tt in range(3):
                cs = slice(tt * 512, (tt + 1) * 512)
                for q in range(2):
                    s = slabs[q]
                    wv = WR[:, s, :].rearrange("p (kc n) -> p kc n", n=512)
                    for mm in range(4):
                        m = q * 4 + mm
                        bk = bank((0, 1, 2, 3))

                        def f(e, bk=bk, cs=cs, wv=wv, mm=mm):
                            ins = None
                            for kc in range(KC):
                                ins = e.matmul(banks[bk][:, :], lhsT=wv[:, kc, mm * 128:(mm + 1) * 128],
                                               rhs=QA[:, kc, cs], start=(kc == 0), stop=(kc == KC - 1))
                            return ins

                        S.op("pe", f, reads=[("ws", s)] + [("qa", kc, hp, tt) for kc in range(KC) for hp in range(2)],
                             writes=[B(bk)])
                        residual(0, 2, bk, m, tt)

        def lru():
            l = 1
            norm_mod(1, 0)
            S.barrier()
            ybuf = carve(arena, 24576, BF16, [128, KC, NT])
            o = 49152
            xrp = carve(arena, o, F32, [128, 1548]); o += 6192
            gg = []; xc = []; xcb = []
            for i in range(2):
                gg.append(carve(arena, o, F32, [128, NT])); o += 6144
                xc.append(carve(arena, o, F32, [128, NT])); o += 6144
                xcb.append(carve(arena, o, BF16, [128, NT])); o += 3072
            dirb = []
            for d in range(2):
                ra = carve(arena, o, F32, [128, NT]); o += 6144
                itb = carve(arena, o, F32, [128, NT]); o += 6144
                hs = carve(arena, o, F32, [128, NT]); o += 6144
                dirb.append((ra, itb, hs))
            wgb = carve(arena, o, BF16, [128, 2, 2, 8, 128]); o += 8192
            assert o <= ARENA_B + STRIP, o
            lam = VT[:, C_LAM:C_LAM + 16]
            yv, wv_, dv, lw = ltmp[:, 0, :], ltmp[:, 1, :], ltmp[:, 2, :], ltmp[:, 3, :]
            S.op("act", lambda e: e.activation(out=yv, in_=lam, func=AF.Exp, scale=-1.0), reads=["VT"], writes=["l_y"])
            S.op("dve", lambda e: e.tensor_scalar(out=wv_, in0=yv, scalar1=1.0, scalar2=None, op0=ALU.add),
                 reads=["l_y"], writes=["l_w"])
            S.op("dve", lambda e: e.tensor_scalar(out=dv, in0=wv_, scalar1=-1.0, scalar2=1e-30, op0=ALU.add, op1=ALU.max),
                 reads=["l_w"], writes=["l_d"])
            S.op("dve", lambda e: e.reciprocal(out=dv, in_=dv), reads=["l_d"], writes=["l_d"])
            S.op("act", lambda e: e.activation(out=lw, in_=wv_, func=AF.Ln), reads=["l_w"], writes=["l_lw"])
            S.op("dve", lambda e: e.tensor_tensor(out=dv, in0=dv, in1=yv, op=ALU.mult), reads=["l_d", "l_y"], writes=["l_d"])
            S.op("dve", lambda e: e.scalar_tensor_tensor(out=nls[:], in0=lw, scalar=-8.0, in1=dv, op0=ALU.mult, op1=ALU.mult),
                 reads=["l_lw", "l_d"], writes=["nls"])
            nls2 = ltmp[:, 0, :]
            S.op("dve", lambda e: e.tensor_scalar(out=nls2, in0=nls[:], scalar1=2.0, scalar2=None, op0=ALU.mult),
                 reads=["nls", "l_y", "l_d"], writes=["nls2", "l_y"])
            S.op("pool", lambda e: e.memset(xrp, 0.0), writes=["xrp"])
            wg = wgb
            for wi_, wd in enumerate((wa_d, wi_d)):
                for d in range(2):
                    S.dma("pool", lambda e, o_=wgb[:, wi_, d, :, :], i_=wd[d].rearrange("n k j -> k n j"):
                          e.dma_start(out=o_, in_=i_), writes=["wgb"])
            SEQ = [(2, 0, 256), (261, 256, 256), (520, 512, 1024)]
            pair = {}

            def stageA(m):
                bf_ = m % 2
                jp, jj = m // 2, m % 2
                if jj == 0:
                    s = w_get("pair")
                    pair["s"] = s
                s = pair["s"]
                wv = WR[:, s, :].rearrange("p (kc t n) -> p kc t n", t=2, n=256)
                for tt in range(3):
                    cs = slice(tt * 512, (tt + 1) * 512)
                    bg = bank()
                    bx = bank()
                    for t, bk in ((0, bg), (1, bx)):
                        def f(e, t=t, bk=bk, jj=jj, cs=cs, wv=wv):
                            ins = None
                            for kc in range(KC):
                                ins = e.matmul(banks[bk][:, :], lhsT=wv[:, kc, t, jj * 128:(jj + 1) * 128],
                                               rhs=xm[:, kc, cs], start=(kc == 0), stop=(kc == KC - 1))
                            return ins

                        S.op("pe", f, reads=[("ws", s)] + [("xm", kc, tt) for kc in range(KC)], writes=[B(bk)])
                    S.op("act", lambda e, bg=bg, cs=cs: e.activation(out=gg[bf_][:, cs], in_=banks[bg][:, :],
                                                                     func=AF.Gelu_apprx_tanh),
                         reads=[B(bg)], writes=[("gg", bf_)])
                    if tt == 0:
                        for sq in range(2):
                            po = SEQ[sq][0]
                            S.op("dve", lambda e, bx=bx, sq=sq, po=po: e.tensor_copy(
                                out=xrp[:, po:po + 256], in_=banks[bx][:, sq * 256:(sq + 1) * 256]),
                                reads=[B(bx)], writes=["xrp"])
                    else:
                        po = 520 + (tt - 1) * 512
                        S.op("dve", lambda e, bx=bx, po=po: e.tensor_copy(out=xrp[:, po:po + 512], in_=banks[bx][:, :]),
                             reads=[B(bx)], writes=["xrp"])
                for j in range(4):
                    S.op("dve", lambda e, j=j: e.tensor_scalar(out=Dg[:, j, :], in0=ident[:], scalar1=vt(C_CW + j * 8 + m),
                                                               scalar2=None, op0=ALU.mult),
                         reads=["ident", "VT"], writes=[("dg", j)])
                cgroups = [[(0, 0, 0, 256), (256, 259, 256, 256)], [(0, 518, 512, 512)], [(0, 1030, 1024, 512)]]
                for grp in cgroups:
                    bk = bank()

                    def fc(e, grp=grp, bk=bk):
                        ins = None
                        for (c0, base, co, ln) in grp:
                            for j in range(4):
                                ins = e.matmul(banks[bk][:, c0:c0 + ln], lhsT=Dg[:, j, :], rhs=xrp[:, base + j:base + j + ln],
                                               start=(j == 0), stop=(j == 3))
                        return ins

                    S.op("pe", fc, reads=["xrp"] + [("dg", j) for j in range(4)], writes=[B(bk)])
                    co0 = grp[0][2]
                    n_ = sum(g_[3] for g_ in grp)
                    S.op("dve", lambda e, bk=bk, co0=co0, n_=n_: e.tensor_scalar(
                        out=xc[bf_][:, co0:co0 + n_], in0=banks[bk][:, 0:n_], scalar1=vt(C_CB + m), scalar2=None, op0=ALU.add),
                        reads=[B(bk), "VT"], writes=[("xc", bf_)])
                S.op("dve", lambda e: e.tensor_copy(out=xcb[bf_], in_=xc[bf_]), reads=[("xc", bf_)], writes=[("xcb", bf_)])

            def stageB(m):
                bf_ = m % 2
                for d in range(2):
                    ra, itb, hs = dirb[d]
                    for tt in range(3):
                        cs = slice(tt * 512, (tt + 1) * 512)
                        ba_ = bank()
                        bi_ = bank()
                        for w_, bk in ((0, ba_), (1, bi_)):
                            S.op("pe", lambda e, w_=w_, bk=bk, cs=cs, d=d: e.matmul(
                                banks[bk][:, :], lhsT=wg[:, w_, d, m, :], rhs=xcb[bf_][:, cs], start=True, stop=True),
                                reads=["wgb", ("xcb", bf_)], writes=[B(bk)])
                        S.op("act", lambda e, ba_=ba_, cs=cs, d=d, ra=ra: e.activation(
                            out=ra[:, cs], in_=banks[ba_][:, :], func=AF.Sigmoid, bias=vt(C_BA + d * 8 + m)),
                            reads=[B(ba_), "VT"], writes=[("ra", d)])
                        S.op("act", lambda e, bi_=bi_, cs=cs, d=d, itb=itb: e.activation(
                            out=itb[:, cs], in_=banks[bi_][:, :], func=AF.Sigmoid, bias=vt(C_BI + d * 8 + m)),
                            reads=[B(bi_), "VT"], writes=[("it", d)])
                    col = d * 8 + m
                    S.op("act", lambda e, hs=hs, ra=ra, col=col: e.activation(out=hs, in_=ra, func=AF.Exp, scale=nls2[:, col:col + 1]),
                         reads=[("ra", d), "nls2"], writes=[("hs", d)])
                    S.op("act", lambda e, ra=ra, col=col: e.activation(out=ra, in_=ra, func=AF.Exp, scale=nls[:, col:col + 1]),
                         reads=[("ra", d), "nls"], writes=[("ra", d)])
                    S.op("act", lambda e, hs=hs: e.activation(out=hs, in_=hs, func=AF.Sqrt, scale=-1.0, bias=1.0),
                         reads=[("hs", d)], writes=[("hs", d)])
                    S.op("pool", lambda e, itb=itb: e.tensor_tensor(out=itb, in0=itb, in1=xc[bf_], op=ALU.mult),
                         reads=[("it", d), ("xc", bf_)], writes=[("it", d)])
                    S.op("pool", lambda e, itb=itb, hs=hs: e.tensor_tensor(out=itb, in0=itb, in1=hs, op=ALU.mult),
                         reads=[("it", d), ("hs", d)], writes=[("it", d)])
                    for sq, (po, co, ln) in enumerate(SEQ):
                        init = 0.0 if sq < 2 else vt(C_H0 + d * 8 + m)
                        if d == 0:
                            S.op("dve", lambda e, co=co, ln=ln, init=init, ra=ra, itb=itb, hs=hs: e.tensor_tensor_scan(
                                out=hs[:, co:co + ln], data0=ra[:, co:co + ln], data1=itb[:, co:co + ln],
                                initial=init, op0=ALU.mult, op1=ALU.add),
                                reads=[("ra", d), ("it", d), "VT"], writes=[("hs", d)])
                        else:
                            lo = co - 1 if co > 0 else None
                            S.op("dve", lambda e, co=co, ln=ln, init=init, lo=lo, ra=ra, itb=itb, hs=hs: e.tensor_tensor_scan(
                                out=hs[:, co + ln - 1:lo:-1], data0=ra[:, co + ln - 1:lo:-1],
                                data1=itb[:, co + ln - 1:lo:-1], initial=init, op0=ALU.mult, op1=ALU.add),
                                reads=[("ra", d), ("it", d), "VT"], writes=[("hs", d)])
                        if sq < 2:
                            c_ = co + ln - 1 if d == 0 else co
                            nhc = sq * 16 + d * 8 + m
                            S.op("dve", lambda e, c_=c_, nhc=nhc, hs=hs: e.tensor_copy(
                                out=NH[:, nhc:nhc + 1], in_=hs[:, c_:c_ + 1]), reads=[("hs", d)], writes=["NH"])
                hf, hb = dirb[0][2], dirb[1][2]
                S.op("dve", lambda e: e.tensor_tensor(out=hf, in0=hf, in1=hb, op=ALU.add),
                     reads=[("hs", 0), ("hs", 1)], writes=[("hs", 0)])
                S.op("dve", lambda e: e.tensor_tensor(out=ybuf[:, m, :], in0=hf, in1=gg[bf_], op=ALU.mult),
                     reads=[("hs", 0), ("gg", bf_)], writes=[("y", m)])

            stageA(0)
            for m in range(8):
                if m + 1 < 8:
                    stageA(m + 1)
                stageB(m)
            S.barrier()
            S.op("pe", lambda e: e.transpose(out=banks[6][0:32, 0:128], in_=NH[:, :], identity=ident[:]),
                 reads=["NH", "ident"], writes=[B(6)])
            nhs = carve(arena, 110592, F32, [128, 128])
            S.op("dve", lambda e: e.tensor_copy(out=nhs[0:32, :], in_=banks[6][0:32, 0:128]), reads=[B(6)], writes=["nhs"])
            S.dma("sp", lambda e: e.dma_start(out=nh_d, in_=nhs[0:32, :]), reads=["nhs"], out=True)
            do_mod(1, range(4, 6))
            slabs = [w_get("k8"), w_get("k8", ahead=2)]
            for tt in range(3):
                cs = slice(tt * 512, (tt + 1) * 512)
                for q in range(2):
                    s = slabs[q]
                    wv = WR[:, s, :].rearrange("p (kc n) -> p kc n", n=512)
                    for mm in range(4):
                        m = q * 4 + mm
                        bk = bank()

                        def f(e, bk=bk, cs=cs, wv=wv, mm=mm):
                            ins = None
                            for kc in range(KC):
                                ins = e.matmul(banks[bk][:, :], lhsT=wv[:, kc, mm * 128:(mm + 1) * 128],
                                               rhs=ybuf[:, kc, cs], start=(kc == 0), stop=(kc == KC - 1))
                            return ins

                        S.op("pe", f, reads=[("ws", s)] + [("y", kc) for kc in range(KC)], writes=[B(bk)])
                        residual(1, 2, bk, m, tt)

        do_mod(0, [3])
        if cfg["attn"]:
            attention()
        else:
            do_mod(0, range(4, 6))
        do_mod(0, range(6, 10))
        if cfg["ffn0"]:
            ffn(0)
        else:
            do_mod(0, range(10, 12))
        do_mod(1, range(0, 4))
        if cfg["lru"]:
            lru()
        else:
            do_mod(1, range(4, 6))
        do_mod(1, range(6, 10))
        if cfg["ffn1"]:
            ffn(1)
        else:
            do_mod(1, range(10, 12))
        S.barrier()

        yTs = [carve(arena, 0, F32, [128, KC, 512]), carve(arena, 32768, F32, [128, KC, 512])]
        ost = [carve(arena, 16384 + i * 4096, F32, [128, D]) for i in range(4)]
        fin_r = {0: stats(0), 1: stats(1)}
        for tt in range(3):
            cs = slice(tt * 512, (tt + 1) * 512)
            if tt == 1:
                fin_r[2] = stats(2)
            r = fin_r[tt]
            yb = tt % 2
            yT = yTs[yb]
            for kc in range(KC):
                S.op("dve", lambda e, kc=kc, r=r, cs=cs, yT=yT: e.scalar_tensor_tensor(
                    out=yT[:, kc, :], in0=X[:, kc, cs], scalar=vt(C_FG + kc), in1=rstd[r], op0=ALU.mult, op1=ALU.mult),
                    reads=[("X", kc, tt), ("rstd", r), "VT"], writes=[("yT", yb, kc)])
            for i in range(4):
                g = tt * 4 + i
                ob_ = g % 4
                o_ = ost[ob_]
                for hh in range(2):
                    bk = bank()

                    def f(e, bk=bk, hh=hh, i=i, yT=yT):
                        ins = None
                        for q in range(4):
                            ins = e.transpose(out=banks[bk][:, q * 128:(q + 1) * 128],
                                              in_=yT[:, hh * 4 + q, i * 128:(i + 1) * 128], identity=ident[:])
                        return ins

                    S.op("pe", f, reads=[("yT", yb, hh * 4 + q) for q in range(4)] + ["ident"], writes=[B(bk)])
                    copy_op(evac_eng(), o_[:, hh * 512:(hh + 1) * 512], banks[bk][:, :], [B(bk)], [("ost", ob_, hh)])
                S.dma("sp", lambda e, g=g, o_=o_: e.dma_start(out=y_d[g * 128:(g + 1) * 128, :], in_=o_),
                      reads=[("ost", ob_, 0), ("ost", ob_, 1)], out=True)
        S.finish()

        block = es.enter_context(nc.Block())

        @block.tensor
        def _(e):
            S.emit("pe", e)

        @block.scalar
        def _(e):
            S.emit("act", e)

        @block.vector
        def _(e):
            S.emit("dve", e)

        @block.gpsimd
        def _(e):
            S.emit("pool", e)

        @block.sync
        def _(e):
            S.emit("sp", e)
    return nc


def _colmask():
    cm = np.zeros((128, 64), np.float32)
    for cq in range(64):
        cs = min(max(cq - 8, 0), 48)
        for p in range(128):
            ck = p % 64
            if cs <= ck < cs + 16:
                cm[p, 63 - cq] = 1.0
    return cm


def make_in_maps(inp):
    f = lambda a: np.ascontiguousarray(np.asarray(a, dtype=np.float32))
    x_prompt, x_sample = f(inp["x_prompt"]), f(inp["x_sample"])
    c, c_ctx = f(inp["c"]), f(inp["c_ctx"])
    shared = {
        "ident": np.eye(128, dtype=np.float32),
        "cm": _colmask(),
        "rpb": f(inp["attn_rpb"])[0],
        "w_mod": f(inp["w_mod"]),
        "w_qkv": f(inp["attn_w_qkv"])[0],
        "w_o": f(inp["attn_w_o"])[0],
        "w_in": f(inp["lru_w_in"])[0],
        "w_a": f(inp["lru_w_a"])[0],
        "w_i": f(inp["lru_w_i"])[0],
        "w_out": f(inp["lru_w_out"])[0],
        "w_gu": f(inp["ffn_w_gu"]),
        "w_down": f(inp["ffn_w_down"]),
    }
    common_rows = [
        f(inp["b_mod"]).reshape(96, 128),
        f(inp["norm_g"]).reshape(32, 128),
        f(inp["final_g"]).reshape(8, 128),
        f(inp["lru_conv_w"])[0].reshape(32, 128),
        f(inp["lru_conv_b"])[0].reshape(8, 128),
        f(inp["lru_b_a"])[0].reshape(16, 128),
        f(inp["lru_b_i"])[0].reshape(16, 128),
        f(inp["lru_lam"])[0].reshape(16, 128),
    ]
    maps = []
    for i in range(NCORES):
        smalls = np.concatenate(common_rows + [c_ctx.reshape(8, 128), c[i].reshape(8, 128),
                                               f(inp["state_h"])[i, 0].reshape(16, 128)], axis=0)
        assert smalls.shape == (256, 128)
        m = dict(shared)
        m["x"] = np.concatenate([x_prompt[2 * i], x_prompt[2 * i + 1], x_sample[i]], axis=0)
        m["smalls"] = np.ascontiguousarray(smalls)
        m["ck"] = f(inp["cache_k"])[i, 0].reshape(512, D)
        m["cv"] = f(inp["cache_v"])[i, 0].reshape(512, D)
        maps.append(m)
    return maps


_NC_CACHE = {}


def run(inp, cfg=FULL):
    key = tuple(sorted(cfg.items()))
    if key not in _NC_CACHE:
        _NC_CACHE[key] = build(cfg)
    nc = _NC_CACHE[key]
    import os
    if os.environ.get("DBG1CORE"):
        res = run_bass_kernel_spmd(nc, make_in_maps(inp)[:1], core_ids=[0])
        rs = [res.results[0]] * NCORES
    else:
        res = run_bass_kernel_spmd(nc, make_in_maps(inp), core_ids=list(range(NCORES)))
        rs = res.results
    y_prompt = np.stack([rs[i // 2]["y"][(i % 2) * 256:(i % 2) * 256 + 256] for i in range(16)], axis=0)
    y_sample = np.stack([rs[i]["y"][512:1536] for i in range(8)], axis=0)
    nk = np.stack([rs[i // 2]["nk"][(i % 2) * 256:(i % 2) * 256 + 256] for i in range(16)], axis=0)
    nv = np.stack([rs[i // 2]["nv"][(i % 2) * 256:(i % 2) * 256 + 256] for i in range(16)], axis=0)
    nk = nk.reshape(16, 1, 256, 16, 64)
    nv = nv.reshape(16, 1, 256, 16, 64)
    nh = np.stack([rs[i // 2]["nh"].reshape(2, 2, 1024)[i % 2] for i in range(16)], axis=0).reshape(16, 1, 2, 1024)
    return (y_prompt.astype(np.float32), y_sample.astype(np.float32), nk.astype(np.float32),
            nv.astype(np.float32), nh.astype(np.float32))


def kernel(**inputs):
    return run(inputs, FULL)
```

```python
import numpy as np
from contextlib import ExitStack
import concourse.bass as bass
import concourse.mybir as mybir
from concourse.bass_utils import run_bass_kernel_spmd

F32 = mybir.dt.float32
BF16 = mybir.dt.bfloat16
AF = mybir.ActivationFunctionType
ALU = mybir.AluOpType

D = 1024
KC = 8
NT = 1536
DFF = 2816
FC = 22
NCORES = 8
EPS = 1e-6


class Sched:
    ENG = ("pe", "act", "dve", "pool", "sp")

    def __init__(self, nc, es):
        self.nc = nc
        self.ops = {e: [] for e in self.ENG}
        self.cnt = {e: 0 for e in self.ENG}
        self.esem = {e: es.enter_context(nc.semaphore("s_" + e)) for e in self.ENG}
        self.dsem = {q: [es.enter_context(nc.semaphore("d_%s%d" % (q, i))) for i in range(n)]
                     for q, n in (("sp", 24), ("pool", 12))}
        self.dtgt = {}
        self.drr = {"sp": 0, "pool": 0}
        self.lastw = {}
        self.rd = {}
        self.seen = {e: {} for e in self.ENG}
        self.pending = {e: {} for e in self.ENG}
        self.out_tokens = []

    def _deps(self, eng, idx, reads, writes, strict=False):
        waits = dict(self.pending[eng])
        self.pending[eng] = {}
        for sk in list(waits):
            if self.seen[eng].get(sk, 0) >= waits[sk]:
                del waits[sk]

        def need(tok, raw):
            sk, v, pe, pidx = tok
            if pe == eng and eng != "pool":
                if eng == "pe":
                    return
                if raw == "war":
                    return
            if self.seen[eng].get(sk, 0) >= v:
                return
            if waits.get(sk, 0) < v:
                waits[sk] = v

        for r in reads:
            t = self.lastw.get(r)
            if t:
                need(t, True)
        for w in writes:
            t = self.lastw.get(w)
            if t:
                need(t, "waw")
            for t in self.rd.get(w, {}).values():
                need(t, "war")
        for sk, v in waits.items():
            self.seen[eng][sk] = v
        return waits

    def _commit(self, tok, reads, writes):
        key = tok[2] if tok[2] else tok[0]
        for r in reads:
            self.rd.setdefault(r, {})[key] = tok
        for w in writes:
            self.lastw[w] = tok
            self.rd[w] = {}

    def op(self, eng, fn, reads=(), writes=()):
        idx = self.cnt[eng]
        waits = self._deps(eng, idx, reads, writes)
        self.cnt[eng] += 1
        tok = (("e", eng), idx + 1, eng, idx)
        self._commit(tok, reads, writes)
        self.ops[eng].append((list(waits.items()), fn, None))
        return tok

    def dma(self, q, fn, reads=(), writes=(), out=False):
        idx = self.cnt[q]
        waits = self._deps(q, idx, reads, writes, strict=True)
        k = self.drr[q]
        self.drr[q] += 1
        sk = ("d", q, k % len(self.dsem[q]))
        prev = self.dtgt.get(sk, 0)
        if prev and self.seen[q].get(sk, 0) < prev:
            waits[sk] = prev
            self.seen[q][sk] = prev
        self.dtgt[sk] = prev + 16
        tok = (sk, prev + 16, None, None)
        self._commit(tok, reads, writes)
        self.ops[q].append((list(waits.items()), fn, sk))
        if out:
            self.out_tokens.append(tok)
        return tok

    def barrier(self):
        for e in self.ENG:
            p = self.pending[e]
            for f in self.ENG:
                if f != e and self.cnt[f] > 0:
                    sk = ("e", f)
                    p[sk] = max(p.get(sk, 0), self.cnt[f])
            for sk, v in self.dtgt.items():
                p[sk] = max(p.get(sk, 0), v)

    def finish(self):
        waits = {}
        for sk, v in self.dtgt.items():
            waits[sk] = v
        for f in self.ENG:
            if f != "sp" and self.cnt[f] > 0:
                waits[("e", f)] = self.cnt[f]
        self.ops["sp"].append((list(waits.items()), None, None))

    def sem(self, sk):
        return self.esem[sk[1]] if sk[0] == "e" else self.dsem[sk[1]][sk[2]]

    def emit(self, eng, e):
        for waits, fn, dsk in self.ops[eng]:
            attach = None
            if eng != "pe" and fn is not None and waits:
                attach = waits[-1]
                waits = waits[:-1]
            for sk, v in waits:
                e.wait_ge(self.sem(sk), v)
            if fn is None:
                continue
            if eng == "pe":
                px = _PEProxy(e, self.sem(attach[0]), attach[1]) if attach is not None else e
                ins = fn(px)
            else:
                ins = fn(e)
                if attach is not None:
                    ins._wait_ge(self.sem(attach[0]), attach[1])
            if dsk is None:
                ins.then_inc(self.esem[eng], 1)
            else:
                ins.then_inc(self.sem(dsk), 16)


class _PEProxy:
    def __init__(self, e, sem, val):
        self.e, self.sem, self.val = e, sem, val

    def _first(self, ins):
        if self.sem is not None:
            ins._wait_ge(self.sem, self.val)
            self.sem = None
        return ins

    def matmul(self, *a, **k):
        return self._first(self.e.matmul(*a, **k))

    def transpose(self, *a, **k):
        return self._first(self.e.transpose(*a, **k))


def _prod(s):
    r = 1
    for v in s:
        r *= v
    return r


def carve(base, off, dtype, shape):
    esz = 4 if dtype == F32 else 2
    nb = _prod(shape[1:]) * esz
    assert off % 4 == 0 and nb % 4 == 0
    sl = base[:, off // 4:(off + nb) // 4]
    if dtype != F32:
        sl = sl.bitcast(dtype)
    if len(shape) == 2:
        return sl
    names = "abcd"[:len(shape) - 1]
    pat = "p (%s) -> p %s" % (" ".join(names), " ".join(names))
    kw = {names[i]: shape[i + 1] for i in range(1, len(names))}
    return sl.rearrange(pat, **kw)


FULL = dict(attn=True, ffn0=True, lru=True, ffn1=True)


def build(cfg=FULL):
    nc = bass.Bass("TRN2", target_bir_lowering=False)

    def din(name, shape, dt=F32):
        return nc.dram_tensor(name, shape, dt, kind="ExternalInput").ap()

    def dout(name, shape):
        return nc.dram_tensor(name, shape, F32, kind="ExternalOutput").ap()

    x_d = din("x", [NT, D])
    smalls_d = din("smalls", [256, 128])
    ident_d = din("ident", [128, 128])
    cm_d = din("cm", [128, 64])
    ck_d = din("ck", [512, D])
    cv_d = din("cv", [512, D])
    rpb_d = din("rpb", [16, 15, 31])
    wmod_d = din("w_mod", [2, D, 6144])
    wqkv_d = din("w_qkv", [D, 3072])
    wo_d = din("w_o", [D, D])
    win_d = din("w_in", [D, 2048])
    wa_d = din("w_a", [2, 8, 128, 128])
    wi_d = din("w_i", [2, 8, 128, 128])
    wout_d = din("w_out", [D, D])
    wgu_d = din("w_gu", [2, D, 2 * DFF])
    wdown_d = din("w_down", [2, DFF, D])
    y_d = dout("y", [NT, D])
    nk_d = dout("nk", [512, D])
    nv_d = dout("nv", [512, D])
    nh_d = dout("nh", [32, 128])
    epd_t = nc.dram_tensor("epd", [16 * 15 * 127 + 256], BF16, kind="Internal")
    epd = epd_t.ap()

    with ExitStack() as es:
        S = Sched(nc, es)

        def sb(name, shape, dt):
            return es.enter_context(nc.sbuf_tensor("sb_" + name, shape, dt))

        X = sb("X", [128, KC, NT], F32)
        WR = sb("WR", [128, 3, 4096], BF16)
        VT = sb("VT", [128, 256], F32)
        MOD = sb("MOD", [128, 2, 48, 2], F32)
        GS = sb("GS", [128, 2, 2, KC, 2], F32)
        ident = sb("ident", [128, 128], F32)
        ones = sb("ones", [128, 128], BF16)
        cmb = sb("cmb", [128, 64], BF16)
        scT = sb("scT", [128, KC, 2], BF16)
        nls = sb("nls", [128, 16], F32)
        ltmp = sb("ltmp", [128, 4, 16], F32)
        NH = sb("NH", [128, 32], F32)
        ARENA_B = 122880
        STRIP = 12288
        arena = sb("arena", [128, (ARENA_B + STRIP) // 4], F32)
        rstd = [carve(arena, ARENA_B + i * 2048, F32, [128, 512]) for i in range(3)]
        tmpf = [carve(arena, ARENA_B + 6144 + i * 2048, F32, [128, 512]) for i in range(3)]
        xsq8 = carve(arena, 98304, BF16, [128, KC, 512])
        banks = [es.enter_context(nc.psum_tensor("bk%d" % i, [128, 512], F32)) for i in range(8)]

        state = {"bank": 0, "ev": 0, "xsq": 0, "tmp": 0, "kst": 0, "pt": 0}

        def bank(pool=(0, 1, 2, 3, 4, 5)):
            i = pool[state["bank"] % len(pool)]
            state["bank"] += 1
            return i

        def evac_eng():
            state["ev"] += 1
            return "act" if state["ev"] % 2 else "dve"

        def copy_op(eng, out, in_, reads, writes, scale=None):
            if eng == "act":
                if scale is None:
                    S.op("act", lambda e, o=out, i=in_: e.activation(out=o, in_=i, func=AF.Copy), reads, writes)
                else:
                    S.op("act", lambda e, o=out, i=in_, s=scale: e.activation(out=o, in_=i, func=AF.Copy, scale=s),
                         reads, writes)
            else:
                if scale is None:
                    S.op(eng, lambda e, o=out, i=in_: e.tensor_copy(out=o, in_=i), reads, writes)
                else:
                    S.op(eng, lambda e, o=out, i=in_, s=scale: e.tensor_scalar(out=o, in0=i, scalar1=s, scalar2=None,
                                                                               op0=ALU.mult), reads, writes)

        def B(i):
            return ("bank", i)

        def vt(col, n=1):
            return VT[:, col:col + n]

        COND = [0, 1, 1]

        plan = []

        def add_k8(w2d, c0):
            plan.append(("k8", (w2d, c0)))

        for_layers = []
        def mod_slabs(l, qs):
            for q in qs:
                plan.append(("k8", (wmod_d[l], q * 512)))

        mod_slabs(0, range(0, 4))
        if cfg["attn"]:
            for q in range(6):
                plan.append(("k8", (wqkv_d, q * 512)))
            mod_slabs(0, range(4, 6))
            for q in range(2):
                plan.append(("k8", (wo_d, q * 512)))
        else:
            mod_slabs(0, range(4, 6))
        mod_slabs(0, range(6, 10))
        if cfg["ffn0"]:
            for j in range(11):
                plan.append(("pair", (wgu_d[0], j * 256, DFF)))
            mod_slabs(0, range(10, 12))
            for m in range(8):
                plan.append(("down", (wdown_d[0], m * 128)))
        else:
            mod_slabs(0, range(10, 12))
        mod_slabs(1, range(0, 4))
        if cfg["lru"]:
            for j in range(4):
                plan.append(("pair", (win_d, j * 256, 1024)))
            mod_slabs(1, range(4, 6))
            for q in range(2):
                plan.append(("k8", (wout_d, q * 512)))
        else:
            mod_slabs(1, range(4, 6))
        mod_slabs(1, range(6, 10))
        if cfg["ffn1"]:
            for j in range(11):
                plan.append(("pair", (wgu_d[1], j * 256, DFF)))
            mod_slabs(1, range(10, 12))
            for m in range(8):
                plan.append(("down", (wdown_d[1], m * 128)))
        else:
            mod_slabs(1, range(10, 12))

        wstate = {"loaded": 0, "next": 0}

        def w_load(j):
            kind, a = plan[j]
            s = j % 3
            res = [("ws", s)]
            if kind == "k8":
                w2d, c0 = a
                src = w2d[:, c0:c0 + 512].rearrange("(kc p) n -> p kc n", p=128)
                dst = WR[:, s, :].rearrange("p (kc n) -> p kc n", n=512)
                S.dma("pool", lambda e, o=dst, i=src: e.dma_start(out=o, in_=i), writes=res)
            elif kind == "pair":
                w2d, c0, off = a
                dst = WR[:, s, :].rearrange("p (kc t n) -> p kc t n", t=2, n=256)
                for t in range(2):
                    src = w2d[:, t * off + c0:t * off + c0 + 256].rearrange("(kc p) n -> p kc n", p=128)
                    S.dma("pool", lambda e, o=dst[:, :, t, :], i=src: e.dma_start(out=o, in_=i), writes=res)
            elif kind == "down":
                w2d, c0 = a
                src = w2d[:, c0:c0 + 128].rearrange("(kc p) n -> p kc n", p=128)
                dst = WR[:, s, 0:FC * 128].rearrange("p (kc n) -> p kc n", n=128)
                S.dma("pool", lambda e, o=dst, i=src: e.dma_start(out=o, in_=i), writes=res)
            elif kind == "lrug":
                dst = WR[:, s, :].rearrange("p (w d n j) -> p w d n j", w=2, d=2, n=8)
                for wi_, wd in enumerate((wa_d, wi_d)):
                    for d in range(2):
                        src = wd[d].rearrange("n k j -> k n j")
                        S.dma("pool", lambda e, o=dst[:, wi_, d, :, :], i=src: e.dma_start(out=o, in_=i), writes=res)

        def w_get(kind, ahead=3):
            i = wstate["next"]
            wstate["next"] += 1
            assert plan[i][0] == kind, (i, plan[i][0], kind)
            while wstate["loaded"] < min(i + ahead, len(plan)):
                w_load(wstate["loaded"])
                wstate["loaded"] += 1
            return i % 3

        while wstate["loaded"] < min(3, len(plan)):
            w_load(wstate["loaded"])
            wstate["loaded"] += 1

        S0 = carve(arena, 0, F32, [128, 2, 128])
        cmf = carve(arena, 1024, F32, [128, 64])
        S.dma("sp", lambda e: e.dma_start(out=ident[:], in_=ident_d), writes=["ident"])
        S.dma("sp", lambda e: e.dma_start(out=S0, in_=smalls_d.rearrange("(t r) c -> r t c", t=2)), writes=["S0"])
        S.op("pool", lambda e: e.memset(ones[:], 1.0), writes=["ones"])
        S.dma("sp", lambda e: e.dma_start(out=cmf, in_=cm_d), writes=["cmf"])
        S.op("pool", lambda e: e.tensor_copy(out=cmb[:], in_=cmf), reads=["cmf"], writes=["cmb"])

        def f_sm(e):
            e.transpose(out=banks[6][:, 0:128], in_=S0[:, 0, :], identity=ident[:])
            return e.transpose(out=banks[6][:, 128:256], in_=S0[:, 1, :], identity=ident[:])

        S.op("pe", f_sm, reads=["ident", "S0"], writes=[B(6)])
        S.op("dve", lambda e: e.tensor_copy(out=VT[:], in_=banks[6][:, 0:256]), reads=[B(6)], writes=["VT"])
        C_BMOD, C_NG, C_FG, C_CW, C_CB, C_BA, C_BI, C_LAM, C_CP, C_H0 = 0, 96, 128, 136, 168, 176, 192, 208, 224, 240
        for cnd in range(2):
            S.op("act", lambda e, c=cnd: e.activation(out=scT[:, :, c], in_=VT[:, C_CP + c * 8:C_CP + c * 8 + 8],
                                                      func=AF.Silu), reads=["VT"], writes=["scT"])

        def do_mod(l, qs):
            for q in qs:
                s = w_get("k8")
                wv = WR[:, s, :].rearrange("p (kc n) -> p kc n", n=512)

                def f(e, wv=wv, q=q):
                    ins = None
                    for jj in range(4):
                        j = 4 * q + jj
                        for kc in range(KC):
                            ins = e.matmul(banks[7][:, 2 * j:2 * j + 2], lhsT=wv[:, kc, jj * 128:(jj + 1) * 128],
                                           rhs=scT[:, kc, :], start=(kc == 0), stop=(kc == KC - 1))
                    return ins

                S.op("pe", f, reads=[("ws", s), "scT"], writes=[B(7)])
                pv = banks[7][:, 0:96].rearrange("p (j c) -> p j c", c=2)
                for cnd in range(2):
                    S.op("dve", lambda e, q=q, c=cnd, l=l: e.tensor_tensor(
                        out=MOD[:, l, 4 * q:4 * q + 4, c], in0=pv[:, 4 * q:4 * q + 4, c],
                        in1=VT[:, C_BMOD + l * 48 + 4 * q:C_BMOD + l * 48 + 4 * q + 4], op=ALU.add),
                        reads=[B(7), "VT"], writes=[("mod", l, q)])

        def mod_vec(l, which, kc, cnd):
            return MOD[:, l, which * 8 + kc, cnd:cnd + 1]

        def mod_res(l, which):
            return [("mod", l, 2 * which), ("mod", l, 2 * which + 1)]

        def make_gs(l, s):
            which = 1 if s == 0 else 4
            for cnd in range(2):
                S.op("dve", lambda e, c=cnd: e.tensor_scalar(out=GS[:, l, s, :, c], in0=MOD[:, l, which * 8:which * 8 + 8, c],
                                                             scalar1=1.0, scalar2=None, op0=ALU.add),
                     reads=mod_res(l, which), writes=[("gs", l, s, cnd)])
                S.op("dve", lambda e, c=cnd: e.tensor_tensor(out=GS[:, l, s, :, c], in0=GS[:, l, s, :, c],
                                                             in1=VT[:, C_NG + l * 16 + s * 8:C_NG + l * 16 + s * 8 + 8],
                                                             op=ALU.mult),
                     reads=[("gs", l, s, cnd), "VT"], writes=[("gs", l, s, cnd)])

        def stats(tt):
            cs = slice(tt * 512, (tt + 1) * 512)
            S.op("act", lambda e: e.activation(out=xsq8[:, 0:4, :], in_=X[:, 0:4, cs], func=AF.Square),
                 reads=[("X", kc, tt) for kc in range(4)], writes=[("xsq", 0)])
            S.op("dve", lambda e: e.tensor_tensor(out=xsq8[:, 4:8, :], in0=X[:, 4:8, cs], in1=X[:, 4:8, cs], op=ALU.mult),
                 reads=[("X", kc, tt) for kc in range(4, 8)], writes=[("xsq", 1)])

            def f(e):
                ins = None
                for kc in range(KC):
                    ins = e.matmul(banks[6][:, :], lhsT=ones[:], rhs=xsq8[:, kc, :], start=(kc == 0), stop=(kc == KC - 1))
                return ins

            S.op("pe", f, reads=[("xsq", 0), ("xsq", 1), "ones"], writes=[B(6)])
            r = tt % 3
            S.op("act", lambda e, r=r: e.activation(out=rstd[r], in_=banks[6][:, :], func=AF.Ln, scale=1.0 / D, bias=EPS),
                 reads=[B(6)], writes=[("rstd", r)])
            S.op("act", lambda e, r=r: e.activation(out=rstd[r], in_=rstd[r], func=AF.Exp, scale=-0.5),
                 reads=[("rstd", r)], writes=[("rstd", r)])
            return r

        xm = carve(arena, 0, BF16, [128, KC, NT])

        def norm_mod(l, s):
            sh = 0 if s == 0 else 3
            make_gs(l, s)

            def modulate(tt, r):
                cnd = COND[tt]
                cs = slice(tt * 512, (tt + 1) * 512)
                for kc in range(KC):
                    b = state["tmp"] % 3
                    state["tmp"] += 1
                    S.op("dve", lambda e, b=b, kc=kc, r=r, cs=cs: e.tensor_tensor(out=tmpf[b], in0=X[:, kc, cs],
                                                                                 in1=rstd[r], op=ALU.mult),
                         reads=[("X", kc, tt), ("rstd", r)], writes=[("tmpf", b)])
                    S.op("act", lambda e, b=b, kc=kc, cs=cs, cnd=cnd: e.activation(
                        out=xm[:, kc, cs], in_=tmpf[b], func=AF.Identity,
                        scale=GS[:, l, s, kc, cnd:cnd + 1], bias=mod_vec(l, sh, kc, cnd)),
                        reads=[("tmpf", b), ("gs", l, s, cnd)] + mod_res(l, sh), writes=[("xm", kc, tt)])

            r0 = stats(0)
            r1 = stats(1)
            modulate(0, r0)
            r2 = stats(2)
            modulate(1, r1)
            modulate(2, r2)

        def residual(l, which, bk, m, tt):
            cnd = COND[tt]
            cs = slice(tt * 512, (tt + 1) * 512)
            S.op("dve", lambda e: e.scalar_tensor_tensor(out=X[:, m, cs], in0=banks[bk][:, :],
                                                         scalar=mod_vec(l, which, m, cnd), in1=X[:, m, cs],
                                                         op0=ALU.mult, op1=ALU.add),
                 reads=[B(bk), ("X", m, tt)] + mod_res(l, which), writes=[("X", m, tt)])

        do_mod(0, [0, 1, 2])

        xst = [carve(arena, 8192 + i * 16384, F32, [128, 4, D]) for i in range(2)]
        for tt in range(3):
            st = xst[tt % 2]
            for i in range(4):
                g = tt * 4 + i
                S.dma("sp", lambda e, st=st, i=i, g=g: e.dma_start(out=st[:, i, :], in_=x_d[g * 128:(g + 1) * 128, :]),
                      writes=[("xst", tt % 2, i)])
            for kc in range(KC):
                bk = bank()

                def f(e, st=st, kc=kc, bk=bk):
                    ins = None
                    for i in range(4):
                        ins = e.transpose(out=banks[bk][:, i * 128:(i + 1) * 128], in_=st[:, i, kc * 128:(kc + 1) * 128],
                                          identity=ident[:])
                    return ins

                S.op("pe", f, reads=[("xst", tt % 2, i) for i in range(4)] + ["ident"], writes=[B(bk)])
                copy_op(evac_eng(), X[:, kc, tt * 512:(tt + 1) * 512], banks[bk][:, :], [B(bk)], [("X", kc, tt)])
        S.barrier()

        def ffn(l):
            norm_mod(l, 1)
            hbuf = carve(arena, 24576, BF16, [128, FC, NT])
            sgs = [carve(arena, 24576 + 67584 + i * 2048, F32, [128, 512]) for i in range(3)]
            sgi = [0]
            for jp in range(11):
                s = w_get("pair")
                wv = WR[:, s, :].rearrange("p (kc t n) -> p kc t n", t=2, n=256)
                for jj in range(2):
                    j = 2 * jp + jj
                    for tt in range(3):
                        cs = slice(tt * 512, (tt + 1) * 512)
                        bg = bank()
                        bu = bank()
                        for t, bk in ((0, bg), (1, bu)):
                            def f(e, t=t, bk=bk, jj=jj, cs=cs, wv=wv):
                                ins = None
                                for kc in range(KC):
                                    ins = e.matmul(banks[bk][:, :], lhsT=wv[:, kc, t, jj * 128:(jj + 1) * 128],
                                                   rhs=xm[:, kc, cs], start=(kc == 0), stop=(kc == KC - 1))
                                return ins

                            S.op("pe", f, reads=[("ws", s)] + [("xm", kc, tt) for kc in range(KC)], writes=[B(bk)])
                        g = sgi[0] % 3
                        sgi[0] += 1
                        S.op("act", lambda e, g=g, bg=bg: e.activation(out=sgs[g], in_=banks[bg][:, :], func=AF.Silu),
                             reads=[B(bg)], writes=[("sg", g)])
                        S.op("dve", lambda e, g=g, bu=bu, j=j, cs=cs: e.tensor_tensor(out=hbuf[:, j, cs], in0=banks[bu][:, :],
                                                                                     in1=sgs[g], op=ALU.mult),
                             reads=[B(bu), ("sg", g)], writes=[("h", j, tt)])
            if (l == 0 and True) or l == 1:
                do_mod(l, range(10, 12))
            for m in range(KC):
                s = w_get("down")
                wv = WR[:, s, 0:FC * 128].rearrange("p (kc n) -> p kc n", n=128)
                for tt in range(3):
                    cs = slice(tt * 512, (tt + 1) * 512)
                    bk = bank()

                    def f(e, bk=bk, cs=cs, wv=wv):
                        ins = None
                        for j in range(FC):
                            ins = e.matmul(banks[bk][:, :], lhsT=wv[:, j, :], rhs=hbuf[:, j, cs],
                                           start=(j == 0), stop=(j == FC - 1))
                        return ins

                    S.op("pe", f, reads=[("ws", s)] + [("h", j, tt) for j in range(FC)], writes=[B(bk)])
                    residual(l, 5, bk, m, tt)
            S.barrier()

        def attention():
            l = 0
            norm_mod(0, 0)
            QA = carve(arena, 24576, BF16, [128, KC, NT])
            KT = carve(arena, 49152, BF16, [128, KC, NT])
            VA = carve(arena, 73728, BF16, [128, 12, 8, 192])
            A0 = 0
            Pt = [carve(arena, A0 + 18432 + i * 1024, BF16, [128, 512]) for i in range(6)]
            Ct = [carve(arena, A0 + 4096 + i * 2816, BF16, [128, 22, 64]) for i in range(2)]
            rden = [carve(arena, A0 + 9728 + i * 2048, F32, [128, 512]) for i in range(2)]
            QZ = [[carve(arena, A0 + 13824 + (hp_ * 2 + i) * 1024, BF16, [128, 512]) for i in range(2)] for hp_ in range(2)]
            R1 = carve(arena, 110592 + 128, F32, [128, 2, 31])
            EPs = carve(arena, 110592 + 128 + 248, BF16, [128, 2, 128])
            kst = carve(arena, 110592 + 1024, F32, [128, 2816])
            if cfg.get("a_vam", True):
                S.op("pool", lambda e: e.memset(VA[:, :, :, 64:128], 1.0),
                     writes=[("vaug", t) for t in range(12)] + [("xsq", 0), ("xsq", 1)])
            ETAB = cfg.get("a_etab", True)
            if ETAB:
                S.dma("sp", lambda e: e.dma_start(out=R1[0:120], in_=rpb_d.rearrange("(hg h) r c -> (h r) hg c", hg=2)),
                      writes=["R1"])
                S.op("pool", lambda e: e.memset(EPs[0:120], 0.0), writes=["EPs"])
                S.op("act", lambda e: e.activation(out=EPs[0:120, :, 48:79], in_=R1[0:120, :, :], func=AF.Exp),
                     reads=["R1", "EPs"], writes=["EPs"])
                epd_v = epd[0:16 * 15 * 127].rearrange("(hg q j) -> q hg j", hg=2, j=127)
                S.dma("sp", lambda e: e.dma_start(out=epd_v, in_=EPs[0:120, :, 0:127]), reads=["EPs"], writes=["epd"])
            def prep_ct(h):
                i = h % 2
                if not ETAB:
                    return
                for a, s0 in ((0, 4), (1, 3)):
                    src = bass.AP(tensor=epd_t, offset=h * 15 * 127, ap=[[1, 64], [127, 15], [1, 64]])
                    S.dma("sp", lambda e, i=i, a=a, s0=s0, src=src: e.dma_start(
                        out=Ct[i][a * 64:(a + 1) * 64, s0:s0 + 15, :], in_=src), reads=["epd"], writes=[("ct", i)])
                S.op("pool", lambda e, i=i: e.tensor_tensor(out=Ct[i][:, 3:19, :], in0=Ct[i][:, 3:19, :],
                                                            in1=cmb[:].unsqueeze(1).to_broadcast([128, 16, 64]),
                                                            op=ALU.mult),
                     reads=[("ct", i), "cmb"], writes=[("ct", i)])

            for q in range(6):
                s = w_get("k8")
                wv = WR[:, s, :].rearrange("p (kc n) -> p kc n", n=512)
                xr_ = [("xm", kc, tt) for kc in range(KC) for tt in range(3)]
                if q < 4:
                    dst, nm, scl = (QA, "qa", 0.125) if q < 2 else (KT, "kt", None)
                    for mm in range(4):
                        c = (q % 2) * 4 + mm
                        for tt in range(3):
                            cs = slice(tt * 512, (tt + 1) * 512)
                            bk = bank()

                            def f(e, bk=bk, cs=cs, wv=wv, mm=mm):
                                ins = None
                                for kc in range(KC):
                                    ins = e.matmul(banks[bk][:, :], lhsT=wv[:, kc, mm * 128:(mm + 1) * 128],
                                                   rhs=xm[:, kc, cs], start=(kc == 0), stop=(kc == KC - 1))
                                return ins

                            S.op("pe", f, reads=[("ws", s)] + [("xm", kc, tt) for kc in range(KC)], writes=[B(bk)])
                            if nm == "qa":
                                wr = [("qa", c, 0, tt), ("qa", c, 1, tt)]
                            else:
                                wr = [("kt", c, tt)]
                            copy_op(evac_eng(), dst[:, c, cs], banks[bk][:, :], [B(bk)], wr, scale=scl)
                if q in (2, 3, 4, 5) and cfg.get("a_tok", True):
                    isv = q >= 4
                    half = q % 2
                    for g in range(12 if isv else 4):
                        bk = bank()
                        ts_ = slice(g * 128, (g + 1) * 128)
                        tt = g // 4

                        def f(e, bk=bk, ts_=ts_, wv=wv):
                            ins = None
                            for kc in range(KC):
                                ins = e.matmul(banks[bk][:, :], lhsT=xm[:, kc, ts_], rhs=wv[:, kc, :],
                                               start=(kc == 0), stop=(kc == KC - 1))
                            return ins

                        S.op("pe", f, reads=[("ws", s)] + [("xm", kc, tt) for kc in range(KC)], writes=[B(bk)])
                        TM = cfg.get("a_tokm", 3)
                        eng = evac_eng()
                        if isv and TM >= 2:
                            vtile = 8 + g if g < 4 else g - 4
                            pv4 = banks[bk][:, :].rearrange("p (a t d) -> p a t d", t=2, d=64)
                            for t in range(2):
                                copy_op(eng, VA[:, vtile, half * 4:half * 4 + 4, t * 128:t * 128 + 64], pv4[:, :, t, :],
                                        [B(bk)], [("vaug", vtile)])
                        if g < 4 and TM >= 3:
                            so = (state["kst"] % 5) * 512
                            state["kst"] += 1
                            stg = kst[:, so:so + 512]
                            copy_op(eng, stg, banks[bk][:, :], [B(bk)], [("kst", so)])
                            od = nv_d if isv else nk_d
                            if TM >= 4 or TM == 3 and cfg.get("a_tokm", 3) == 3 and not cfg.get("a_nodma", False):
                              S.dma("sp", lambda e, od=od, g=g, half=half, stg=stg: e.dma_start(
                                out=od[g * 128:(g + 1) * 128, half * 512:(half + 1) * 512], in_=stg),
                                reads=[("kst", so)], out=True)
            S.barrier()
            for i in range(2):
                S.op("pool", lambda e, i=i: e.memset(Ct[i], 0.0), writes=[("ct", i)])
            for hp_ in range(2):
                for i in range(2):
                    S.op("pool", lambda e, hp_=hp_, i=i: e.memset(QZ[hp_][i], 0.0), writes=[("qz", hp_, i)])

            def prep_qz(c, hp, i, tt, q0, n, eng="dve"):
                ps_ = slice(hp * 64, hp * 64 + 64)
                S.op(eng, lambda e: e.tensor_copy(out=QZ[hp][i][ps_, 0:n], in_=QA[ps_, c, q0:q0 + n]),
                     reads=[("qa", c, hp, tt)], writes=[("qz", hp, i)])
            do_mod(0, range(4, 6))

            def run_items(items):
                n_it = len(items)
                import os
                DEPTH = int(os.environ.get('ADEPTH', '4'))
                for n in range(n_it + DEPTH):
                    if n < n_it:
                        items[n]["S"]()
                    if n >= DEPTH:
                        items[n - DEPTH]["PV"]()

            obank = [0]

            items = []
            for s_ in range(2):
                for h in range(16):
                    c, hp = h // 2, h % 2
                    ps = slice(hp * 64, hp * 64 + 64)
                    dps = slice(64, 128) if hp == 0 else slice(0, 64)
                    q0 = s_ * 256
                    it = {}

                    def fS(c=c, ps=ps, q0=q0, s_=s_, it=it):
                        bk = bank((0, 1, 2, 3, 6, 7))
                        p = state["pt"] % 6
                        state["pt"] += 1
                        it["p"] = p

                        hp_ = 0 if ps.start == 0 else 1
                        prep_qz(c, hp_, s_, 0, q0, 256, eng="pool")

                        def f(e):
                            ins = None
                            for j in range(2):
                                ins = e.matmul(banks[bk][:, j * 256:(j + 1) * 256],
                                               lhsT=KT[:, c, q0 + j * 128:q0 + (j + 1) * 128],
                                               rhs=QZ[hp_][s_][:, 0:256], start=True, stop=True)
                            return ins

                        S.op("pe", f, reads=[("kt", c, 0), ("qz", hp_, s_)], writes=[B(bk)])
                        S.op("act", lambda e: e.activation(out=Pt[p], in_=banks[bk][:, :], func=AF.Exp),
                             reads=[B(bk)], writes=[("pt", p)])

                    def fPV(c=c, hp=hp, ps=ps, dps=dps, q0=q0, s_=s_, it=it):
                        p = it["p"]
                        ob = 4 + obank[0] % 2
                        obank[0] += 1

                        def f(e):
                            ins = None
                            for j in range(2):
                                ins = e.matmul(banks[ob][:, 0:256], lhsT=VA[:, 8 + s_ * 2 + j, c, hp * 64:hp * 64 + 128],
                                               rhs=Pt[p][:, j * 256:(j + 1) * 256], start=(j == 0), stop=(j == 1))
                            return ins

                        S.op("pe", f, reads=[("pt", p), ("vaug", 8 + s_ * 2), ("vaug", 9 + s_ * 2)], writes=[B(ob)])
                        r = ob - 4
                        S.op("act", lambda e: e.activation(out=rden[r][dps, 0:256], in_=banks[ob][dps, 0:256], func=AF.Ln),
                             reads=[B(ob)], writes=[("rden", r)])
                        S.op("act", lambda e: e.activation(out=rden[r][dps, 0:256], in_=rden[r][dps, 0:256], func=AF.Exp,
                                                           scale=-1.0),
                             reads=[("rden", r)], writes=[("rden", r)])
                        S.op("dve", lambda e: e.tensor_tensor(out=QA[ps, c, q0:q0 + 256], in0=banks[ob][ps, 0:256],
                                                              in1=rden[r][dps, 0:256], op=ALU.mult),
                             reads=[B(ob), ("rden", r)], writes=[("qa", c, hp, 0)])

                    it["S"] = fS
                    it["PV"] = fPV
                    items.append(it)
            if cfg.get("a_ctx", True):
                run_items(items)

            for ct in range(4 if cfg.get("a_cache", True) else 0):
                for t in range(2):
                    src = cv_d[ct * 128:(ct + 1) * 128, :].rearrange("p (a t d) -> p a t d", t=2, d=64)[:, :, t, :]
                    S.dma("pool", lambda e, ct=ct, t=t, src=src: e.dma_start(
                        out=VA[:, 8 + ct, :, t * 128:t * 128 + 64], in_=src), writes=[("vaug", 8 + ct)])
            kstg = kst[:, 0:2048].rearrange("p (a b) -> p a b", a=2)
            for ct in range(4 if cfg.get("a_cache", True) else 0):
                buf = ct % 2
                S.dma("sp", lambda e, ct=ct, buf=buf: e.dma_start(out=kstg[:, buf, :], in_=ck_d[ct * 128:(ct + 1) * 128, :]),
                      writes=[("kstg", buf)] + [("kst", so) for so in range(0, 2560, 512)])
                for hh in range(2):
                    bk = bank((0, 1, 2, 3))

                    def f(e, bk=bk, hh=hh, buf=buf):
                        ins = None
                        for q in range(4):
                            c = hh * 4 + q
                            ins = e.transpose(out=banks[bk][:, q * 128:(q + 1) * 128],
                                              in_=kstg[:, buf, c * 128:(c + 1) * 128], identity=ident[:])
                        return ins

                    S.op("pe", f, reads=[("kstg", buf), "ident"], writes=[B(bk)])
                    copy_op(evac_eng(), KT[:, hh * 4:hh * 4 + 4, ct * 128:(ct + 1) * 128],
                            banks[bk][:, :].rearrange("p (a b) -> p a b", b=128), [B(bk)],
                            [("kt", hh * 4 + q, 0) for q in range(4)])

            PB = [carve(arena, 112640 + i * 1024, BF16, [128, 512]) for i in range(10)] + \
                 [carve(arena, A0 + i * 1024, BF16, [128, 512]) for i in range(2)]
            for i in range(12):
                S.op("pool", lambda e, i=i: e.memset(PB[i], 0.0),
                     writes=[("pb", i), ("kstg", 0), ("kstg", 1)] + [("kst", so) for so in range(0, 2560, 512)])

            def rs_(r):
                return min(max(r - 4, 0), 8)

            def valid(rk, r):
                return rs_(r) <= rk <= rs_(r) + 7

            lat_tiles = {0: list(range(0, 6)), 1: list(range(2, 8))}
            items = []
            prep_ct(0)
            prep_qz(0, 0, 0, 1, 512, 512)
            for h in range(16):
                c, hp = h // 2, h % 2
                ps = slice(hp * 64, hp * 64 + 64)
                dps = slice(64, 128) if hp == 0 else slice(0, 64)
                for qt in range(2):
                    tt = 1 + qt
                    q0 = 512 + qt * 512
                    tiles = [("c", ct) for ct in range(4)] + [("l", kt) for kt in lat_tiles[qt]]
                    for n, (kind, t) in enumerate(tiles):
                        it = {}
                        first = (n == 0)
                        last = (n == len(tiles) - 1)

                        def fS(c=c, hp=hp, ps=ps, q0=q0, tt=tt, qt=qt, kind=kind, t=t, it=it, h=h, first=first):
                            if first:
                                nh_, nq_ = (h, 1) if qt == 0 else (h + 1, 0)
                                if nh_ < 16:
                                    prep_qz(nh_ // 2, nh_ % 2, nq_, 1 + nq_, 512 + nq_ * 512, 512)
                                if qt == 1 and h + 1 < 16:
                                    prep_ct(h + 1)
                            bk = bank((0, 1, 2, 3, 6, 7))
                            p = state["pt"] % 6
                            state["pt"] += 1
                            it["p"] = p
                            k0 = t * 128 if kind == "c" else 512 + t * 128
                            kres = ("kt", c, 0) if kind == "c" else ("kt", c, 1 + t // 4)
                            if kind == "l":
                                vu = [b for b in range(8) if valid(2 * t, 8 * qt + b) or valid(2 * t + 1, 8 * qt + b)]
                                cols = slice(vu[0] * 64, (vu[-1] + 1) * 64)
                                assert vu == list(range(vu[0], vu[-1] + 1))
                            else:
                                cols = slice(0, 512)
                            it["cols"] = cols
                            S.op("pe", lambda e: e.matmul(banks[bk][:, cols], lhsT=KT[:, c, k0:k0 + 128],
                                                          rhs=QZ[hp][qt][:, cols], start=True, stop=True),
                                 reads=[kres, ("qz", hp, qt)], writes=[B(bk)])
                            S.op("act", lambda e: e.activation(out=Pt[p][:, cols], in_=banks[bk][:, cols], func=AF.Exp),
                                 reads=[B(bk)], writes=[("pt", p)])
                            if kind == "l":
                                Dd = 2 * t - 8 * qt
                                s_hi = Dd + 11
                                ci = h % 2
                                pbi = qt * 6 + lat_tiles[qt].index(t)
                                it["pb"] = pbi
                                vb = [[b for b in range(8) if valid(2 * t + a, 8 * qt + b)] for a in range(2)]
                                if vb[0] == vb[1]:
                                    jobs = [(slice(0, 128), vb[0])]
                                else:
                                    jobs = [(slice(a * 64, (a + 1) * 64), vb[a]) for a in range(2)]
                                for psl, vbl in jobs:
                                    if not vbl:
                                        continue
                                    b_lo, b_hi = vbl[0], vbl[-1] + 1
                                    assert vbl == list(range(b_lo, b_hi))
                                    hi_ = s_hi - b_lo
                                    lo_ = s_hi - b_hi
                                    ev_ = Ct[ci][psl, hi_:(lo_ if lo_ >= 0 else None):-1, ::-1]
                                    o3 = PB[pbi][psl, b_lo * 64:b_hi * 64].rearrange("p (b c) -> p b c", c=64)
                                    i3 = Pt[p][psl, b_lo * 64:b_hi * 64].rearrange("p (b c) -> p b c", c=64)
                                    S.op("dve", lambda e, o3=o3, i3=i3, ev_=ev_: e.tensor_tensor(out=o3, in0=i3, in1=ev_, op=ALU.mult),
                                         reads=[("pt", p), ("ct", ci)], writes=[("pb", pbi)])

                        def fPV(c=c, hp=hp, ps=ps, dps=dps, q0=q0, tt=tt, kind=kind, t=t, it=it, first=first, last=last):
                            p = it["p"]
                            if first:
                                obank[0] += 1
                            ob = 4 + obank[0] % 2
                            vtile = 8 + t if kind == "c" else t
                            cols = it["cols"]
                            if kind == "l":
                                rhs_, rres = PB[it["pb"]][:, cols], ("pb", it["pb"])
                            else:
                                rhs_, rres = Pt[p], ("pt", p)
                            S.op("pe", lambda e: e.matmul(banks[ob][:, cols], lhsT=VA[:, vtile, c, hp * 64:hp * 64 + 128],
                                                          rhs=rhs_, start=first, stop=last),
                                 reads=[rres, ("vaug", vtile)], writes=[B(ob)])
                            if last:
                                r = ob - 4
                                S.op("act", lambda e: e.activation(out=rden[r][dps, :], in_=banks[ob][dps, :], func=AF.Ln),
                                     reads=[B(ob)], writes=[("rden", r)])
                                S.op("act", lambda e: e.activation(out=rden[r][dps, :], in_=rden[r][dps, :], func=AF.Exp,
                                                                   scale=-1.0),
                                     reads=[("rden", r)], writes=[("rden", r)])
                                S.op("dve", lambda e: e.tensor_tensor(out=QA[ps, c, q0:q0 + 512], in0=banks[ob][ps, :],
                                                                      in1=rden[r][dps, :], op=ALU.mult),
                                     reads=[B(ob), ("rden", r)], writes=[("qa", c, hp, tt)])

                        it["S"] = fS
                        it["PV"] = fPV
                        items.append(it)
            if cfg.get("a_lat", True):
                run_items(items)

            S.barrier()
            slabs = [w_get("k8"), w_get("k8", ahead=2)]
            for tt in range(3):
                cs = slice(tt * 512, (tt + 1) * 512)
                for q in range(2):
                    s = slabs[q]
                    wv = WR[:, s, :].rearrange("p (kc n) -> p kc n", n=512)
                    for mm in range(4):
                        m = q * 4 + mm
                        bk = bank((0, 1, 2, 3))

                        def f(e, bk=bk, cs=cs, wv=wv, mm=mm):
                            ins = None
                            for kc in range(KC):
                                ins = e.matmul(banks[bk][:, :], lhsT=wv[:, kc, mm * 128:(mm + 1) * 128],
                                               rhs=QA[:, kc, cs], start=(kc == 0), stop=(kc == KC - 1))
                            return ins

                        S.op("pe", f, reads=[("ws", s)] + [("qa", kc, hp, tt) for kc in range(KC) for hp in range(2)],
                             writes=[B(bk)])
                        residual(0, 2, bk, m, tt)

        def lru():
            l = 1
            norm_mod(1, 0)
            S.barrier()
            ybuf = carve(arena, 24576, BF16, [128, KC, NT])
            o = 49152
            xrp = carve(arena, o, F32, [128, 1548]); o += 6192
            gg = []; xc = []; xcb = []
            for i in range(2):
                gg.append(carve(arena, o, F32, [128, NT])); o += 6144
                xc.append(carve(arena, o, F32, [128, NT])); o += 6144
                xcb.append(carve(arena, o, BF16, [128, NT])); o += 3072
            dirb = []
            for d in range(2):
                ra = carve(arena, o, F32, [128, NT]); o += 6144
                itb = carve(arena, o, F32, [128, NT]); o += 6144
                hs = carve(arena, o, F32, [128, NT]); o += 6144
                dirb.append((ra, itb, hs))
            wgb = carve(arena, o, BF16, [128, 2, 2, 8, 128]); o += 8192
            assert o <= ARENA_B + STRIP, o
            lam = VT[:, C_LAM:C_LAM + 16]
            yv, wv_, dv, lw = ltmp[:, 0, :], ltmp[:, 1, :], ltmp[:, 2, :], ltmp[:, 3, :]
            S.op("act", lambda e: e.activation(out=yv, in_=lam, func=AF.Exp, scale=-1.0), reads=["VT"], writes=["l_y"])
            S.op("dve", lambda e: e.tensor_scalar(out=wv_, in0=yv, scalar1=1.0, scalar2=None, op0=ALU.add),
                 reads=["l_y"], writes=["l_w"])
            S.op("dve", lambda e: e.tensor_scalar(out=dv, in0=wv_, scalar1=-1.0, scalar2=1e-30, op0=ALU.add, op1=ALU.max),
                 reads=["l_w"], writes=["l_d"])
            S.op("dve", lambda e: e.reciprocal(out=dv, in_=dv), reads=["l_d"], writes=["l_d"])
            S.op("act", lambda e: e.activation(out=lw, in_=wv_, func=AF.Ln), reads=["l_w"], writes=["l_lw"])
            S.op("dve", lambda e: e.tensor_tensor(out=dv, in0=dv, in1=yv, op=ALU.mult), reads=["l_d", "l_y"], writes=["l_d"])
            S.op("dve", lambda e: e.scalar_tensor_tensor(out=nls[:], in0=lw, scalar=-8.0, in1=dv, op0=ALU.mult, op1=ALU.mult),
                 reads=["l_lw", "l_d"], writes=["nls"])
            nls2 = ltmp[:, 0, :]
            S.op("dve", lambda e: e.tensor_scalar(out=nls2, in0=nls[:], scalar1=2.0, scalar2=None, op0=ALU.mult),
                 reads=["nls", "l_y", "l_d"], writes=["nls2", "l_y"])
            S.op("pool", lambda e: e.memset(xrp, 0.0), writes=["xrp"])
            wg = wgb
            for wi_, wd in enumerate((wa_d, wi_d)):
                for d in range(2):
                    S.dma("pool", lambda e, o_=wgb[:, wi_, d, :, :], i_=wd[d].rearrange("n k j -> k n j"):
                          e.dma_start(out=o_, in_=i_), writes=["wgb"])
            SEQ = [(2, 0, 256), (261, 256, 256), (520, 512, 1024)]
            pair = {}

            def stageA(m):
                bf_ = m % 2
                jp, jj = m // 2, m % 2
                if jj == 0:
                    s = w_get("pair")
                    pair["s"] = s
                s = pair["s"]
                wv = WR[:, s, :].rearrange("p (kc t n) -> p kc t n", t=2, n=256)
                for tt in range(3):
                    cs = slice(tt * 512, (tt + 1) * 512)
                    bg = bank()
                    bx = bank()
                    for t, bk in ((0, bg), (1, bx)):
                        def f(e, t=t, bk=bk, jj=jj, cs=cs, wv=wv):
                            ins = None
                            for kc in range(KC):
                                ins = e.matmul(banks[bk][:, :], lhsT=wv[:, kc, t, jj * 128:(jj + 1) * 128],
                                               rhs=xm[:, kc, cs], start=(kc == 0), stop=(kc == KC - 1))
                            return ins

                        S.op("pe", f, reads=[("ws", s)] + [("xm", kc, tt) for kc in range(KC)], writes=[B(bk)])
                    S.op("act", lambda e, bg=bg, cs=cs: e.activation(out=gg[bf_][:, cs], in_=banks[bg][:, :],
                                                                     func=AF.Gelu_apprx_tanh),
                         reads=[B(bg)], writes=[("gg", bf_)])
                    if tt == 0:
                        for sq in range(2):
                            po = SEQ[sq][0]
                            S.op("act", lambda e, bx=bx, sq=sq, po=po: e.activation(
                                out=xrp[:, po:po + 256], in_=banks[bx][:, sq * 256:(sq + 1) * 256], func=AF.Copy),
                                reads=[B(bx)], writes=["xrp"])
                    else:
                        po = 520 + (tt - 1) * 512
                        S.op("act", lambda e, bx=bx, po=po: e.activation(out=xrp[:, po:po + 512], in_=banks[bx][:, :],
                                                                         func=AF.Copy),
                             reads=[B(bx)], writes=["xrp"])
                for (po, co, ln) in SEQ:
                    base = po - 2
                    S.op("dve", lambda e, base=base, co=co, ln=ln: e.tensor_scalar(
                        out=xc[bf_][:, co:co + ln], in0=xrp[:, base:base + ln], scalar1=vt(C_CW + 0 * 8 + m),
                        scalar2=vt(C_CB + m), op0=ALU.mult, op1=ALU.add), reads=["xrp", "VT"], writes=[("xc", bf_)])
                    for j in range(1, 4):
                        S.op("dve", lambda e, base=base, co=co, ln=ln, j=j: e.scalar_tensor_tensor(
                            out=xc[bf_][:, co:co + ln], in0=xrp[:, base + j:base + j + ln], scalar=vt(C_CW + j * 8 + m),
                            in1=xc[bf_][:, co:co + ln], op0=ALU.mult, op1=ALU.add),
                            reads=["xrp", ("xc", bf_), "VT"], writes=[("xc", bf_)])
                S.op("dve", lambda e: e.tensor_copy(out=xcb[bf_], in_=xc[bf_]), reads=[("xc", bf_)], writes=[("xcb", bf_)])

            def stageB(m):
                bf_ = m % 2
                for d in range(2):
                    ra, itb, hs = dirb[d]
                    for tt in range(3):
                        cs = slice(tt * 512, (tt + 1) * 512)
                        ba_ = bank()
                        bi_ = bank()
                        for w_, bk in ((0, ba_), (1, bi_)):
                            S.op("pe", lambda e, w_=w_, bk=bk, cs=cs, d=d: e.matmul(
                                banks[bk][:, :], lhsT=wg[:, w_, d, m, :], rhs=xcb[bf_][:, cs], start=True, stop=True),
                                reads=["wgb", ("xcb", bf_)], writes=[B(bk)])
                        S.op("act", lambda e, ba_=ba_, cs=cs, d=d, ra=ra: e.activation(
                            out=ra[:, cs], in_=banks[ba_][:, :], func=AF.Sigmoid, bias=vt(C_BA + d * 8 + m)),
                            reads=[B(ba_), "VT"], writes=[("ra", d)])
                        S.op("act", lambda e, bi_=bi_, cs=cs, d=d, itb=itb: e.activation(
                            out=itb[:, cs], in_=banks[bi_][:, :], func=AF.Sigmoid, bias=vt(C_BI + d * 8 + m)),
                            reads=[B(bi_), "VT"], writes=[("it", d)])
                    col = d * 8 + m
                    S.op("act", lambda e, hs=hs, ra=ra, col=col: e.activation(out=hs, in_=ra, func=AF.Exp, scale=nls2[:, col:col + 1]),
                         reads=[("ra", d), "nls2"], writes=[("hs", d)])
                    S.op("act", lambda e, ra=ra, col=col: e.activation(out=ra, in_=ra, func=AF.Exp, scale=nls[:, col:col + 1]),
                         reads=[("ra", d), "nls"], writes=[("ra", d)])
                    S.op("act", lambda e, hs=hs: e.activation(out=hs, in_=hs, func=AF.Sqrt, scale=-1.0, bias=1.0),
                         reads=[("hs", d)], writes=[("hs", d)])
                    S.op("pool", lambda e, itb=itb: e.tensor_tensor(out=itb, in0=itb, in1=xc[bf_], op=ALU.mult),
                         reads=[("it", d), ("xc", bf_)], writes=[("it", d)])
                    S.op("pool", lambda e, itb=itb, hs=hs: e.tensor_tensor(out=itb, in0=itb, in1=hs, op=ALU.mult),
                         reads=[("it", d), ("hs", d)], writes=[("it", d)])
                    for sq, (po, co, ln) in enumerate(SEQ):
                        init = 0.0 if sq < 2 else vt(C_H0 + d * 8 + m)
                        if d == 0:
                            S.op("dve", lambda e, co=co, ln=ln, init=init, ra=ra, itb=itb, hs=hs: e.tensor_tensor_scan(
                                out=hs[:, co:co + ln], data0=ra[:, co:co + ln], data1=itb[:, co:co + ln],
                                initial=init, op0=ALU.mult, op1=ALU.add),
                                reads=[("ra", d), ("it", d), "VT"], writes=[("hs", d)])
                        else:
                            lo = co - 1 if co > 0 else None
                            S.op("dve", lambda e, co=co, ln=ln, init=init, lo=lo, ra=ra, itb=itb, hs=hs: e.tensor_tensor_scan(
                                out=hs[:, co + ln - 1:lo:-1], data0=ra[:, co + ln - 1:lo:-1],
                                data1=itb[:, co + ln - 1:lo:-1], initial=init, op0=ALU.mult, op1=ALU.add),
                                reads=[("ra", d), ("it", d), "VT"], writes=[("hs", d)])
                        if sq < 2:
                            c_ = co + ln - 1 if d == 0 else co
                            nhc = sq * 16 + d * 8 + m
                            S.op("dve", lambda e, c_=c_, nhc=nhc, hs=hs: e.tensor_copy(
                                out=NH[:, nhc:nhc + 1], in_=hs[:, c_:c_ + 1]), reads=[("hs", d)], writes=["NH"])
                hf, hb = dirb[0][2], dirb[1][2]
                S.op("pool", lambda e: e.tensor_tensor(out=hf, in0=hf, in1=hb, op=ALU.add),
                     reads=[("hs", 0), ("hs", 1)], writes=[("hs", 0)])
                S.op("dve", lambda e: e.tensor_tensor(out=ybuf[:, m, :], in0=hf, in1=gg[bf_], op=ALU.mult),
                     reads=[("hs", 0), ("gg", bf_)], writes=[("y", m)])

            stageA(0)
            for m in range(8):
                if m + 1 < 8:
                    stageA(m + 1)
                stageB(m)
            S.barrier()
            S.op("pe", lambda e: e.transpose(out=banks[6][0:32, 0:128], in_=NH[:, :], identity=ident[:]),
                 reads=["NH", "ident"], writes=[B(6)])
            nhs = carve(arena, 110592, F32, [128, 128])
            S.op("dve", lambda e: e.tensor_copy(out=nhs[0:32, :], in_=banks[6][0:32, 0:128]), reads=[B(6)], writes=["nhs"])
            S.dma("sp", lambda e: e.dma_start(out=nh_d, in_=nhs[0:32, :]), reads=["nhs"], out=True)
            do_mod(1, range(4, 6))
            slabs = [w_get("k8"), w_get("k8", ahead=2)]
            for tt in range(3):
                cs = slice(tt * 512, (tt + 1) * 512)
                for q in range(2):
                    s = slabs[q]
                    wv = WR[:, s, :].rearrange("p (kc n) -> p kc n", n=512)
                    for mm in range(4):
                        m = q * 4 + mm
                        bk = bank()

                        def f(e, bk=bk, cs=cs, wv=wv, mm=mm):
                            ins = None
                            for kc in range(KC):
                                ins = e.matmul(banks[bk][:, :], lhsT=wv[:, kc, mm * 128:(mm + 1) * 128],
                                               rhs=ybuf[:, kc, cs], start=(kc == 0), stop=(kc == KC - 1))
                            return ins

                        S.op("pe", f, reads=[("ws", s)] + [("y", kc) for kc in range(KC)], writes=[B(bk)])
                        residual(1, 2, bk, m, tt)

        do_mod(0, [3])
        if cfg["attn"]:
            attention()
        else:
            do_mod(0, range(4, 6))
        do_mod(0, range(6, 10))
        if cfg["ffn0"]:
            ffn(0)
        else:
            do_mod(0, range(10, 12))
        do_mod(1, range(0, 4))
        if cfg["lru"]:
            lru()
        else:
            do_mod(1, range(4, 6))
        do_mod(1, range(6, 10))
        if cfg["ffn1"]:
            ffn(1)
        else:
            do_mod(1, range(10, 12))
        S.barrier()

        yTs = [carve(arena, 0, F32, [128, KC, 512]), carve(arena, 32768, F32, [128, KC, 512])]
        ost = [carve(arena, 16384 + i * 4096, F32, [128, D]) for i in range(4)]
        fin_r = {0: stats(0), 1: stats(1)}
        for tt in range(3):
            cs = slice(tt * 512, (tt + 1) * 512)
            if tt == 1:
                fin_r[2] = stats(2)
            r = fin_r[tt]
            yb = tt % 2
            yT = yTs[yb]
            for kc in range(KC):
                S.op("dve", lambda e, kc=kc, r=r, cs=cs, yT=yT: e.scalar_tensor_tensor(
                    out=yT[:, kc, :], in0=X[:, kc, cs], scalar=vt(C_FG + kc), in1=rstd[r], op0=ALU.mult, op1=ALU.mult),
                    reads=[("X", kc, tt), ("rstd", r), "VT"], writes=[("yT", yb, kc)])
            for i in range(4):
                g = tt * 4 + i
                ob_ = g % 4
                o_ = ost[ob_]
                for hh in range(2):
                    bk = bank()

                    def f(e, bk=bk, hh=hh, i=i, yT=yT):
                        ins = None
                        for q in range(4):
                            ins = e.transpose(out=banks[bk][:, q * 128:(q + 1) * 128],
                                              in_=yT[:, hh * 4 + q, i * 128:(i + 1) * 128], identity=ident[:])
                        return ins

                    S.op("pe", f, reads=[("yT", yb, hh * 4 + q) for q in range(4)] + ["ident"], writes=[B(bk)])
                    copy_op(evac_eng(), o_[:, hh * 512:(hh + 1) * 512], banks[bk][:, :], [B(bk)], [("ost", ob_, hh)])
                S.dma("sp", lambda e, g=g, o_=o_: e.dma_start(out=y_d[g * 128:(g + 1) * 128, :], in_=o_),
                      reads=[("ost", ob_, 0), ("ost", ob_, 1)], out=True)
        S.finish()

        block = es.enter_context(nc.Block())

        @block.tensor
        def _(e):
            S.emit("pe", e)

        @block.scalar
        def _(e):
            S.emit("act", e)

        @block.vector
        def _(e):
            S.emit("dve", e)

        @block.gpsimd
        def _(e):
            S.emit("pool", e)

        @block.sync
        def _(e):
            S.emit("sp", e)
    return nc


def _colmask():
    cm = np.zeros((128, 64), np.float32)
    for cq in range(64):
        cs = min(max(cq - 8, 0), 48)
        for p in range(128):
            ck = p % 64
            if cs <= ck < cs + 16:
                cm[p, 63 - cq] = 1.0
    return cm


def make_in_maps(inp):
    f = lambda a: np.ascontiguousarray(np.asarray(a, dtype=np.float32))
    x_prompt, x_sample = f(inp["x_prompt"]), f(inp["x_sample"])
    c, c_ctx = f(inp["c"]), f(inp["c_ctx"])
    shared = {
        "ident": np.eye(128, dtype=np.float32),
        "cm": _colmask(),
        "rpb": f(inp["attn_rpb"])[0],
        "w_mod": f(inp["w_mod"]),
        "w_qkv": f(inp["attn_w_qkv"])[0],
        "w_o": f(inp["attn_w_o"])[0],
        "w_in": f(inp["lru_w_in"])[0],
        "w_a": f(inp["lru_w_a"])[0],
        "w_i": f(inp["lru_w_i"])[0],
        "w_out": f(inp["lru_w_out"])[0],
        "w_gu": f(inp["ffn_w_gu"]),
        "w_down": f(inp["ffn_w_down"]),
    }
    common_rows = [
        f(inp["b_mod"]).reshape(96, 128),
        f(inp["norm_g"]).reshape(32, 128),
        f(inp["final_g"]).reshape(8, 128),
        f(inp["lru_conv_w"])[0].reshape(32, 128),
        f(inp["lru_conv_b"])[0].reshape(8, 128),
        f(inp["lru_b_a"])[0].reshape(16, 128),
        f(inp["lru_b_i"])[0].reshape(16, 128),
        f(inp["lru_lam"])[0].reshape(16, 128),
    ]
    maps = []
    for i in range(NCORES):
        smalls = np.concatenate(common_rows + [c_ctx.reshape(8, 128), c[i].reshape(8, 128),
                                               f(inp["state_h"])[i, 0].reshape(16, 128)], axis=0)
        assert smalls.shape == (256, 128)
        m = dict(shared)
        m["x"] = np.concatenate([x_prompt[2 * i], x_prompt[2 * i + 1], x_sample[i]], axis=0)
        m["smalls"] = np.ascontiguousarray(smalls)
        m["ck"] = f(inp["cache_k"])[i, 0].reshape(512, D)
        m["cv"] = f(inp["cache_v"])[i, 0].reshape(512, D)
        maps.append(m)
    return maps


_NC_CACHE = {}


def run(inp, cfg=FULL):
    key = tuple(sorted(cfg.items()))
    if key not in _NC_CACHE:
        _NC_CACHE[key] = build(cfg)
    nc = _NC_CACHE[key]
    import os
    if os.environ.get("DBG1CORE"):
        res = run_bass_kernel_spmd(nc, make_in_maps(inp)[:1], core_ids=[0])
        rs = [res.results[0]] * NCORES
    else:
        res = run_bass_kernel_spmd(nc, make_in_maps(inp), core_ids=list(range(NCORES)))
        rs = res.results
    y_prompt = np.stack([rs[i // 2]["y"][(i % 2) * 256:(i % 2) * 256 + 256] for i in range(16)], axis=0)
    y_sample = np.stack([rs[i]["y"][512:1536] for i in range(8)], axis=0)
    nk = np.stack([rs[i // 2]["nk"][(i % 2) * 256:(i % 2) * 256 + 256] for i in range(16)], axis=0)
    nv = np.stack([rs[i // 2]["nv"][(i % 2) * 256:(i % 2) * 256 + 256] for i in range(16)], axis=0)
    nk = nk.reshape(16, 1, 256, 16, 64)
    nv = nv.reshape(16, 1, 256, 16, 64)
    nh = np.stack([rs[i // 2]["nh"].reshape(2, 2, 1024)[i % 2] for i in range(16)], axis=0).reshape(16, 1, 2, 1024)
    return (y_prompt.astype(np.float32), y_sample.astype(np.float32), nk.astype(np.float32),
            nv.astype(np.float32), nh.astype(np.float32))


def kernel(**inputs):
    return run(inputs, FULL)
```

```python
import numpy as np
from contextlib import ExitStack
import concourse.bass as bass
import concourse.mybir as mybir
from concourse.bass_utils import run_bass_kernel_spmd

F32 = mybir.dt.float32
BF16 = mybir.dt.bfloat16
AF = mybir.ActivationFunctionType
ALU = mybir.AluOpType

D = 1024
KC = 8
NT = 1536
DFF = 2816
FC = 22
NCORES = 8
EPS = 1e-6


class Sched:
    ENG = ("pe", "act", "dve", "pool", "sp")

    def __init__(self, nc, es):
        self.nc = nc
        self.ops = {e: [] for e in self.ENG}
        self.cnt = {e: 0 for e in self.ENG}
        self.esem = {e: es.enter_context(nc.semaphore("s_" + e)) for e in self.ENG}
        self.dsem = {q: [es.enter_context(nc.semaphore("d_%s%d" % (q, i))) for i in range(n)]
                     for q, n in (("sp", 24), ("pool", 12))}
        self.dtgt = {}
        self.drr = {"sp": 0, "pool": 0}
        self.lastw = {}
        self.rd = {}
        self.seen = {e: {} for e in self.ENG}
        self.pending = {e: {} for e in self.ENG}
        self.out_tokens = []

    def _deps(self, eng, idx, reads, writes, strict=False):
        waits = dict(self.pending[eng])
        self.pending[eng] = {}
        for sk in list(waits):
            if self.seen[eng].get(sk, 0) >= waits[sk]:
                del waits[sk]

        def need(tok, raw):
            sk, v, pe, pidx = tok
            if pe == eng and eng != "pool":
                if eng == "pe":
                    return
                if raw == "war":
                    return
            if self.seen[eng].get(sk, 0) >= v:
                return
            if waits.get(sk, 0) < v:
                waits[sk] = v

        for r in reads:
            t = self.lastw.get(r)
            if t:
                need(t, True)
        for w in writes:
            t = self.lastw.get(w)
            if t:
                need(t, "waw")
            for t in self.rd.get(w, {}).values():
                need(t, "war")
        for sk, v in waits.items():
            self.seen[eng][sk] = v
        return waits

    def _commit(self, tok, reads, writes):
        key = tok[2] if tok[2] else tok[0]
        for r in reads:
            self.rd.setdefault(r, {})[key] = tok
        for w in writes:
            self.lastw[w] = tok
            self.rd[w] = {}

    def op(self, eng, fn, reads=(), writes=()):
        idx = self.cnt[eng]
        waits = self._deps(eng, idx, reads, writes)
        self.cnt[eng] += 1
        tok = (("e", eng), idx + 1, eng, idx)
        self._commit(tok, reads, writes)
        self.ops[eng].append((list(waits.items()), fn, None))
        return tok

    def dma(self, q, fn, reads=(), writes=(), out=False):
        idx = self.cnt[q]
        waits = self._deps(q, idx, reads, writes, strict=True)
        k = self.drr[q]
        self.drr[q] += 1
        sk = ("d", q, k % len(self.dsem[q]))
        prev = self.dtgt.get(sk, 0)
        if prev and self.seen[q].get(sk, 0) < prev:
            waits[sk] = prev
            self.seen[q][sk] = prev
        self.dtgt[sk] = prev + 16
        tok = (sk, prev + 16, None, None)
        self._commit(tok, reads, writes)
        self.ops[q].append((list(waits.items()), fn, sk))
        if out:
            self.out_tokens.append(tok)
        return tok

    def barrier(self):
        for e in self.ENG:
            p = self.pending[e]
            for f in self.ENG:
                if f != e and self.cnt[f] > 0:
                    sk = ("e", f)
                    p[sk] = max(p.get(sk, 0), self.cnt[f])
            for sk, v in self.dtgt.items():
                p[sk] = max(p.get(sk, 0), v)

    def finish(self):
        waits = {}
        for sk, v in self.dtgt.items():
            waits[sk] = v
        for f in self.ENG:
            if f != "sp" and self.cnt[f] > 0:
                waits[("e", f)] = self.cnt[f]
        self.ops["sp"].append((list(waits.items()), None, None))

    def sem(self, sk):
        return self.esem[sk[1]] if sk[0] == "e" else self.dsem[sk[1]][sk[2]]

    def emit(self, eng, e):
        for waits, fn, dsk in self.ops[eng]:
            attach = None
            if eng != "pe" and fn is not None and waits:
                attach = waits[-1]
                waits = waits[:-1]
            for sk, v in waits:
                e.wait_ge(self.sem(sk), v)
            if fn is None:
                continue
            if eng == "pe":
                px = _PEProxy(e, self.sem(attach[0]), attach[1]) if attach is not None else e
                ins = fn(px)
            else:
                ins = fn(e)
                if attach is not None:
                    ins._wait_ge(self.sem(attach[0]), attach[1])
            if dsk is None:
                ins.then_inc(self.esem[eng], 1)
            else:
                ins.then_inc(self.sem(dsk), 16)


class _PEProxy:
    def __init__(self, e, sem, val):
        self.e, self.sem, self.val = e, sem, val

    def _first(self, ins):
        if self.sem is not None:
            ins._wait_ge(self.sem, self.val)
            self.sem = None
        return ins

    def matmul(self, *a, **k):
        return self._first(self.e.matmul(*a, **k))

    def transpose(self, *a, **k):
        return self._first(self.e.transpose(*a, **k))


def _prod(s):
    r = 1
    for v in s:
        r *= v
    return r


def carve(base, off, dtype, shape):
    esz = 4 if dtype == F32 else 2
    nb = _prod(shape[1:]) * esz
    assert off % 4 == 0 and nb % 4 == 0
    sl = base[:, off // 4:(off + nb) // 4]
    if dtype != F32:
        sl = sl.bitcast(dtype)
    if len(shape) == 2:
        return sl
    names = "abcd"[:len(shape) - 1]
    pat = "p (%s) -> p %s" % (" ".join(names), " ".join(names))
    kw = {names[i]: shape[i + 1] for i in range(1, len(names))}
    return sl.rearrange(pat, **kw)


FULL = dict(attn=True, ffn0=True, lru=True, ffn1=True)


def build(cfg=FULL):
    nc = bass.Bass("TRN2", target_bir_lowering=False)

    def din(name, shape, dt=F32):
        return nc.dram_tensor(name, shape, dt, kind="ExternalInput").ap()

    def dout(name, shape):
        return nc.dram_tensor(name, shape, F32, kind="ExternalOutput").ap()

    x_d = din("x", [NT, D])
    smalls_d = din("smalls", [256, 128])
    ident_d = din("ident", [128, 128])
    cm_d = din("cm", [128, 64])
    ck_d = din("ck", [512, D])
    cv_d = din("cv", [512, D])
    rpb_d = din("rpb", [16, 15, 31])
    wmod_d = din("w_mod", [2, D, 6144])
    wqkv_d = din("w_qkv", [D, 3072])
    wo_d = din("w_o", [D, D])
    win_d = din("w_in", [D, 2048])
    wa_d = din("w_a", [2, 8, 128, 128])
    wi_d = din("w_i", [2, 8, 128, 128])
    wout_d = din("w_out", [D, D])
    wgu_d = din("w_gu", [2, D, 2 * DFF])
    wdown_d = din("w_down", [2, DFF, D])
    y_d = dout("y", [NT, D])
    nk_d = dout("nk", [512, D])
    nv_d = dout("nv", [512, D])
    nh_d = dout("nh", [32, 128])
    epd_t = nc.dram_tensor("epd", [16 * 15 * 127 + 256], BF16, kind="Internal")
    epd = epd_t.ap()

    with ExitStack() as es:
        S = Sched(nc, es)

        def sb(name, shape, dt):
            return es.enter_context(nc.sbuf_tensor("sb_" + name, shape, dt))

        X = sb("X", [128, KC, NT], F32)
        WR = sb("WR", [128, 3, 4096], BF16)
        VT = sb("VT", [128, 256], F32)
        MOD = sb("MOD", [128, 2, 48, 2], F32)
        GS = sb("GS", [128, 2, 2, KC, 2], F32)
        ident = sb("ident", [128, 128], F32)
        ones = sb("ones", [128, 128], BF16)
        cmb = sb("cmb", [128, 64], BF16)
        scT = sb("scT", [128, KC, 2], BF16)
        nls = sb("nls", [128, 16], F32)
        ltmp = sb("ltmp", [128, 4, 16], F32)
        NH = sb("NH", [128, 32], F32)
        ARENA_B = 122880
        STRIP = 12288
        arena = sb("arena", [128, (ARENA_B + STRIP) // 4], F32)
        rstd = [carve(arena, ARENA_B + i * 2048, F32, [128, 512]) for i in range(3)]
        tmpf = [carve(arena, ARENA_B + 6144 + i * 2048, F32, [128, 512]) for i in range(3)]
        xsq8 = carve(arena, 98304, BF16, [128, KC, 512])
        banks = [es.enter_context(nc.psum_tensor("bk%d" % i, [128, 512], F32)) for i in range(8)]

        state = {"bank": 0, "ev": 0, "xsq": 0, "tmp": 0, "kst": 0, "pt": 0}

        def bank(pool=(0, 1, 2, 3, 4, 5)):
            i = pool[state["bank"] % len(pool)]
            state["bank"] += 1
            return i

        def evac_eng():
            state["ev"] += 1
            return "act" if state["ev"] % 2 else "dve"

        def copy_op(eng, out, in_, reads, writes, scale=None):
            if eng == "act":
                if scale is None:
                    S.op("act", lambda e, o=out, i=in_: e.activation(out=o, in_=i, func=AF.Copy), reads, writes)
                else:
                    S.op("act", lambda e, o=out, i=in_, s=scale: e.activation(out=o, in_=i, func=AF.Copy, scale=s),
                         reads, writes)
            else:
                if scale is None:
                    S.op(eng, lambda e, o=out, i=in_: e.tensor_copy(out=o, in_=i), reads, writes)
                else:
                    S.op(eng, lambda e, o=out, i=in_, s=scale: e.tensor_scalar(out=o, in0=i, scalar1=s, scalar2=None,
                                                                               op0=ALU.mult), reads, writes)

        def B(i):
            return ("bank", i)

        def vt(col, n=1):
            return VT[:, col:col + n]

        COND = [0, 1, 1]

        plan = []

        def add_k8(w2d, c0):
            plan.append(("k8", (w2d, c0)))

        for_layers = []
        def mod_slabs(l, qs):
            for q in qs:
                plan.append(("k8", (wmod_d[l], q * 512)))

        mod_slabs(0, range(0, 4))
        if cfg["attn"]:
            for q in range(6):
                plan.append(("k8", (wqkv_d, q * 512)))
            mod_slabs(0, range(4, 6))
            mod_slabs(0, range(6, 10))
            for q in range(2):
                plan.append(("k8", (wo_d, q * 512)))
        else:
            mod_slabs(0, range(4, 6))
            mod_slabs(0, range(6, 10))
        if cfg["ffn0"]:
            for j in range(11):
                plan.append(("pair", (wgu_d[0], j * 256, DFF)))
            mod_slabs(0, range(10, 12))
            for m in range(8):
                plan.append(("down", (wdown_d[0], m * 128)))
        else:
            mod_slabs(0, range(10, 12))
        mod_slabs(1, range(0, 4))
        if cfg["lru"]:
            plan.append(("pair", (win_d, 0 * 256, 1024)))
            plan.append(("pair", (win_d, 1 * 256, 1024)))
            mod_slabs(1, [6])
            plan.append(("pair", (win_d, 2 * 256, 1024)))
            mod_slabs(1, [7, 8])
            plan.append(("pair", (win_d, 3 * 256, 1024)))
            mod_slabs(1, [9])
            mod_slabs(1, range(4, 6))
            for q in range(2):
                plan.append(("k8", (wout_d, q * 512)))
        else:
            mod_slabs(1, range(4, 6))
            mod_slabs(1, range(6, 10))
        if cfg["ffn1"]:
            for j in range(11):
                plan.append(("pair", (wgu_d[1], j * 256, DFF)))
            mod_slabs(1, range(10, 12))
            for m in range(8):
                plan.append(("down", (wdown_d[1], m * 128)))
        else:
            mod_slabs(1, range(10, 12))

        wstate = {"loaded": 0, "next": 0}

        def w_load(j):
            kind, a = plan[j]
            s = j % 3
            res = [("ws", s)]
            if kind == "k8":
                w2d, c0 = a
                src = w2d[:, c0:c0 + 512].rearrange("(kc p) n -> p kc n", p=128)
                dst = WR[:, s, :].rearrange("p (kc n) -> p kc n", n=512)
                S.dma("pool", lambda e, o=dst, i=src: e.dma_start(out=o, in_=i), writes=res)
            elif kind == "pair":
                w2d, c0, off = a
                dst = WR[:, s, :].rearrange("p (kc t n) -> p kc t n", t=2, n=256)
                for t in range(2):
                    src = w2d[:, t * off + c0:t * off + c0 + 256].rearrange("(kc p) n -> p kc n", p=128)
                    S.dma("pool", lambda e, o=dst[:, :, t, :], i=src: e.dma_start(out=o, in_=i), writes=res)
            elif kind == "down":
                w2d, c0 = a
                src = w2d[:, c0:c0 + 128].rearrange("(kc p) n -> p kc n", p=128)
                dst = WR[:, s, 0:FC * 128].rearrange("p (kc n) -> p kc n", n=128)
                S.dma("pool", lambda e, o=dst, i=src: e.dma_start(out=o, in_=i), writes=res)
            elif kind == "lrug":
                dst = WR[:, s, :].rearrange("p (w d n j) -> p w d n j", w=2, d=2, n=8)
                for wi_, wd in enumerate((wa_d, wi_d)):
                    for d in range(2):
                        src = wd[d].rearrange("n k j -> k n j")
                        S.dma("pool", lambda e, o=dst[:, wi_, d, :, :], i=src: e.dma_start(out=o, in_=i), writes=res)

        def w_get(kind, ahead=3):
            i = wstate["next"]
            wstate["next"] += 1
            assert plan[i][0] == kind, (i, plan[i][0], kind)
            while wstate["loaded"] < min(i + ahead, len(plan)):
                w_load(wstate["loaded"])
                wstate["loaded"] += 1
            return i % 3

        while wstate["loaded"] < min(3, len(plan)):
            w_load(wstate["loaded"])
            wstate["loaded"] += 1

        S0 = carve(arena, 0, F32, [128, 2, 128])
        cmf = carve(arena, 1024, F32, [128, 64])
        S.dma("sp", lambda e: e.dma_start(out=ident[:], in_=ident_d), writes=["ident"])
        S.dma("sp", lambda e: e.dma_start(out=S0, in_=smalls_d.rearrange("(t r) c -> r t c", t=2)), writes=["S0"])
        S.op("pool", lambda e: e.memset(ones[:], 1.0), writes=["ones"])
        S.dma("sp", lambda e: e.dma_start(out=cmf, in_=cm_d), writes=["cmf"])
        S.op("pool", lambda e: e.tensor_copy(out=cmb[:], in_=cmf), reads=["cmf"], writes=["cmb"])

        def f_sm(e):
            e.transpose(out=banks[6][:, 0:128], in_=S0[:, 0, :], identity=ident[:])
            return e.transpose(out=banks[6][:, 128:256], in_=S0[:, 1, :], identity=ident[:])

        S.op("pe", f_sm, reads=["ident", "S0"], writes=[B(6)])
        S.op("dve", lambda e: e.tensor_copy(out=VT[:], in_=banks[6][:, 0:256]), reads=[B(6)], writes=["VT"])
        C_BMOD, C_NG, C_FG, C_CW, C_CB, C_BA, C_BI, C_LAM, C_CP, C_H0 = 0, 96, 128, 136, 168, 176, 192, 208, 224, 240
        for cnd in range(2):
            S.op("act", lambda e, c=cnd: e.activation(out=scT[:, :, c], in_=VT[:, C_CP + c * 8:C_CP + c * 8 + 8],
                                                      func=AF.Silu), reads=["VT"], writes=["scT"])

        def do_mod(l, qs, ahead=3):
            for q in qs:
                s = w_get("k8", ahead=ahead)
                wv = WR[:, s, :].rearrange("p (kc n) -> p kc n", n=512)

                def f(e, wv=wv, q=q):
                    ins = None
                    for jj in range(4):
                        j = 4 * q + jj
                        for kc in range(KC):
                            ins = e.matmul(banks[7][:, 2 * j:2 * j + 2], lhsT=wv[:, kc, jj * 128:(jj + 1) * 128],
                                           rhs=scT[:, kc, :], start=(kc == 0), stop=(kc == KC - 1))
                    return ins

                S.op("pe", f, reads=[("ws", s), "scT"], writes=[B(7)])
                pv = banks[7][:, 0:96].rearrange("p (j c) -> p j c", c=2)
                for cnd in range(2):
                    S.op("dve", lambda e, q=q, c=cnd, l=l: e.tensor_tensor(
                        out=MOD[:, l, 4 * q:4 * q + 4, c], in0=pv[:, 4 * q:4 * q + 4, c],
                        in1=VT[:, C_BMOD + l * 48 + 4 * q:C_BMOD + l * 48 + 4 * q + 4], op=ALU.add),
                        reads=[B(7), "VT"], writes=[("mod", l, q)])

        def mod_vec(l, which, kc, cnd):
            return MOD[:, l, which * 8 + kc, cnd:cnd + 1]

        def mod_res(l, which):
            return [("mod", l, 2 * which), ("mod", l, 2 * which + 1)]

        def make_gs(l, s):
            which = 1 if s == 0 else 4
            for cnd in range(2):
                S.op("dve", lambda e, c=cnd: e.tensor_scalar(out=GS[:, l, s, :, c], in0=MOD[:, l, which * 8:which * 8 + 8, c],
                                                             scalar1=1.0, scalar2=None, op0=ALU.add),
                     reads=mod_res(l, which), writes=[("gs", l, s, cnd)])
                S.op("dve", lambda e, c=cnd: e.tensor_tensor(out=GS[:, l, s, :, c], in0=GS[:, l, s, :, c],
                                                             in1=VT[:, C_NG + l * 16 + s * 8:C_NG + l * 16 + s * 8 + 8],
                                                             op=ALU.mult),
                     reads=[("gs", l, s, cnd), "VT"], writes=[("gs", l, s, cnd)])

        def stats(tt):
            cs = slice(tt * 512, (tt + 1) * 512)
            S.op("act", lambda e: e.activation(out=xsq8[:, 0:4, :], in_=X[:, 0:4, cs], func=AF.Square),
                 reads=[("X", kc, tt) for kc in range(4)], writes=[("xsq", 0)])
            S.op("dve", lambda e: e.tensor_tensor(out=xsq8[:, 4:8, :], in0=X[:, 4:8, cs], in1=X[:, 4:8, cs], op=ALU.mult),
                 reads=[("X", kc, tt) for kc in range(4, 8)], writes=[("xsq", 1)])

            def f(e):
                ins = None
                for kc in range(KC):
                    ins = e.matmul(banks[6][:, :], lhsT=ones[:], rhs=xsq8[:, kc, :], start=(kc == 0), stop=(kc == KC - 1))
                return ins

            S.op("pe", f, reads=[("xsq", 0), ("xsq", 1), "ones"], writes=[B(6)])
            r = tt % 3
            S.op("act", lambda e, r=r: e.activation(out=rstd[r], in_=banks[6][:, :], func=AF.Ln, scale=1.0 / D, bias=EPS),
                 reads=[B(6)], writes=[("rstd", r)])
            S.op("act", lambda e, r=r: e.activation(out=rstd[r], in_=rstd[r], func=AF.Exp, scale=-0.5),
                 reads=[("rstd", r)], writes=[("rstd", r)])
            return r

        xm = carve(arena, 0, BF16, [128, KC, NT])

        def norm_mod(l, s):
            sh = 0 if s == 0 else 3
            make_gs(l, s)

            def modulate(tt, r):
                cnd = COND[tt]
                cs = slice(tt * 512, (tt + 1) * 512)
                for kc in range(KC):
                    b = state["tmp"] % 3
                    state["tmp"] += 1
                    S.op("dve", lambda e, b=b, kc=kc, r=r, cs=cs: e.tensor_tensor(out=tmpf[b], in0=X[:, kc, cs],
                                                                                 in1=rstd[r], op=ALU.mult),
                         reads=[("X", kc, tt), ("rstd", r)], writes=[("tmpf", b)])
                    S.op("act", lambda e, b=b, kc=kc, cs=cs, cnd=cnd: e.activation(
                        out=xm[:, kc, cs], in_=tmpf[b], func=AF.Identity,
                        scale=GS[:, l, s, kc, cnd:cnd + 1], bias=mod_vec(l, sh, kc, cnd)),
                        reads=[("tmpf", b), ("gs", l, s, cnd)] + mod_res(l, sh), writes=[("xm", kc, tt)])

            r0 = stats(0)
            r1 = stats(1)
            modulate(0, r0)
            r2 = stats(2)
            modulate(1, r1)
            modulate(2, r2)

        def residual(l, which, bk, m, tt):
            cnd = COND[tt]
            cs = slice(tt * 512, (tt + 1) * 512)
            S.op("dve", lambda e: e.scalar_tensor_tensor(out=X[:, m, cs], in0=banks[bk][:, :],
                                                         scalar=mod_vec(l, which, m, cnd), in1=X[:, m, cs],
                                                         op0=ALU.mult, op1=ALU.add),
                 reads=[B(bk), ("X", m, tt)] + mod_res(l, which), writes=[("X", m, tt)])

        do_mod(0, [0, 1, 2])

        xst = [carve(arena, 8192 + i * 16384, F32, [128, 4, D]) for i in range(2)]
        for tt in range(3):
            st = xst[tt % 2]
            for i in range(4):
                g = tt * 4 + i
                S.dma("sp", lambda e, st=st, i=i, g=g: e.dma_start(out=st[:, i, :], in_=x_d[g * 128:(g + 1) * 128, :]),
                      writes=[("xst", tt % 2, i)])
            for kc in range(KC):
                bk = bank()

                def f(e, st=st, kc=kc, bk=bk):
                    ins = None
                    for i in range(4):
                        ins = e.transpose(out=banks[bk][:, i * 128:(i + 1) * 128], in_=st[:, i, kc * 128:(kc + 1) * 128],
                                          identity=ident[:])
                    return ins

                S.op("pe", f, reads=[("xst", tt % 2, i) for i in range(4)] + ["ident"], writes=[B(bk)])
                copy_op(evac_eng(), X[:, kc, tt * 512:(tt + 1) * 512], banks[bk][:, :], [B(bk)], [("X", kc, tt)])
        S.barrier()

        def ffn(l):
            norm_mod(l, 1)
            hbuf = carve(arena, 24576, BF16, [128, FC, NT])
            sgs = [carve(arena, 24576 + 67584 + i * 2048, F32, [128, 512]) for i in range(3)]
            sgi = [0]
            for jp in range(11):
                s = w_get("pair")
                wv = WR[:, s, :].rearrange("p (kc t n) -> p kc t n", t=2, n=256)
                for jj in range(2):
                    j = 2 * jp + jj
                    for tt in range(3):
                        cs = slice(tt * 512, (tt + 1) * 512)
                        bg = bank()
                        bu = bank()
                        for t, bk in ((0, bg), (1, bu)):
                            def f(e, t=t, bk=bk, jj=jj, cs=cs, wv=wv):
                                ins = None
                                for kc in range(KC):
                                    ins = e.matmul(banks[bk][:, :], lhsT=wv[:, kc, t, jj * 128:(jj + 1) * 128],
                                                   rhs=xm[:, kc, cs], start=(kc == 0), stop=(kc == KC - 1))
                                return ins

                            S.op("pe", f, reads=[("ws", s)] + [("xm", kc, tt) for kc in range(KC)], writes=[B(bk)])
                        g = sgi[0] % 3
                        sgi[0] += 1
                        S.op("act", lambda e, g=g, bg=bg: e.activation(out=sgs[g], in_=banks[bg][:, :], func=AF.Silu),
                             reads=[B(bg)], writes=[("sg", g)])
                        S.op("dve", lambda e, g=g, bu=bu, j=j, cs=cs: e.tensor_tensor(out=hbuf[:, j, cs], in0=banks[bu][:, :],
                                                                                     in1=sgs[g], op=ALU.mult),
                             reads=[B(bu), ("sg", g)], writes=[("h", j, tt)])
            if (l == 0 and True) or l == 1:
                do_mod(l, range(10, 12))
            for m in range(KC):
                s = w_get("down")
                wv = WR[:, s, 0:FC * 128].rearrange("p (kc n) -> p kc n", n=128)
                for tt in range(3):
                    cs = slice(tt * 512, (tt + 1) * 512)
                    bk = bank()

                    def f(e, bk=bk, cs=cs, wv=wv):
                        ins = None
                        for j in range(FC):
                            ins = e.matmul(banks[bk][:, :], lhsT=wv[:, j, :], rhs=hbuf[:, j, cs],
                                           start=(j == 0), stop=(j == FC - 1))
                        return ins

                    S.op("pe", f, reads=[("ws", s)] + [("h", j, tt) for j in range(FC)], writes=[B(bk)])
                    residual(l, 5, bk, m, tt)
            S.barrier()

        def attention():
            l = 0
            norm_mod(0, 0)
            QA = carve(arena, 24576, BF16, [128, KC, NT])
            KT = carve(arena, 49152, BF16, [128, KC, NT])
            VA = carve(arena, 73728, BF16, [128, 12, 8, 192])
            A0 = 0
            Pt = [carve(arena, A0 + 18432 + i * 1024, BF16, [128, 512]) for i in range(6)]
            Ct = [carve(arena, A0 + 4096 + i * 2816, BF16, [128, 22, 64]) for i in range(2)]
            rden = [carve(arena, A0 + 9728 + i * 2048, F32, [128, 512]) for i in range(2)]
            QZ = [[carve(arena, A0 + 13824 + (hp_ * 2 + i) * 1024, BF16, [128, 512]) for i in range(2)] for hp_ in range(2)]
            R1 = carve(arena, 110592 + 128, F32, [128, 2, 31])
            EPs = carve(arena, 110592 + 128 + 248, BF16, [128, 2, 128])
            kst = carve(arena, 110592 + 1024, F32, [128, 2816])
            if cfg.get("a_vam", True):
                S.op("pool", lambda e: e.memset(VA[:, :, :, 64:128], 1.0),
                     writes=[("vaug", t) for t in range(12)] + [("xsq", 0), ("xsq", 1)])
            ETAB = cfg.get("a_etab", True)
            if ETAB:
                S.dma("sp", lambda e: e.dma_start(out=R1[0:120], in_=rpb_d.rearrange("(hg h) r c -> (h r) hg c", hg=2)),
                      writes=["R1"])
                S.op("pool", lambda e: e.memset(EPs[0:120], 0.0), writes=["EPs"])
                S.op("act", lambda e: e.activation(out=EPs[0:120, :, 48:79], in_=R1[0:120, :, :], func=AF.Exp),
                     reads=["R1", "EPs"], writes=["EPs"])
                epd_v = epd[0:16 * 15 * 127].rearrange("(hg q j) -> q hg j", hg=2, j=127)
                S.dma("sp", lambda e: e.dma_start(out=epd_v, in_=EPs[0:120, :, 0:127]), reads=["EPs"], writes=["epd"])
            def prep_ct(h):
                i = h % 2
                if not ETAB:
                    return
                for a, s0 in ((0, 4), (1, 3)):
                    src = bass.AP(tensor=epd_t, offset=h * 15 * 127, ap=[[1, 64], [127, 15], [1, 64]])
                    S.dma("sp", lambda e, i=i, a=a, s0=s0, src=src: e.dma_start(
                        out=Ct[i][a * 64:(a + 1) * 64, s0:s0 + 15, :], in_=src), reads=["epd"], writes=[("ct", i)])
                S.op("pool", lambda e, i=i: e.tensor_tensor(out=Ct[i][:, 3:19, :], in0=Ct[i][:, 3:19, :],
                                                            in1=cmb[:].unsqueeze(1).to_broadcast([128, 16, 64]),
                                                            op=ALU.mult),
                     reads=[("ct", i), "cmb"], writes=[("ct", i)])

            for q in range(6):
                s = w_get("k8")
                wv = WR[:, s, :].rearrange("p (kc n) -> p kc n", n=512)
                xr_ = [("xm", kc, tt) for kc in range(KC) for tt in range(3)]
                if q < 4:
                    dst, nm, scl = (QA, "qa", 0.125) if q < 2 else (KT, "kt", None)
                    for mm in range(4):
                        c = (q % 2) * 4 + mm
                        for tt in range(3):
                            cs = slice(tt * 512, (tt + 1) * 512)
                            bk = bank()

                            def f(e, bk=bk, cs=cs, wv=wv, mm=mm):
                                ins = None
                                for kc in range(KC):
                                    ins = e.matmul(banks[bk][:, :], lhsT=wv[:, kc, mm * 128:(mm + 1) * 128],
                                                   rhs=xm[:, kc, cs], start=(kc == 0), stop=(kc == KC - 1))
                                return ins

                            S.op("pe", f, reads=[("ws", s)] + [("xm", kc, tt) for kc in range(KC)], writes=[B(bk)])
                            if nm == "qa":
                                wr = [("qa", c, 0, tt), ("qa", c, 1, tt)]
                            else:
                                wr = [("kt", c, tt)]
                            copy_op(evac_eng(), dst[:, c, cs], banks[bk][:, :], [B(bk)], wr, scale=scl)
                if q in (2, 3, 4, 5) and cfg.get("a_tok", True):
                    isv = q >= 4
                    half = q % 2
                    for g in range(12 if isv else 4):
                        bk = bank()
                        ts_ = slice(g * 128, (g + 1) * 128)
                        tt = g // 4

                        def f(e, bk=bk, ts_=ts_, wv=wv):
                            ins = None
                            for kc in range(KC):
                                ins = e.matmul(banks[bk][:, :], lhsT=xm[:, kc, ts_], rhs=wv[:, kc, :],
                                               start=(kc == 0), stop=(kc == KC - 1))
                            return ins

                        S.op("pe", f, reads=[("ws", s)] + [("xm", kc, tt) for kc in range(KC)], writes=[B(bk)])
                        TM = cfg.get("a_tokm", 3)
                        eng = evac_eng()
                        if isv and TM >= 2:
                            vtile = 8 + g if g < 4 else g - 4
                            pv4 = banks[bk][:, :].rearrange("p (a t d) -> p a t d", t=2, d=64)
                            for t in range(2):
                                copy_op(eng, VA[:, vtile, half * 4:half * 4 + 4, t * 128:t * 128 + 64], pv4[:, :, t, :],
                                        [B(bk)], [("vaug", vtile)])
                        if g < 4 and TM >= 3:
                            so = (state["kst"] % 5) * 512
                            state["kst"] += 1
                            stg = kst[:, so:so + 512]
                            copy_op(eng, stg, banks[bk][:, :], [B(bk)], [("kst", so)])
                            od = nv_d if isv else nk_d
                            if TM >= 4 or TM == 3 and cfg.get("a_tokm", 3) == 3 and not cfg.get("a_nodma", False):
                              S.dma("sp", lambda e, od=od, g=g, half=half, stg=stg: e.dma_start(
                                out=od[g * 128:(g + 1) * 128, half * 512:(half + 1) * 512], in_=stg),
                                reads=[("kst", so)], out=True)
            S.barrier()
            for i in range(2):
                S.op("pool", lambda e, i=i: e.memset(Ct[i], 0.0), writes=[("ct", i)])
            for hp_ in range(2):
                for i in range(2):
                    S.op("pool", lambda e, hp_=hp_, i=i: e.memset(QZ[hp_][i], 0.0), writes=[("qz", hp_, i)])

            def prep_qz(c, hp, i, tt, q0, n, eng="dve"):
                ps_ = slice(hp * 64, hp * 64 + 64)
                S.op(eng, lambda e: e.tensor_copy(out=QZ[hp][i][ps_, 0:n], in_=QA[ps_, c, q0:q0 + n]),
                     reads=[("qa", c, hp, tt)], writes=[("qz", hp, i)])
            do_mod(0, range(4, 6))

            def run_items(items):
                n_it = len(items)
                import os
                DEPTH = int(os.environ.get('ADEPTH', '4'))
                for n in range(n_it + DEPTH):
                    if n < n_it:
                        items[n]["S"]()
                    if n >= DEPTH:
                        items[n - DEPTH]["PV"]()

            obank = [0]

            items = []
            for s_ in range(2):
                for h in range(16):
                    c, hp = h // 2, h % 2
                    ps = slice(hp * 64, hp * 64 + 64)
                    dps = slice(64, 128) if hp == 0 else slice(0, 64)
                    q0 = s_ * 256
                    it = {}

                    def fS(c=c, ps=ps, q0=q0, s_=s_, it=it):
                        bk = bank((0, 1, 2, 3, 6))
                        p = state["pt"] % 6
                        state["pt"] += 1
                        it["p"] = p

                        hp_ = 0 if ps.start == 0 else 1
                        prep_qz(c, hp_, s_, 0, q0, 256, eng="pool")

                        def f(e):
                            ins = None
                            for j in range(2):
                                ins = e.matmul(banks[bk][:, j * 256:(j + 1) * 256],
                                               lhsT=KT[:, c, q0 + j * 128:q0 + (j + 1) * 128],
                                               rhs=QZ[hp_][s_][:, 0:256], start=True, stop=True)
                            return ins

                        S.op("pe", f, reads=[("kt", c, 0), ("qz", hp_, s_)], writes=[B(bk)])
                        S.op("act", lambda e: e.activation(out=Pt[p], in_=banks[bk][:, :], func=AF.Exp),
                             reads=[B(bk)], writes=[("pt", p)])

                    def fPV(c=c, hp=hp, ps=ps, dps=dps, q0=q0, s_=s_, it=it):
                        p = it["p"]
                        ob = 4 + obank[0] % 2
                        obank[0] += 1

                        def f(e):
                            ins = None
                            for j in range(2):
                                ins = e.matmul(banks[ob][:, 0:256], lhsT=VA[:, 8 + s_ * 2 + j, c, hp * 64:hp * 64 + 128],
                                               rhs=Pt[p][:, j * 256:(j + 1) * 256], start=(j == 0), stop=(j == 1))
                            return ins

                        S.op("pe", f, reads=[("pt", p), ("vaug", 8 + s_ * 2), ("vaug", 9 + s_ * 2)], writes=[B(ob)])
                        r = ob - 4
                        S.op("act", lambda e: e.activation(out=rden[r][dps, 0:256], in_=banks[ob][dps, 0:256], func=AF.Ln),
                             reads=[B(ob)], writes=[("rden", r)])
                        S.op("act", lambda e: e.activation(out=rden[r][dps, 0:256], in_=rden[r][dps, 0:256], func=AF.Exp,
                                                           scale=-1.0),
                             reads=[("rden", r)], writes=[("rden", r)])
                        S.op("dve", lambda e: e.tensor_tensor(out=QA[ps, c, q0:q0 + 256], in0=banks[ob][ps, 0:256],
                                                              in1=rden[r][dps, 0:256], op=ALU.mult),
                             reads=[B(ob), ("rden", r)], writes=[("qa", c, hp, 0)])

                    it["S"] = fS
                    it["PV"] = fPV
                    items.append(it)
            if cfg.get("a_ctx", True):
                run_items(items)

            for ct in range(4 if cfg.get("a_cache", True) else 0):
                for t in range(2):
                    src = cv_d[ct * 128:(ct + 1) * 128, :].rearrange("p (a t d) -> p a t d", t=2, d=64)[:, :, t, :]
                    S.dma("pool", lambda e, ct=ct, t=t, src=src: e.dma_start(
                        out=VA[:, 8 + ct, :, t * 128:t * 128 + 64], in_=src), writes=[("vaug", 8 + ct)])
            kstg = kst[:, 0:2048].rearrange("p (a b) -> p a b", a=2)
            for ct in range(4 if cfg.get("a_cache", True) else 0):
                buf = ct % 2
                S.dma("sp", lambda e, ct=ct, buf=buf: e.dma_start(out=kstg[:, buf, :], in_=ck_d[ct * 128:(ct + 1) * 128, :]),
                      writes=[("kstg", buf)] + [("kst", so) for so in range(0, 2560, 512)])
                for hh in range(2):
                    bk = bank((0, 1, 2, 3))

                    def f(e, bk=bk, hh=hh, buf=buf):
                        ins = None
                        for q in range(4):
                            c = hh * 4 + q
                            ins = e.transpose(out=banks[bk][:, q * 128:(q + 1) * 128],
                                              in_=kstg[:, buf, c * 128:(c + 1) * 128], identity=ident[:])
                        return ins

                    S.op("pe", f, reads=[("kstg", buf), "ident"], writes=[B(bk)])
                    copy_op(evac_eng(), KT[:, hh * 4:hh * 4 + 4, ct * 128:(ct + 1) * 128],
                            banks[bk][:, :].rearrange("p (a b) -> p a b", b=128), [B(bk)],
                            [("kt", hh * 4 + q, 0) for q in range(4)])

            PB = [carve(arena, 112640 + i * 1024, BF16, [128, 512]) for i in range(10)] + \
                 [carve(arena, A0 + i * 1024, BF16, [128, 512]) for i in range(2)]
            for i in range(12):
                S.op("pool", lambda e, i=i: e.memset(PB[i], 0.0),
                     writes=[("pb", i), ("kstg", 0), ("kstg", 1)] + [("kst", so) for so in range(0, 2560, 512)])

            def rs_(r):
                return min(max(r - 4, 0), 8)

            def valid(rk, r):
                return rs_(r) <= rk <= rs_(r) + 7

            lat_tiles = {0: list(range(0, 6)), 1: list(range(2, 8))}
            items = []
            prep_ct(0)
            prep_qz(0, 0, 0, 1, 512, 512)
            for h in range(16):
                c, hp = h // 2, h % 2
                ps = slice(hp * 64, hp * 64 + 64)
                dps = slice(64, 128) if hp == 0 else slice(0, 64)
                for qt in range(2):
                    tt = 1 + qt
                    q0 = 512 + qt * 512
                    tiles = [("c", ct) for ct in range(4)] + [("l", kt) for kt in lat_tiles[qt]]
                    for n, (kind, t) in enumerate(tiles):
                        it = {}
                        first = (n == 0)
                        last = (n == len(tiles) - 1)

                        def fS(c=c, hp=hp, ps=ps, q0=q0, tt=tt, qt=qt, kind=kind, t=t, it=it, h=h, first=first):
                            if first:
                                nh_, nq_ = (h, 1) if qt == 0 else (h + 1, 0)
                                if nh_ < 16:
                                    prep_qz(nh_ // 2, nh_ % 2, nq_, 1 + nq_, 512 + nq_ * 512, 512)
                                if qt == 1 and h + 1 < 16:
                                    prep_ct(h + 1)
                                if qt == 0 and h in (2, 6, 10, 14):
                                    do_mod(0, [6 + h // 4])
                            bk = bank((0, 1, 2, 3, 6))
                            p = state["pt"] % 6
                            state["pt"] += 1
                            it["p"] = p
                            k0 = t * 128 if kind == "c" else 512 + t * 128
                            kres = ("kt", c, 0) if kind == "c" else ("kt", c, 1 + t // 4)
                            if kind == "l":
                                vu = [b for b in range(8) if valid(2 * t, 8 * qt + b) or valid(2 * t + 1, 8 * qt + b)]
                                cols = slice(vu[0] * 64, (vu[-1] + 1) * 64)
                                assert vu == list(range(vu[0], vu[-1] + 1))
                            else:
                                cols = slice(0, 512)
                            it["cols"] = cols
                            S.op("pe", lambda e: e.matmul(banks[bk][:, cols], lhsT=KT[:, c, k0:k0 + 128],
                                                          rhs=QZ[hp][qt][:, cols], start=True, stop=True),
                                 reads=[kres, ("qz", hp, qt)], writes=[B(bk)])
                            S.op("act", lambda e: e.activation(out=Pt[p][:, cols], in_=banks[bk][:, cols], func=AF.Exp),
                                 reads=[B(bk)], writes=[("pt", p)])
                            if kind == "l":
                                Dd = 2 * t - 8 * qt
                                s_hi = Dd + 11
                                ci = h % 2
                                pbi = qt * 6 + lat_tiles[qt].index(t)
                                it["pb"] = pbi
                                vb = [[b for b in range(8) if valid(2 * t + a, 8 * qt + b)] for a in range(2)]
                                if vb[0] == vb[1]:
                                    jobs = [(slice(0, 128), vb[0])]
                                else:
                                    jobs = [(slice(a * 64, (a + 1) * 64), vb[a]) for a in range(2)]
                                for psl, vbl in jobs:
                                    if not vbl:
                                        continue
                                    b_lo, b_hi = vbl[0], vbl[-1] + 1
                                    assert vbl == list(range(b_lo, b_hi))
                                    hi_ = s_hi - b_lo
                                    lo_ = s_hi - b_hi
                                    ev_ = Ct[ci][psl, hi_:(lo_ if lo_ >= 0 else None):-1, ::-1]
                                    o3 = PB[pbi][psl, b_lo * 64:b_hi * 64].rearrange("p (b c) -> p b c", c=64)
                                    i3 = Pt[p][psl, b_lo * 64:b_hi * 64].rearrange("p (b c) -> p b c", c=64)
                                    S.op("dve", lambda e, o3=o3, i3=i3, ev_=ev_: e.tensor_tensor(out=o3, in0=i3, in1=ev_, op=ALU.mult),
                                         reads=[("pt", p), ("ct", ci)], writes=[("pb", pbi)])

                        def fPV(c=c, hp=hp, ps=ps, dps=dps, q0=q0, tt=tt, kind=kind, t=t, it=it, first=first, last=last):
                            p = it["p"]
                            if first:
                                obank[0] += 1
                            ob = 4 + obank[0] % 2
                            vtile = 8 + t if kind == "c" else t
                            cols = it["cols"]
                            if kind == "l":
                                rhs_, rres = PB[it["pb"]][:, cols], ("pb", it["pb"])
                            else:
                                rhs_, rres = Pt[p], ("pt", p)
                            S.op("pe", lambda e: e.matmul(banks[ob][:, cols], lhsT=VA[:, vtile, c, hp * 64:hp * 64 + 128],
                                                          rhs=rhs_, start=first, stop=last),
                                 reads=[rres, ("vaug", vtile)], writes=[B(ob)])
                            if last:
                                r = ob - 4
                                S.op("act", lambda e: e.activation(out=rden[r][dps, :], in_=banks[ob][dps, :], func=AF.Ln),
                                     reads=[B(ob)], writes=[("rden", r)])
                                S.op("act", lambda e: e.activation(out=rden[r][dps, :], in_=rden[r][dps, :], func=AF.Exp,
                                                                   scale=-1.0),
                                     reads=[("rden", r)], writes=[("rden", r)])
                                S.op("dve", lambda e: e.tensor_tensor(out=QA[ps, c, q0:q0 + 512], in0=banks[ob][ps, :],
                                                                      in1=rden[r][dps, :], op=ALU.mult),
                                     reads=[B(ob), ("rden", r)], writes=[("qa", c, hp, tt)])

                        it["S"] = fS
                        it["PV"] = fPV
                        items.append(it)
            if cfg.get("a_lat", True):
                run_items(items)

            S.barrier()
            slabs = [w_get("k8"), w_get("k8", ahead=2)]
            for tt in range(3):
                cs = slice(tt * 512, (tt + 1) * 512)
                for q in range(2):
                    s = slabs[q]
                    wv = WR[:, s, :].rearrange("p (kc n) -> p kc n", n=512)
                    for mm in range(4):
                        m = q * 4 + mm
                        bk = bank((0, 1, 2, 3))

                        def f(e, bk=bk, cs=cs, wv=wv, mm=mm):
                            ins = None
                            for kc in range(KC):
                                ins = e.matmul(banks[bk][:, :], lhsT=wv[:, kc, mm * 128:(mm + 1) * 128],
                                               rhs=QA[:, kc, cs], start=(kc == 0), stop=(kc == KC - 1))
                            return ins

                        S.op("pe", f, reads=[("ws", s)] + [("qa", kc, hp, tt) for kc in range(KC) for hp in range(2)],
                             writes=[B(bk)])
                        residual(0, 2, bk, m, tt)

        def lru():
            l = 1
            norm_mod(1, 0)
            S.barrier()
            ybuf = carve(arena, 24576, BF16, [128, KC, NT])
            o = 49152
            xrp = carve(arena, o, F32, [128, 1548]); o += 6192
            gg = []; xc = []; xcb = []
            for i in range(2):
                gg.append(carve(arena, o, F32, [128, NT])); o += 6144
                xc.append(carve(arena, o, F32, [128, NT])); o += 6144
                xcb.append(carve(arena, o, BF16, [128, NT])); o += 3072
            dirb = []
            for d in range(2):
                ra = carve(arena, o, F32, [128, NT]); o += 6144
                itb = carve(arena, o, F32, [128, NT]); o += 6144
                hs = carve(arena, o, F32, [128, NT]); o += 6144
                dirb.append((ra, itb, hs))
            wgb = carve(arena, o, BF16, [128, 2, 2, 8, 128]); o += 8192
            assert o <= ARENA_B + STRIP, o
            lam = VT[:, C_LAM:C_LAM + 16]
            yv, wv_, dv, lw = ltmp[:, 0, :], ltmp[:, 1, :], ltmp[:, 2, :], ltmp[:, 3, :]
            S.op("act", lambda e: e.activation(out=yv, in_=lam, func=AF.Exp, scale=-1.0), reads=["VT"], writes=["l_y"])
            S.op("dve", lambda e: e.tensor_scalar(out=wv_, in0=yv, scalar1=1.0, scalar2=None, op0=ALU.add),
                 reads=["l_y"], writes=["l_w"])
            S.op("dve", lambda e: e.tensor_scalar(out=dv, in0=wv_, scalar1=-1.0, scalar2=1e-30, op0=ALU.add, op1=ALU.max),
                 reads=["l_w"], writes=["l_d"])
            S.op("dve", lambda e: e.reciprocal(out=dv, in_=dv), reads=["l_d"], writes=["l_d"])
            S.op("act", lambda e: e.activation(out=lw, in_=wv_, func=AF.Ln), reads=["l_w"], writes=["l_lw"])
            S.op("dve", lambda e: e.tensor_tensor(out=dv, in0=dv, in1=yv, op=ALU.mult), reads=["l_d", "l_y"], writes=["l_d"])
            S.op("dve", lambda e: e.scalar_tensor_tensor(out=nls[:], in0=lw, scalar=-8.0, in1=dv, op0=ALU.mult, op1=ALU.mult),
                 reads=["l_lw", "l_d"], writes=["nls"])
            nls2 = ltmp[:, 0, :]
            S.op("dve", lambda e: e.tensor_scalar(out=nls2, in0=nls[:], scalar1=2.0, scalar2=None, op0=ALU.mult),
                 reads=["nls", "l_y", "l_d"], writes=["nls2", "l_y"])
            S.op("pool", lambda e: e.memset(xrp, 0.0), writes=["xrp"])
            wg = wgb
            for wi_, wd in enumerate((wa_d, wi_d)):
                for d in range(2):
                    S.dma("pool", lambda e, o_=wgb[:, wi_, d, :, :], i_=wd[d].rearrange("n k j -> k n j"):
                          e.dma_start(out=o_, in_=i_), writes=["wgb"])
            SEQ = [(2, 0, 256), (261, 256, 256), (520, 512, 1024)]
            pair = {}

            def stageA(m):
                bf_ = m % 2
                jp, jj = m // 2, m % 2
                if jj == 0:
                    s = w_get("pair")
                    pair["s"] = s
                s = pair["s"]
                wv = WR[:, s, :].rearrange("p (kc t n) -> p kc t n", t=2, n=256)
                for tt in range(3):
                    cs = slice(tt * 512, (tt + 1) * 512)
                    bg = bank()
                    bx = bank()
                    for t, bk in ((0, bg), (1, bx)):
                        def f(e, t=t, bk=bk, jj=jj, cs=cs, wv=wv):
                            ins = None
                            for kc in range(KC):
                                ins = e.matmul(banks[bk][:, :], lhsT=wv[:, kc, t, jj * 128:(jj + 1) * 128],
                                               rhs=xm[:, kc, cs], start=(kc == 0), stop=(kc == KC - 1))
                            return ins

                        S.op("pe", f, reads=[("ws", s)] + [("xm", kc, tt) for kc in range(KC)], writes=[B(bk)])
                    S.op("act", lambda e, bg=bg, cs=cs: e.activation(out=gg[bf_][:, cs], in_=banks[bg][:, :],
                                                                     func=AF.Gelu_apprx_tanh),
                         reads=[B(bg)], writes=[("gg", bf_)])
                    if tt == 0:
                        for sq in range(2):
                            po = SEQ[sq][0]
                            S.op("dve", lambda e, bx=bx, sq=sq, po=po: e.tensor_copy(
                                out=xrp[:, po:po + 256], in_=banks[bx][:, sq * 256:(sq + 1) * 256]),
                                reads=[B(bx)], writes=["xrp"])
                    else:
                        po = 520 + (tt - 1) * 512
                        S.op("dve", lambda e, bx=bx, po=po: e.tensor_copy(out=xrp[:, po:po + 512], in_=banks[bx][:, :]),
                             reads=[B(bx)], writes=["xrp"])
                for (po, co, ln) in SEQ:
                    base = po - 2
                    S.op("dve", lambda e, base=base, co=co, ln=ln: e.tensor_scalar(
                        out=xc[bf_][:, co:co + ln], in0=xrp[:, base:base + ln], scalar1=vt(C_CW + 0 * 8 + m),
                        scalar2=vt(C_CB + m), op0=ALU.mult, op1=ALU.add), reads=["xrp", "VT"], writes=[("xc", bf_)])
                    for j in range(1, 4):
                        S.op("dve", lambda e, base=base, co=co, ln=ln, j=j: e.scalar_tensor_tensor(
                            out=xc[bf_][:, co:co + ln], in0=xrp[:, base + j:base + j + ln], scalar=vt(C_CW + j * 8 + m),
                            in1=xc[bf_][:, co:co + ln], op0=ALU.mult, op1=ALU.add),
                            reads=["xrp", ("xc", bf_), "VT"], writes=[("xc", bf_)])
                S.op("dve", lambda e: e.tensor_copy(out=xcb[bf_], in_=xc[bf_]), reads=[("xc", bf_)], writes=[("xcb", bf_)])

            def stageB(m):
                bf_ = m % 2
                for d in range(2):
                    ra, itb, hs = dirb[d]
                    for tt in range(3):
                        cs = slice(tt * 512, (tt + 1) * 512)
                        ba_ = bank()
                        bi_ = bank()
                        for w_, bk in ((0, ba_), (1, bi_)):
                            S.op("pe", lambda e, w_=w_, bk=bk, cs=cs, d=d: e.matmul(
                                banks[bk][:, :], lhsT=wg[:, w_, d, m, :], rhs=xcb[bf_][:, cs], start=True, stop=True),
                                reads=["wgb", ("xcb", bf_)], writes=[B(bk)])
                        S.op("act", lambda e, ba_=ba_, cs=cs, d=d, ra=ra: e.activation(
                            out=ra[:, cs], in_=banks[ba_][:, :], func=AF.Sigmoid, bias=vt(C_BA + d * 8 + m)),
                            reads=[B(ba_), "VT"], writes=[("ra", d)])
                        S.op("act", lambda e, bi_=bi_, cs=cs, d=d, itb=itb: e.activation(
                            out=itb[:, cs], in_=banks[bi_][:, :], func=AF.Sigmoid, bias=vt(C_BI + d * 8 + m)),
                            reads=[B(bi_), "VT"], writes=[("it", d)])
                    col = d * 8 + m
                    S.op("act", lambda e, hs=hs, ra=ra, col=col: e.activation(out=hs, in_=ra, func=AF.Exp, scale=nls2[:, col:col + 1]),
                         reads=[("ra", d), "nls2"], writes=[("hs", d)])
                    S.op("act", lambda e, ra=ra, col=col: e.activation(out=ra, in_=ra, func=AF.Exp, scale=nls[:, col:col + 1]),
                         reads=[("ra", d), "nls"], writes=[("ra", d)])
                    S.op("act", lambda e, hs=hs: e.activation(out=hs, in_=hs, func=AF.Sqrt, scale=-1.0, bias=1.0),
                         reads=[("hs", d)], writes=[("hs", d)])
                    S.op("pool", lambda e, itb=itb: e.tensor_tensor(out=itb, in0=itb, in1=xc[bf_], op=ALU.mult),
                         reads=[("it", d), ("xc", bf_)], writes=[("it", d)])
                    S.op("pool", lambda e, itb=itb, hs=hs: e.tensor_tensor(out=itb, in0=itb, in1=hs, op=ALU.mult),
                         reads=[("it", d), ("hs", d)], writes=[("it", d)])
                    for sq, (po, co, ln) in enumerate(SEQ):
                        init = 0.0 if sq < 2 else vt(C_H0 + d * 8 + m)
                        if d == 0:
                            S.op("dve", lambda e, co=co, ln=ln, init=init, ra=ra, itb=itb, hs=hs: e.tensor_tensor_scan(
                                out=hs[:, co:co + ln], data0=ra[:, co:co + ln], data1=itb[:, co:co + ln],
                                initial=init, op0=ALU.mult, op1=ALU.add),
                                reads=[("ra", d), ("it", d), "VT"], writes=[("hs", d)])
                        else:
                            lo = co - 1 if co > 0 else None
                            S.op("dve", lambda e, co=co, ln=ln, init=init, lo=lo, ra=ra, itb=itb, hs=hs: e.tensor_tensor_scan(
                                out=hs[:, co + ln - 1:lo:-1], data0=ra[:, co + ln - 1:lo:-1],
                                data1=itb[:, co + ln - 1:lo:-1], initial=init, op0=ALU.mult, op1=ALU.add),
                                reads=[("ra", d), ("it", d), "VT"], writes=[("hs", d)])
                        if sq < 2:
                            c_ = co + ln - 1 if d == 0 else co
                            nhc = sq * 16 + d * 8 + m
                            S.op("dve", lambda e, c_=c_, nhc=nhc, hs=hs: e.tensor_copy(
                                out=NH[:, nhc:nhc + 1], in_=hs[:, c_:c_ + 1]), reads=[("hs", d)], writes=["NH"])
                hf, hb = dirb[0][2], dirb[1][2]
                S.op("dve", lambda e: e.tensor_tensor(out=hf, in0=hf, in1=hb, op=ALU.add),
                     reads=[("hs", 0), ("hs", 1)], writes=[("hs", 0)])
                S.op("dve", lambda e: e.tensor_tensor(out=ybuf[:, m, :], in0=hf, in1=gg[bf_], op=ALU.mult),
                     reads=[("hs", 0), ("gg", bf_)], writes=[("y", m)])

            stageA(0)
            for m in range(8):
                if m + 1 < 8:
                    stageA(m + 1)
                stageB(m)
                if 2 <= m <= 5:
                    do_mod(1, [4 + m], ahead=2)
            S.barrier()
            S.op("pe", lambda e: e.transpose(out=banks[6][0:32, 0:128], in_=NH[:, :], identity=ident[:]),
                 reads=["NH", "ident"], writes=[B(6)])
            nhs = carve(arena, 110592, F32, [128, 128])
            S.op("dve", lambda e: e.tensor_copy(out=nhs[0:32, :], in_=banks[6][0:32, 0:128]), reads=[B(6)], writes=["nhs"])
            S.dma("sp", lambda e: e.dma_start(out=nh_d, in_=nhs[0:32, :]), reads=["nhs"], out=True)
            do_mod(1, range(4, 6))
            slabs = [w_get("k8"), w_get("k8", ahead=2)]
            for tt in range(3):
                cs = slice(tt * 512, (tt + 1) * 512)
                for q in range(2):
                    s = slabs[q]
                    wv = WR[:, s, :].rearrange("p (kc n) -> p kc n", n=512)
                    for mm in range(4):
                        m = q * 4 + mm
                        bk = bank()

                        def f(e, bk=bk, cs=cs, wv=wv, mm=mm):
                            ins = None
                            for kc in range(KC):
                                ins = e.matmul(banks[bk][:, :], lhsT=wv[:, kc, mm * 128:(mm + 1) * 128],
                                               rhs=ybuf[:, kc, cs], start=(kc == 0), stop=(kc == KC - 1))
                            return ins

                        S.op("pe", f, reads=[("ws", s)] + [("y", kc) for kc in range(KC)], writes=[B(bk)])
                        residual(1, 2, bk, m, tt)

        do_mod(0, [3])
        if cfg["attn"]:
            attention()
        else:
            do_mod(0, range(4, 6))
        if not cfg["attn"]:
            do_mod(0, range(6, 10))
        if cfg["ffn0"]:
            ffn(0)
        else:
            do_mod(0, range(10, 12))
        do_mod(1, range(0, 4))
        if cfg["lru"]:
            lru()
        else:
            do_mod(1, range(4, 6))
        if not cfg["lru"]:
            do_mod(1, range(6, 10))
        if cfg["ffn1"]:
            ffn(1)
        else:
            do_mod(1, range(10, 12))
        S.barrier()

        yTs = [carve(arena, 0, F32, [128, KC, 512]), carve(arena, 32768, F32, [128, KC, 512])]
        ost = [carve(arena, 16384 + i * 4096, F32, [128, D]) for i in range(4)]
        fin_r = {0: stats(0), 1: stats(1)}
        for tt in range(3):
            cs = slice(tt * 512, (tt + 1) * 512)
            if tt == 1:
                fin_r[2] = stats(2)
            r = fin_r[tt]
            yb = tt % 2
            yT = yTs[yb]
            for kc in range(KC):
                S.op("dve", lambda e, kc=kc, r=r, cs=cs, yT=yT: e.scalar_tensor_tensor(
                    out=yT[:, kc, :], in0=X[:, kc, cs], scalar=vt(C_FG + kc), in1=rstd[r], op0=ALU.mult, op1=ALU.mult),
                    reads=[("X", kc, tt), ("rstd", r), "VT"], writes=[("yT", yb, kc)])
            for i in range(4):
                g = tt * 4 + i
                ob_ = g % 4
                o_ = ost[ob_]
                for hh in range(2):
                    bk = bank()

                    def f(e, bk=bk, hh=hh, i=i, yT=yT):
                        ins = None
                        for q in range(4):
                            ins = e.transpose(out=banks[bk][:, q * 128:(q + 1) * 128],
                                              in_=yT[:, hh * 4 + q, i * 128:(i + 1) * 128], identity=ident[:])
                        return ins

                    S.op("pe", f, reads=[("yT", yb, hh * 4 + q) for q in range(4)] + ["ident"], writes=[B(bk)])
                    copy_op(evac_eng(), o_[:, hh * 512:(hh + 1) * 512], banks[bk][:, :], [B(bk)], [("ost", ob_, hh)])
                S.dma("sp", lambda e, g=g, o_=o_: e.dma_start(out=y_d[g * 128:(g + 1) * 128, :], in_=o_),
                      reads=[("ost", ob_, 0), ("ost", ob_, 1)], out=True)
        S.finish()

        block = es.enter_context(nc.Block())

        @block.tensor
        def _(e):
            S.emit("pe", e)

        @block.scalar
        def _(e):
            S.emit("act", e)

        @block.vector
        def _(e):
            S.emit("dve", e)

        @block.gpsimd
        def _(e):
            S.emit("pool", e)

        @block.sync
        def _(e):
            S.emit("sp", e)
    return nc


def _colmask():
    cm = np.zeros((128, 64), np.float32)
    for cq in range(64):
        cs = min(max(cq - 8, 0), 48)
        for p in range(128):
            ck = p % 64
            if cs <= ck < cs + 16:
                cm[p, 63 - cq] = 1.0
    return cm


def make_in_maps(inp):
    f = lambda a: np.ascontiguousarray(np.asarray(a, dtype=np.float32))
    x_prompt, x_sample = f(inp["x_prompt"]), f(inp["x_sample"])
    c, c_ctx = f(inp["c"]), f(inp["c_ctx"])
    shared = {
        "ident": np.eye(128, dtype=np.float32),
        "cm": _colmask(),
        "rpb": f(inp["attn_rpb"])[0],
        "w_mod": f(inp["w_mod"]),
        "w_qkv": f(inp["attn_w_qkv"])[0],
        "w_o": f(inp["attn_w_o"])[0],
        "w_in": f(inp["lru_w_in"])[0],
        "w_a": f(inp["lru_w_a"])[0],
        "w_i": f(inp["lru_w_i"])[0],
        "w_out": f(inp["lru_w_out"])[0],
        "w_gu": f(inp["ffn_w_gu"]),
        "w_down": f(inp["ffn_w_down"]),
    }
    common_rows = [
        f(inp["b_mod"]).reshape(96, 128),
        f(inp["norm_g"]).reshape(32, 128),
        f(inp["final_g"]).reshape(8, 128),
        f(inp["lru_conv_w"])[0].reshape(32, 128),
        f(inp["lru_conv_b"])[0].reshape(8, 128),
        f(inp["lru_b_a"])[0].reshape(16, 128),
        f(inp["lru_b_i"])[0].reshape(16, 128),
        f(inp["lru_lam"])[0].reshape(16, 128),
    ]
    maps = []
    for i in range(NCORES):
        smalls = np.concatenate(common_rows + [c_ctx.reshape(8, 128), c[i].reshape(8, 128),
                                               f(inp["state_h"])[i, 0].reshape(16, 128)], axis=0)
        assert smalls.shape == (256, 128)
        m = dict(shared)
        m["x"] = np.concatenate([x_prompt[2 * i], x_prompt[2 * i + 1], x_sample[i]], axis=0)
        m["smalls"] = np.ascontiguousarray(smalls)
        m["ck"] = f(inp["cache_k"])[i, 0].reshape(512, D)
        m["cv"] = f(inp["cache_v"])[i, 0].reshape(512, D)
        maps.append(m)
    return maps


_NC_CACHE = {}


def run(inp, cfg=FULL):
    key = tuple(sorted(cfg.items()))
    if key not in _NC_CACHE:
        _NC_CACHE[key] = build(cfg)
    nc = _NC_CACHE[key]
    import os
    if os.environ.get("DBG1CORE"):
        res = run_bass_kernel_spmd(nc, make_in_maps(inp)[:1], core_ids=[0])
        rs = [res.results[0]] * NCORES
    else:
        res = run_bass_kernel_spmd(nc, make_in_maps(inp), core_ids=list(range(NCORES)))
        rs = res.results
    y_prompt = np.stack([rs[i // 2]["y"][(i % 2) * 256:(i % 2) * 256 + 256] for i in range(16)], axis=0)
    y_sample = np.stack([rs[i]["y"][512:1536] for i in range(8)], axis=0)
    nk = np.stack([rs[i // 2]["nk"][(i % 2) * 256:(i % 2) * 256 + 256] for i in range(16)], axis=0)
    nv = np.stack([rs[i // 2]["nv"][(i % 2) * 256:(i % 2) * 256 + 256] for i in range(16)], axis=0)
    nk = nk.reshape(16, 1, 256, 16, 64)
    nv = nv.reshape(16, 1, 256, 16, 64)
    nh = np.stack([rs[i // 2]["nh"].reshape(2, 2, 1024)[i % 2] for i in range(16)], axis=0).reshape(16, 1, 2, 1024)
    return (y_prompt.astype(np.float32), y_sample.astype(np.float32), nk.astype(np.float32),
            nv.astype(np.float32), nh.astype(np.float32))


def kernel(**inputs):
    return run(inputs, FULL)
```

```python
import numpy as np
from contextlib import ExitStack
import concourse.bass as bass
import concourse.mybir as mybir
from concourse.bass_utils import run_bass_kernel_spmd

F32 = mybir.dt.float32
BF16 = mybir.dt.bfloat16
AF = mybir.ActivationFunctionType
ALU = mybir.AluOpType

D = 1024
KC = 8
NT = 1536
DFF = 2816
FC = 22
NCORES = 8
EPS = 1e-6


class Sched:
    ENG = ("pe", "act", "dve", "pool", "sp")

    def __init__(self, nc, es):
        self.nc = nc
        self.ops = {e: [] for e in self.ENG}
        self.cnt = {e: 0 for e in self.ENG}
        self.esem = {e: es.enter_context(nc.semaphore("s_" + e)) for e in self.ENG}
        self.dsem = {q: [es.enter_context(nc.semaphore("d_%s%d" % (q, i))) for i in range(n)]
                     for q, n in (("sp", 24), ("pool", 12))}
        self.dtgt = {}
        self.drr = {"sp": 0, "pool": 0}
        self.lastw = {}
        self.rd = {}
        self.seen = {e: {} for e in self.ENG}
        self.pending = {e: {} for e in self.ENG}
        self.out_tokens = []

    def _deps(self, eng, idx, reads, writes, strict=False):
        waits = dict(self.pending[eng])
        self.pending[eng] = {}
        for sk in list(waits):
            if self.seen[eng].get(sk, 0) >= waits[sk]:
                del waits[sk]

        def need(tok, raw):
            sk, v, pe, pidx = tok
            if pe == eng and eng != "pool":
                if eng == "pe":
                    return
                if raw == "war":
                    return
            if self.seen[eng].get(sk, 0) >= v:
                return
            if waits.get(sk, 0) < v:
                waits[sk] = v

        for r in reads:
            t = self.lastw.get(r)
            if t:
                need(t, True)
        for w in writes:
            t = self.lastw.get(w)
            if t:
                need(t, "waw")
            for t in self.rd.get(w, {}).values():
                need(t, "war")
        for sk, v in waits.items():
            self.seen[eng][sk] = v
        return waits

    def _commit(self, tok, reads, writes):
        key = tok[2] if tok[2] else tok[0]
        for r in reads:
            self.rd.setdefault(r, {})[key] = tok
        for w in writes:
            self.lastw[w] = tok
            self.rd[w] = {}

    def op(self, eng, fn, reads=(), writes=()):
        idx = self.cnt[eng]
        waits = self._deps(eng, idx, reads, writes)
        self.cnt[eng] += 1
        tok = (("e", eng), idx + 1, eng, idx)
        self._commit(tok, reads, writes)
        self.ops[eng].append((list(waits.items()), fn, None))
        return tok

    def dma(self, q, fn, reads=(), writes=(), out=False):
        idx = self.cnt[q]
        waits = self._deps(q, idx, reads, writes, strict=True)
        k = self.drr[q]
        self.drr[q] += 1
        sk = ("d", q, k % len(self.dsem[q]))
        prev = self.dtgt.get(sk, 0)
        if prev and self.seen[q].get(sk, 0) < prev:
            waits[sk] = prev
            self.seen[q][sk] = prev
        self.dtgt[sk] = prev + 16
        tok = (sk, prev + 16, None, None)
        self._commit(tok, reads, writes)
        self.ops[q].append((list(waits.items()), fn, sk))
        if out:
            self.out_tokens.append(tok)
        return tok

    def barrier(self):
        for e in self.ENG:
            p = self.pending[e]
            for f in self.ENG:
                if f != e and self.cnt[f] > 0:
                    sk = ("e", f)
                    p[sk] = max(p.get(sk, 0), self.cnt[f])
            for sk, v in self.dtgt.items():
                p[sk] = max(p.get(sk, 0), v)

    def finish(self):
        waits = {}
        for sk, v in self.dtgt.items():
            waits[sk] = v
        for f in self.ENG:
            if f != "sp" and self.cnt[f] > 0:
                waits[("e", f)] = self.cnt[f]
        self.ops["sp"].append((list(waits.items()), None, None))

    def sem(self, sk):
        return self.esem[sk[1]] if sk[0] == "e" else self.dsem[sk[1]][sk[2]]

    def emit(self, eng, e):
        for waits, fn, dsk in self.ops[eng]:
            attach = None
            if eng != "pe" and fn is not None and waits:
                attach = waits[-1]
                waits = waits[:-1]
            for sk, v in waits:
                e.wait_ge(self.sem(sk), v)
            if fn is None:
                continue
            if eng == "pe":
                px = _PEProxy(e, self.sem(attach[0]), attach[1]) if attach is not None else e
                ins = fn(px)
            else:
                ins = fn(e)
                if attach is not None:
                    ins._wait_ge(self.sem(attach[0]), attach[1])
            if dsk is None:
                ins.then_inc(self.esem[eng], 1)
            else:
                ins.then_inc(self.sem(dsk), 16)


class _PEProxy:
    def __init__(self, e, sem, val):
        self.e, self.sem, self.val = e, sem, val

    def _first(self, ins):
        if self.sem is not None:
            ins._wait_ge(self.sem, self.val)
            self.sem = None
        return ins

    def matmul(self, *a, **k):
        return self._first(self.e.matmul(*a, **k))

    def transpose(self, *a, **k):
        return self._first(self.e.transpose(*a, **k))


def _prod(s):
    r = 1
    for v in s:
        r *= v
    return r


def carve(base, off, dtype, shape):
    esz = 4 if dtype == F32 else 2
    nb = _prod(shape[1:]) * esz
    assert off % 4 == 0 and nb % 4 == 0
    sl = base[:, off // 4:(off + nb) // 4]
    if dtype != F32:
        sl = sl.bitcast(dtype)
    if len(shape) == 2:
        return sl
    names = "abcd"[:len(shape) - 1]
    pat = "p (%s) -> p %s" % (" ".join(names), " ".join(names))
    kw = {names[i]: shape[i + 1] for i in range(1, len(names))}
    return sl.rearrange(pat, **kw)


FULL = dict(attn=True, ffn0=True, lru=True, ffn1=True)


def build(cfg=FULL):
    nc = bass.Bass("TRN2", target_bir_lowering=False)

    def din(name, shape, dt=F32):
        return nc.dram_tensor(name, shape, dt, kind="ExternalInput").ap()

    def dout(name, shape):
        return nc.dram_tensor(name, shape, F32, kind="ExternalOutput").ap()

    x_d = din("x", [NT, D])
    smalls_d = din("smalls", [256, 128])
    ident_d = din("ident", [128, 128])
    cm_d = din("cm", [128, 64])
    ck_d = din("ck", [512, D])
    cv_d = din("cv", [512, D])
    rpb_d = din("rpb", [16, 15, 31])
    wmod_d = din("w_mod", [2, D, 6144])
    wqkv_d = din("w_qkv", [D, 3072])
    wo_d = din("w_o", [D, D])
    win_d = din("w_in", [D, 2048])
    wa_d = din("w_a", [2, 8, 128, 128])
    wi_d = din("w_i", [2, 8, 128, 128])
    wout_d = din("w_out", [D, D])
    wgu_d = din("w_gu", [2, D, 2 * DFF])
    wdown_d = din("w_down", [2, DFF, D])
    y_d = dout("y", [NT, D])
    nk_d = dout("nk", [512, D])
    nv_d = dout("nv", [512, D])
    nh_d = dout("nh", [32, 128])
    epd_t = nc.dram_tensor("epd", [16 * 15 * 127 + 256], BF16, kind="Internal")
    epd = epd_t.ap()

    with ExitStack() as es:
        S = Sched(nc, es)

        def sb(name, shape, dt):
            return es.enter_context(nc.sbuf_tensor("sb_" + name, shape, dt))

        X = sb("X", [128, KC, NT], F32)
        WR = sb("WR", [128, 3, 4096], BF16)
        VT = sb("VT", [128, 256], F32)
        MOD = sb("MOD", [128, 2, 48, 2], F32)
        GS = sb("GS", [128, 2, 2, KC, 2], F32)
        ident = sb("ident", [128, 128], F32)
        ones = sb("ones", [128, 128], BF16)
        cmb = sb("cmb", [128, 64], BF16)
        scT = sb("scT", [128, KC, 2], BF16)
        nls = sb("nls", [128, 16], F32)
        ltmp = sb("ltmp", [128, 4, 16], F32)
        NH = sb("NH", [128, 32], F32)
        ARENA_B = 122880
        STRIP = 12288
        arena = sb("arena", [128, (ARENA_B + STRIP) // 4], F32)
        rstd = [carve(arena, ARENA_B + i * 2048, F32, [128, 512]) for i in range(3)]
        tmpf = [carve(arena, ARENA_B + 6144 + i * 2048, F32, [128, 512]) for i in range(3)]
        xsq8 = carve(arena, 98304, BF16, [128, KC, 512])
        banks = [es.enter_context(nc.psum_tensor("bk%d" % i, [128, 512], F32)) for i in range(8)]

        state = {"bank": 0, "ev": 0, "xsq": 0, "tmp": 0, "kst": 0, "pt": 0}

        def bank(pool=(0, 1, 2, 3, 4, 5)):
            i = pool[state["bank"] % len(pool)]
            state["bank"] += 1
            return i

        def evac_eng():
            state["ev"] += 1
            return "act" if state["ev"] % 2 else "dve"

        def copy_op(eng, out, in_, reads, writes, scale=None):
            if eng == "act":
                if scale is None:
                    S.op("act", lambda e, o=out, i=in_: e.activation(out=o, in_=i, func=AF.Copy), reads, writes)
                else:
                    S.op("act", lambda e, o=out, i=in_, s=scale: e.activation(out=o, in_=i, func=AF.Copy, scale=s),
                         reads, writes)
            else:
                if scale is None:
                    S.op(eng, lambda e, o=out, i=in_: e.tensor_copy(out=o, in_=i), reads, writes)
                else:
                    S.op(eng, lambda e, o=out, i=in_, s=scale: e.tensor_scalar(out=o, in0=i, scalar1=s, scalar2=None,
                                                                               op0=ALU.mult), reads, writes)

        def B(i):
            return ("bank", i)

        def vt(col, n=1):
            return VT[:, col:col + n]

        COND = [0, 1, 1]

        plan = []

        def add_k8(w2d, c0):
            plan.append(("k8", (w2d, c0)))

        for_layers = []
        def mod_slabs(l, qs):
            for q in qs:
                plan.append(("k8", (wmod_d[l], q * 512)))

        mod_slabs(0, range(0, 4))
        if cfg["attn"]:
            for q in range(6):
                plan.append(("k8", (wqkv_d, q * 512)))
            mod_slabs(0, range(4, 6))
            mod_slabs(0, range(6, 10))
            for q in range(2):
                plan.append(("k8", (wo_d, q * 512)))
        else:
            mod_slabs(0, range(4, 6))
            mod_slabs(0, range(6, 10))
        if cfg["ffn0"]:
            for j in range(11):
                plan.append(("pair", (wgu_d[0], j * 256, DFF)))
                if j in (2, 4, 6, 8):
                    mod_slabs(1, [j // 2 - 1])
            mod_slabs(0, range(10, 12))
            for m in range(8):
                plan.append(("down", (wdown_d[0], m * 128)))
        else:
            mod_slabs(0, range(10, 12))
            mod_slabs(1, range(0, 4))
        if cfg["lru"]:
            plan.append(("pair", (win_d, 0 * 256, 1024)))
            plan.append(("pair", (win_d, 1 * 256, 1024)))
            mod_slabs(1, [6])
            plan.append(("pair", (win_d, 2 * 256, 1024)))
            mod_slabs(1, [7, 8])
            plan.append(("pair", (win_d, 3 * 256, 1024)))
            mod_slabs(1, [9])
            mod_slabs(1, range(4, 6))
            for q in range(2):
                plan.append(("k8", (wout_d, q * 512)))
        else:
            mod_slabs(1, range(4, 6))
            mod_slabs(1, range(6, 10))
        if cfg["ffn1"]:
            for j in range(11):
                plan.append(("pair", (wgu_d[1], j * 256, DFF)))
            mod_slabs(1, range(10, 12))
            for m in range(8):
                plan.append(("down", (wdown_d[1], m * 128)))
        else:
            mod_slabs(1, range(10, 12))

        wstate = {"loaded": 0, "next": 0}

        def w_load(j):
            kind, a = plan[j]
            s = j % 3
            res = [("ws", s)]
            if kind == "k8":
                w2d, c0 = a
                src = w2d[:, c0:c0 + 512].rearrange("(kc p) n -> p kc n", p=128)
                dst = WR[:, s, :].rearrange("p (kc n) -> p kc n", n=512)
                S.dma("pool", lambda e, o=dst, i=src: e.dma_start(out=o, in_=i), writes=res)
            elif kind == "pair":
                w2d, c0, off = a
                dst = WR[:, s, :].rearrange("p (kc t n) -> p kc t n", t=2, n=256)
                for t in range(2):
                    src = w2d[:, t * off + c0:t * off + c0 + 256].rearrange("(kc p) n -> p kc n", p=128)
                    S.dma("pool", lambda e, o=dst[:, :, t, :], i=src: e.dma_start(out=o, in_=i), writes=res)
            elif kind == "down":
                w2d, c0 = a
                src = w2d[:, c0:c0 + 128].rearrange("(kc p) n -> p kc n", p=128)
                dst = WR[:, s, 0:FC * 128].rearrange("p (kc n) -> p kc n", n=128)
                S.dma("pool", lambda e, o=dst, i=src: e.dma_start(out=o, in_=i), writes=res)
            elif kind == "lrug":
                dst = WR[:, s, :].rearrange("p (w d n j) -> p w d n j", w=2, d=2, n=8)
                for wi_, wd in enumerate((wa_d, wi_d)):
                    for d in range(2):
                        src = wd[d].rearrange("n k j -> k n j")
                        S.dma("pool", lambda e, o=dst[:, wi_, d, :, :], i=src: e.dma_start(out=o, in_=i), writes=res)

        def w_get(kind, ahead=3):
            i = wstate["next"]
            wstate["next"] += 1
            assert plan[i][0] == kind, (i, plan[i][0], kind)
            while wstate["loaded"] < min(i + ahead, len(plan)):
                w_load(wstate["loaded"])
                wstate["loaded"] += 1
            return i % 3

        while wstate["loaded"] < min(3, len(plan)):
            w_load(wstate["loaded"])
            wstate["loaded"] += 1

        S0 = carve(arena, 0, F32, [128, 2, 128])
        cmf = carve(arena, 1024, F32, [128, 64])
        S.dma("sp", lambda e: e.dma_start(out=ident[:], in_=ident_d), writes=["ident"])
        S.dma("sp", lambda e: e.dma_start(out=S0, in_=smalls_d.rearrange("(t r) c -> r t c", t=2)), writes=["S0"])
        S.op("pool", lambda e: e.memset(ones[:], 1.0), writes=["ones"])
        S.dma("sp", lambda e: e.dma_start(out=cmf, in_=cm_d), writes=["cmf"])
        S.op("pool", lambda e: e.tensor_copy(out=cmb[:], in_=cmf), reads=["cmf"], writes=["cmb"])

        def f_sm(e):
            e.transpose(out=banks[6][:, 0:128], in_=S0[:, 0, :], identity=ident[:])
            return e.transpose(out=banks[6][:, 128:256], in_=S0[:, 1, :], identity=ident[:])

        S.op("pe", f_sm, reads=["ident", "S0"], writes=[B(6)])
        S.op("dve", lambda e: e.tensor_copy(out=VT[:], in_=banks[6][:, 0:256]), reads=[B(6)], writes=["VT"])
        C_BMOD, C_NG, C_FG, C_CW, C_CB, C_BA, C_BI, C_LAM, C_CP, C_H0 = 0, 96, 128, 136, 168, 176, 192, 208, 224, 240
        for cnd in range(2):
            S.op("act", lambda e, c=cnd: e.activation(out=scT[:, :, c], in_=VT[:, C_CP + c * 8:C_CP + c * 8 + 8],
                                                      func=AF.Silu), reads=["VT"], writes=["scT"])

        def do_mod(l, qs, ahead=3):
            for q in qs:
                s = w_get("k8", ahead=ahead)
                wv = WR[:, s, :].rearrange("p (kc n) -> p kc n", n=512)

                def f(e, wv=wv, q=q):
                    ins = None
                    for jj in range(4):
                        j = 4 * q + jj
                        for kc in range(KC):
                            ins = e.matmul(banks[7][:, 2 * j:2 * j + 2], lhsT=wv[:, kc, jj * 128:(jj + 1) * 128],
                                           rhs=scT[:, kc, :], start=(kc == 0), stop=(kc == KC - 1))
                    return ins

                S.op("pe", f, reads=[("ws", s), "scT"], writes=[B(7)])
                pv = banks[7][:, 0:96].rearrange("p (j c) -> p j c", c=2)
                for cnd in range(2):
                    S.op("dve", lambda e, q=q, c=cnd, l=l: e.tensor_tensor(
                        out=MOD[:, l, 4 * q:4 * q + 4, c], in0=pv[:, 4 * q:4 * q + 4, c],
                        in1=VT[:, C_BMOD + l * 48 + 4 * q:C_BMOD + l * 48 + 4 * q + 4], op=ALU.add),
                        reads=[B(7), "VT"], writes=[("mod", l, q)])

        def mod_vec(l, which, kc, cnd):
            return MOD[:, l, which * 8 + kc, cnd:cnd + 1]

        def mod_res(l, which):
            return [("mod", l, 2 * which), ("mod", l, 2 * which + 1)]

        def make_gs(l, s):
            which = 1 if s == 0 else 4
            for cnd in range(2):
                S.op("dve", lambda e, c=cnd: e.tensor_scalar(out=GS[:, l, s, :, c], in0=MOD[:, l, which * 8:which * 8 + 8, c],
                                                             scalar1=1.0, scalar2=None, op0=ALU.add),
                     reads=mod_res(l, which), writes=[("gs", l, s, cnd)])
                S.op("dve", lambda e, c=cnd: e.tensor_tensor(out=GS[:, l, s, :, c], in0=GS[:, l, s, :, c],
                                                             in1=VT[:, C_NG + l * 16 + s * 8:C_NG + l * 16 + s * 8 + 8],
                                                             op=ALU.mult),
                     reads=[("gs", l, s, cnd), "VT"], writes=[("gs", l, s, cnd)])

        def stats(tt):
            cs = slice(tt * 512, (tt + 1) * 512)
            S.op("act", lambda e: e.activation(out=xsq8[:, 0:4, :], in_=X[:, 0:4, cs], func=AF.Square),
                 reads=[("X", kc, tt) for kc in range(4)], writes=[("xsq", 0)])
            S.op("dve", lambda e: e.tensor_tensor(out=xsq8[:, 4:8, :], in0=X[:, 4:8, cs], in1=X[:, 4:8, cs], op=ALU.mult),
                 reads=[("X", kc, tt) for kc in range(4, 8)], writes=[("xsq", 1)])

            def f(e):
                ins = None
                for kc in range(KC):
                    ins = e.matmul(banks[6][:, :], lhsT=ones[:], rhs=xsq8[:, kc, :], start=(kc == 0), stop=(kc == KC - 1))
                return ins

            S.op("pe", f, reads=[("xsq", 0), ("xsq", 1), "ones"], writes=[B(6)])
            r = tt % 3
            S.op("act", lambda e, r=r: e.activation(out=rstd[r], in_=banks[6][:, :], func=AF.Ln, scale=1.0 / D, bias=EPS),
                 reads=[B(6)], writes=[("rstd", r)])
            S.op("act", lambda e, r=r: e.activation(out=rstd[r], in_=rstd[r], func=AF.Exp, scale=-0.5),
                 reads=[("rstd", r)], writes=[("rstd", r)])
            return r

        xm = carve(arena, 0, BF16, [128, KC, NT])

        def norm_mod(l, s):
            sh = 0 if s == 0 else 3
            make_gs(l, s)

            def modulate(tt, r):
                cnd = COND[tt]
                cs = slice(tt * 512, (tt + 1) * 512)
                for kc in range(KC):
                    b = state["tmp"] % 3
                    state["tmp"] += 1
                    S.op("dve", lambda e, b=b, kc=kc, r=r, cs=cs: e.tensor_tensor(out=tmpf[b], in0=X[:, kc, cs],
                                                                                 in1=rstd[r], op=ALU.mult),
                         reads=[("X", kc, tt), ("rstd", r)], writes=[("tmpf", b)])
                    S.op("act", lambda e, b=b, kc=kc, cs=cs, cnd=cnd: e.activation(
                        out=xm[:, kc, cs], in_=tmpf[b], func=AF.Identity,
                        scale=GS[:, l, s, kc, cnd:cnd + 1], bias=mod_vec(l, sh, kc, cnd)),
                        reads=[("tmpf", b), ("gs", l, s, cnd)] + mod_res(l, sh), writes=[("xm", kc, tt)])

            r0 = stats(0)
            r1 = stats(1)
            modulate(0, r0)
            r2 = stats(2)
            modulate(1, r1)
            modulate(2, r2)

        def residual(l, which, bk, m, tt):
            cnd = COND[tt]
            cs = slice(tt * 512, (tt + 1) * 512)
            S.op("dve", lambda e: e.scalar_tensor_tensor(out=X[:, m, cs], in0=banks[bk][:, :],
                                                         scalar=mod_vec(l, which, m, cnd), in1=X[:, m, cs],
                                                         op0=ALU.mult, op1=ALU.add),
                 reads=[B(bk), ("X", m, tt)] + mod_res(l, which), writes=[("X", m, tt)])

        do_mod(0, [0, 1, 2])

        xst = [carve(arena, 8192 + i * 16384, F32, [128, 4, D]) for i in range(2)]
        for tt in range(3):
            st = xst[tt % 2]
            for i in range(4):
                g = tt * 4 + i
                S.dma("sp", lambda e, st=st, i=i, g=g: e.dma_start(out=st[:, i, :], in_=x_d[g * 128:(g + 1) * 128, :]),
                      writes=[("xst", tt % 2, i)])
            for kc in range(KC):
                bk = bank()

                def f(e, st=st, kc=kc, bk=bk):
                    ins = None
                    for i in range(4):
                        ins = e.transpose(out=banks[bk][:, i * 128:(i + 1) * 128], in_=st[:, i, kc * 128:(kc + 1) * 128],
                                          identity=ident[:])
                    return ins

                S.op("pe", f, reads=[("xst", tt % 2, i) for i in range(4)] + ["ident"], writes=[B(bk)])
                copy_op(evac_eng(), X[:, kc, tt * 512:(tt + 1) * 512], banks[bk][:, :], [B(bk)], [("X", kc, tt)])
        S.barrier()

        def ffn(l):
            norm_mod(l, 1)
            hbuf = carve(arena, 24576, BF16, [128, FC, NT])
            sgs = [carve(arena, 24576 + 67584 + i * 2048, F32, [128, 512]) for i in range(3)]
            sgi = [0]
            for jp in range(11):
                s = w_get("pair")
                wv = WR[:, s, :].rearrange("p (kc t n) -> p kc t n", t=2, n=256)
                for jj in range(2):
                    j = 2 * jp + jj
                    for tt in range(3):
                        cs = slice(tt * 512, (tt + 1) * 512)
                        bg = bank()
                        bu = bank()
                        for t, bk in ((0, bg), (1, bu)):
                            def f(e, t=t, bk=bk, jj=jj, cs=cs, wv=wv):
                                ins = None
                                for kc in range(KC):
                                    ins = e.matmul(banks[bk][:, :], lhsT=wv[:, kc, t, jj * 128:(jj + 1) * 128],
                                                   rhs=xm[:, kc, cs], start=(kc == 0), stop=(kc == KC - 1))
                                return ins

                            S.op("pe", f, reads=[("ws", s)] + [("xm", kc, tt) for kc in range(KC)], writes=[B(bk)])
                        g = sgi[0] % 3
                        sgi[0] += 1
                        S.op("act", lambda e, g=g, bg=bg: e.activation(out=sgs[g], in_=banks[bg][:, :], func=AF.Silu),
                             reads=[B(bg)], writes=[("sg", g)])
                        S.op("dve", lambda e, g=g, bu=bu, j=j, cs=cs: e.tensor_tensor(out=hbuf[:, j, cs], in0=banks[bu][:, :],
                                                                                     in1=sgs[g], op=ALU.mult),
                             reads=[B(bu), ("sg", g)], writes=[("h", j, tt)])
                if l == 0 and jp in (2, 4, 6, 8):
                    do_mod(1, [jp // 2 - 1])
            if (l == 0 and True) or l == 1:
                do_mod(l, range(10, 12))
            for m in range(KC):
                s = w_get("down")
                wv = WR[:, s, 0:FC * 128].rearrange("p (kc n) -> p kc n", n=128)
                for tt in range(3):
                    cs = slice(tt * 512, (tt + 1) * 512)
                    bk = bank()

                    def f(e, bk=bk, cs=cs, wv=wv):
                        ins = None
                        for j in range(FC):
                            ins = e.matmul(banks[bk][:, :], lhsT=wv[:, j, :], rhs=hbuf[:, j, cs],
                                           start=(j == 0), stop=(j == FC - 1))
                        return ins

                    S.op("pe", f, reads=[("ws", s)] + [("h", j, tt) for j in range(FC)], writes=[B(bk)])
                    residual(l, 5, bk, m, tt)
            S.barrier()

        def attention():
            l = 0
            norm_mod(0, 0)
            QA = carve(arena, 24576, BF16, [128, KC, NT])
            KT = carve(arena, 49152, BF16, [128, KC, NT])
            VA = carve(arena, 73728, BF16, [128, 12, 8, 192])
            A0 = 0
            Pt = [carve(arena, A0 + 18432 + i * 1024, BF16, [128, 512]) for i in range(6)]
            Ct = [carve(arena, A0 + 4096 + i * 2816, BF16, [128, 22, 64]) for i in range(2)]
            rden = [carve(arena, A0 + 9728 + i * 2048, F32, [128, 512]) for i in range(2)]
            QZ = [[carve(arena, A0 + 13824 + (hp_ * 2 + i) * 1024, BF16, [128, 512]) for i in range(2)] for hp_ in range(2)]
            R1 = carve(arena, 110592 + 128, F32, [128, 2, 31])
            EPs = carve(arena, 110592 + 128 + 248, BF16, [128, 2, 128])
            kst = carve(arena, 110592 + 1024, F32, [128, 2816])
            if cfg.get("a_vam", True):
                S.op("pool", lambda e: e.memset(VA[:, :, :, 64:128], 1.0),
                     writes=[("vaug", t) for t in range(12)] + [("xsq", 0), ("xsq", 1)])
            ETAB = cfg.get("a_etab", True)
            if ETAB:
                S.dma("sp", lambda e: e.dma_start(out=R1[0:120], in_=rpb_d.rearrange("(hg h) r c -> (h r) hg c", hg=2)),
                      writes=["R1"])
                S.op("pool", lambda e: e.memset(EPs[0:120], 0.0), writes=["EPs"])
                S.op("act", lambda e: e.activation(out=EPs[0:120, :, 48:79], in_=R1[0:120, :, :], func=AF.Exp),
                     reads=["R1", "EPs"], writes=["EPs"])
                epd_v = epd[0:16 * 15 * 127].rearrange("(hg q j) -> q hg j", hg=2, j=127)
                S.dma("sp", lambda e: e.dma_start(out=epd_v, in_=EPs[0:120, :, 0:127]), reads=["EPs"], writes=["epd"])
            def prep_ct(h):
                i = h % 2
                if not ETAB:
                    return
                for a, s0 in ((0, 4), (1, 3)):
                    src = bass.AP(tensor=epd_t, offset=h * 15 * 127, ap=[[1, 64], [127, 15], [1, 64]])
                    S.dma("sp", lambda e, i=i, a=a, s0=s0, src=src: e.dma_start(
                        out=Ct[i][a * 64:(a + 1) * 64, s0:s0 + 15, :], in_=src), reads=["epd"], writes=[("ct", i)])
                S.op("pool", lambda e, i=i: e.tensor_tensor(out=Ct[i][:, 3:19, :], in0=Ct[i][:, 3:19, :],
                                                            in1=cmb[:].unsqueeze(1).to_broadcast([128, 16, 64]),
                                                            op=ALU.mult),
                     reads=[("ct", i), "cmb"], writes=[("ct", i)])

            for q in range(6):
                s = w_get("k8")
                wv = WR[:, s, :].rearrange("p (kc n) -> p kc n", n=512)
                xr_ = [("xm", kc, tt) for kc in range(KC) for tt in range(3)]
                if q < 4:
                    dst, nm, scl = (QA, "qa", 0.125) if q < 2 else (KT, "kt", None)
                    for mm in range(4):
                        c = (q % 2) * 4 + mm
                        for tt in range(3):
                            cs = slice(tt * 512, (tt + 1) * 512)
                            bk = bank()

                            def f(e, bk=bk, cs=cs, wv=wv, mm=mm):
                                ins = None
                                for kc in range(KC):
                                    ins = e.matmul(banks[bk][:, :], lhsT=wv[:, kc, mm * 128:(mm + 1) * 128],
                                                   rhs=xm[:, kc, cs], start=(kc == 0), stop=(kc == KC - 1))
                                return ins

                            S.op("pe", f, reads=[("ws", s)] + [("xm", kc, tt) for kc in range(KC)], writes=[B(bk)])
                            if nm == "qa":
                                wr = [("qa", c, 0, tt), ("qa", c, 1, tt)]
                            else:
                                wr = [("kt", c, tt)]
                            copy_op(evac_eng(), dst[:, c, cs], banks[bk][:, :], [B(bk)], wr, scale=scl)
                if q in (2, 3, 4, 5) and cfg.get("a_tok", True):
                    isv = q >= 4
                    half = q % 2
                    for g in range(12 if isv else 4):
                        bk = bank()
                        ts_ = slice(g * 128, (g + 1) * 128)
                        tt = g // 4

                        def f(e, bk=bk, ts_=ts_, wv=wv):
                            ins = None
                            for kc in range(KC):
                                ins = e.matmul(banks[bk][:, :], lhsT=xm[:, kc, ts_], rhs=wv[:, kc, :],
                                               start=(kc == 0), stop=(kc == KC - 1))
                            return ins

                        S.op("pe", f, reads=[("ws", s)] + [("xm", kc, tt) for kc in range(KC)], writes=[B(bk)])
                        TM = cfg.get("a_tokm", 3)
                        eng = evac_eng()
                        if isv and TM >= 2:
                            vtile = 8 + g if g < 4 else g - 4
                            pv4 = banks[bk][:, :].rearrange("p (a t d) -> p a t d", t=2, d=64)
                            for t in range(2):
                                copy_op(eng, VA[:, vtile, half * 4:half * 4 + 4, t * 128:t * 128 + 64], pv4[:, :, t, :],
                                        [B(bk)], [("vaug", vtile)])
                        if g < 4 and TM >= 3:
                            so = (state["kst"] % 5) * 512
                            state["kst"] += 1
                            stg = kst[:, so:so + 512]
                            copy_op(eng, stg, banks[bk][:, :], [B(bk)], [("kst", so)])
                            od = nv_d if isv else nk_d
                            if TM >= 4 or TM == 3 and cfg.get("a_tokm", 3) == 3 and not cfg.get("a_nodma", False):
                              S.dma("sp", lambda e, od=od, g=g, half=half, stg=stg: e.dma_start(
                                out=od[g * 128:(g + 1) * 128, half * 512:(half + 1) * 512], in_=stg),
                                reads=[("kst", so)], out=True)
            S.barrier()
            for i in range(2):
                S.op("pool", lambda e, i=i: e.memset(Ct[i], 0.0), writes=[("ct", i)])
            for hp_ in range(2):
                for i in range(2):
                    S.op("pool", lambda e, hp_=hp_, i=i: e.memset(QZ[hp_][i], 0.0), writes=[("qz", hp_, i)])

            def prep_qz(c, hp, i, tt, q0, n, eng="dve"):
                ps_ = slice(hp * 64, hp * 64 + 64)
                S.op(eng, lambda e: e.tensor_copy(out=QZ[hp][i][ps_, 0:n], in_=QA[ps_, c, q0:q0 + n]),
                     reads=[("qa", c, hp, tt)], writes=[("qz", hp, i)])
            do_mod(0, range(4, 6))

            def run_items(items):
                n_it = len(items)
                import os
                DEPTH = int(os.environ.get('ADEPTH', '4'))
                for n in range(n_it + DEPTH):
                    if n < n_it:
                        items[n]["S"]()
                    if n >= DEPTH:
                        items[n - DEPTH]["PV"]()

            obank = [0]

            items = []
            for s_ in range(2):
                for h in range(16):
                    c, hp = h // 2, h % 2
                    ps = slice(hp * 64, hp * 64 + 64)
                    dps = slice(64, 128) if hp == 0 else slice(0, 64)
                    q0 = s_ * 256
                    it = {}

                    def fS(c=c, ps=ps, q0=q0, s_=s_, it=it):
                        bk = bank((0, 1, 2, 3, 6))
                        p = state["pt"] % 6
                        state["pt"] += 1
                        it["p"] = p

                        hp_ = 0 if ps.start == 0 else 1
                        prep_qz(c, hp_, s_, 0, q0, 256, eng="pool")

                        def f(e):
                            ins = None
                            for j in range(2):
                                ins = e.matmul(banks[bk][:, j * 256:(j + 1) * 256],
                                               lhsT=KT[:, c, q0 + j * 128:q0 + (j + 1) * 128],
                                               rhs=QZ[hp_][s_][:, 0:256], start=True, stop=True)
                            return ins

                        S.op("pe", f, reads=[("kt", c, 0), ("qz", hp_, s_)], writes=[B(bk)])
                        S.op("act", lambda e: e.activation(out=Pt[p], in_=banks[bk][:, :], func=AF.Exp),
                             reads=[B(bk)], writes=[("pt", p)])

                    def fPV(c=c, hp=hp, ps=ps, dps=dps, q0=q0, s_=s_, it=it):
                        p = it["p"]
                        ob = 4 + obank[0] % 2
                        obank[0] += 1

                        def f(e):
                            ins = None
                            for j in range(2):
                                ins = e.matmul(banks[ob][:, 0:256], lhsT=VA[:, 8 + s_ * 2 + j, c, hp * 64:hp * 64 + 128],
                                               rhs=Pt[p][:, j * 256:(j + 1) * 256], start=(j == 0), stop=(j == 1))
                            return ins

                        S.op("pe", f, reads=[("pt", p), ("vaug", 8 + s_ * 2), ("vaug", 9 + s_ * 2)], writes=[B(ob)])
                        r = ob - 4
                        S.op("act", lambda e: e.activation(out=rden[r][dps, 0:256], in_=banks[ob][dps, 0:256], func=AF.Ln),
                             reads=[B(ob)], writes=[("rden", r)])
                        S.op("act", lambda e: e.activation(out=rden[r][dps, 0:256], in_=rden[r][dps, 0:256], func=AF.Exp,
                                                           scale=-1.0),
                             reads=[("rden", r)], writes=[("rden", r)])
                        S.op("dve", lambda e: e.tensor_tensor(out=QA[ps, c, q0:q0 + 256], in0=banks[ob][ps, 0:256],
                                                              in1=rden[r][dps, 0:256], op=ALU.mult),
                             reads=[B(ob), ("rden", r)], writes=[("qa", c, hp, 0)])

                    it["S"] = fS
                    it["PV"] = fPV
                    items.append(it)
            if cfg.get("a_ctx", True):
                run_items(items)

            for ct in range(4 if cfg.get("a_cache", True) else 0):
                for t in range(2):
                    src = cv_d[ct * 128:(ct + 1) * 128, :].rearrange("p (a t d) -> p a t d", t=2, d=64)[:, :, t, :]
                    S.dma("pool", lambda e, ct=ct, t=t, src=src: e.dma_start(
                        out=VA[:, 8 + ct, :, t * 128:t * 128 + 64], in_=src), writes=[("vaug", 8 + ct)])
            kstg = kst[:, 0:2048].rearrange("p (a b) -> p a b", a=2)
            for ct in range(4 if cfg.get("a_cache", True) else 0):
                buf = ct % 2
                S.dma("sp", lambda e, ct=ct, buf=buf: e.dma_start(out=kstg[:, buf, :], in_=ck_d[ct * 128:(ct + 1) * 128, :]),
                      writes=[("kstg", buf)] + [("kst", so) for so in range(0, 2560, 512)])
                for hh in range(2):
                    bk = bank((0, 1, 2, 3))

                    def f(e, bk=bk, hh=hh, buf=buf):
                        ins = None
                        for q in range(4):
                            c = hh * 4 + q
                            ins = e.transpose(out=banks[bk][:, q * 128:(q + 1) * 128],
                                              in_=kstg[:, buf, c * 128:(c + 1) * 128], identity=ident[:])
                        return ins

                    S.op("pe", f, reads=[("kstg", buf), "ident"], writes=[B(bk)])
                    copy_op(evac_eng(), KT[:, hh * 4:hh * 4 + 4, ct * 128:(ct + 1) * 128],
                            banks[bk][:, :].rearrange("p (a b) -> p a b", b=128), [B(bk)],
                            [("kt", hh * 4 + q, 0) for q in range(4)])

            PB = [carve(arena, 112640 + i * 1024, BF16, [128, 512]) for i in range(10)] + \
                 [carve(arena, A0 + i * 1024, BF16, [128, 512]) for i in range(2)]
            for i in range(12):
                S.op("pool", lambda e, i=i: e.memset(PB[i], 0.0),
                     writes=[("pb", i), ("kstg", 0), ("kstg", 1)] + [("kst", so) for so in range(0, 2560, 512)])

            def rs_(r):
                return min(max(r - 4, 0), 8)

            def valid(rk, r):
                return rs_(r) <= rk <= rs_(r) + 7

            lat_tiles = {0: list(range(0, 6)), 1: list(range(2, 8))}
            items = []
            prep_ct(0)
            prep_qz(0, 0, 0, 1, 512, 512)
            for h in range(16):
                c, hp = h // 2, h % 2
                ps = slice(hp * 64, hp * 64 + 64)
                dps = slice(64, 128) if hp == 0 else slice(0, 64)
                for qt in range(2):
                    tt = 1 + qt
                    q0 = 512 + qt * 512
                    tiles = [("c", ct) for ct in range(4)] + [("l", kt) for kt in lat_tiles[qt]]
                    for n, (kind, t) in enumerate(tiles):
                        it = {}
                        first = (n == 0)
                        last = (n == len(tiles) - 1)

                        def fS(c=c, hp=hp, ps=ps, q0=q0, tt=tt, qt=qt, kind=kind, t=t, it=it, h=h, first=first):
                            if first:
                                nh_, nq_ = (h, 1) if qt == 0 else (h + 1, 0)
                                if nh_ < 16:
                                    prep_qz(nh_ // 2, nh_ % 2, nq_, 1 + nq_, 512 + nq_ * 512, 512)
                                if qt == 1 and h + 1 < 16:
                                    prep_ct(h + 1)
                                if qt == 0 and h in (2, 6, 10, 14):
                                    do_mod(0, [6 + h // 4])
                            bk = bank((0, 1, 2, 3, 6))
                            p = state["pt"] % 6
                            state["pt"] += 1
                            it["p"] = p
                            k0 = t * 128 if kind == "c" else 512 + t * 128
                            kres = ("kt", c, 0) if kind == "c" else ("kt", c, 1 + t // 4)
                            if kind == "l":
                                vu = [b for b in range(8) if valid(2 * t, 8 * qt + b) or valid(2 * t + 1, 8 * qt + b)]
                                cols = slice(vu[0] * 64, (vu[-1] + 1) * 64)
                                assert vu == list(range(vu[0], vu[-1] + 1))
                            else:
                                cols = slice(0, 512)
                            it["cols"] = cols
                            S.op("pe", lambda e: e.matmul(banks[bk][:, cols], lhsT=KT[:, c, k0:k0 + 128],
                                                          rhs=QZ[hp][qt][:, cols], start=True, stop=True),
                                 reads=[kres, ("qz", hp, qt)], writes=[B(bk)])
                            S.op("act", lambda e: e.activation(out=Pt[p][:, cols], in_=banks[bk][:, cols], func=AF.Exp),
                                 reads=[B(bk)], writes=[("pt", p)])
                            if kind == "l":
                                Dd = 2 * t - 8 * qt
                                s_hi = Dd + 11
                                ci = h % 2
                                pbi = qt * 6 + lat_tiles[qt].index(t)
                                it["pb"] = pbi
                                vb = [[b for b in range(8) if valid(2 * t + a, 8 * qt + b)] for a in range(2)]
                                if vb[0] == vb[1]:
                                    jobs = [(slice(0, 128), vb[0])]
                                else:
                                    jobs = [(slice(a * 64, (a + 1) * 64), vb[a]) for a in range(2)]
                                for psl, vbl in jobs:
                                    if not vbl:
                                        continue
                                    b_lo, b_hi = vbl[0], vbl[-1] + 1
                                    assert vbl == list(range(b_lo, b_hi))
                                    hi_ = s_hi - b_lo
                                    lo_ = s_hi - b_hi
                                    ev_ = Ct[ci][psl, hi_:(lo_ if lo_ >= 0 else None):-1, ::-1]
                                    o3 = PB[pbi][psl, b_lo * 64:b_hi * 64].rearrange("p (b c) -> p b c", c=64)
                                    i3 = Pt[p][psl, b_lo * 64:b_hi * 64].rearrange("p (b c) -> p b c", c=64)
                                    S.op("dve", lambda e, o3=o3, i3=i3, ev_=ev_: e.tensor_tensor(out=o3, in0=i3, in1=ev_, op=ALU.mult),
                                         reads=[("pt", p), ("ct", ci)], writes=[("pb", pbi)])

                        def fPV(c=c, hp=hp, ps=ps, dps=dps, q0=q0, tt=tt, kind=kind, t=t, it=it, first=first, last=last):
                            p = it["p"]
                            if first:
                                obank[0] += 1
                            ob = 4 + obank[0] % 2
                            vtile = 8 + t if kind == "c" else t
                            cols = it["cols"]
                            if kind == "l":
                                rhs_, rres = PB[it["pb"]][:, cols], ("pb", it["pb"])
                            else:
                                rhs_, rres = Pt[p], ("pt", p)
                            S.op("pe", lambda e: e.matmul(banks[ob][:, cols], lhsT=VA[:, vtile, c, hp * 64:hp * 64 + 128],
                                                          rhs=rhs_, start=first, stop=last),
                                 reads=[rres, ("vaug", vtile)], writes=[B(ob)])
                            if last:
                                r = ob - 4
                                S.op("act", lambda e: e.activation(out=rden[r][dps, :], in_=banks[ob][dps, :], func=AF.Ln),
                                     reads=[B(ob)], writes=[("rden", r)])
                                S.op("act", lambda e: e.activation(out=rden[r][dps, :], in_=rden[r][dps, :], func=AF.Exp,
                                                                   scale=-1.0),
                                     reads=[("rden", r)], writes=[("rden", r)])
                                S.op("dve", lambda e: e.tensor_tensor(out=QA[ps, c, q0:q0 + 512], in0=banks[ob][ps, :],
                                                                      in1=rden[r][dps, :], op=ALU.mult),
                                     reads=[B(ob), ("rden", r)], writes=[("qa", c, hp, tt)])

                        it["S"] = fS
                        it["PV"] = fPV
                        items.append(it)
            if cfg.get("a_lat", True):
                run_items(items)

            S.barrier()
            slabs = [w_get("k8"), w_get("k8", ahead=2)]
            for tt in range(3):
                cs = slice(tt * 512, (tt + 1) * 512)
                for q in range(2):
                    s = slabs[q]
                    wv = WR[:, s, :].rearrange("p (kc n) -> p kc n", n=512)
                    for mm in range(4):
                        m = q * 4 + mm
                        bk = bank((0, 1, 2, 3))

                        def f(e, bk=bk, cs=cs, wv=wv, mm=mm):
                            ins = None
                            for kc in range(KC):
                                ins = e.matmul(banks[bk][:, :], lhsT=wv[:, kc, mm * 128:(mm + 1) * 128],
                                               rhs=QA[:, kc, cs], start=(kc == 0), stop=(kc == KC - 1))
                            return ins

                        S.op("pe", f, reads=[("ws", s)] + [("qa", kc, hp, tt) for kc in range(KC) for hp in range(2)],
                             writes=[B(bk)])
                        residual(0, 2, bk, m, tt)

        def lru():
            l = 1
            norm_mod(1, 0)
            S.barrier()
            ybuf = carve(arena, 24576, BF16, [128, KC, NT])
            o = 49152
            xrp = carve(arena, o, F32, [128, 1548]); o += 6192
            gg = []; xc = []; xcb = []
            for i in range(2):
                gg.append(carve(arena, o, F32, [128, NT])); o += 6144
                xc.append(carve(arena, o, F32, [128, NT])); o += 6144
                xcb.append(carve(arena, o, BF16, [128, NT])); o += 3072
            dirb = []
            for d in range(2):
                ra = carve(arena, o, F32, [128, NT]); o += 6144
                itb = carve(arena, o, F32, [128, NT]); o += 6144
                hs = carve(arena, o, F32, [128, NT]); o += 6144
                dirb.append((ra, itb, hs))
            wgb = carve(arena, o, BF16, [128, 2, 2, 8, 128]); o += 8192
            assert o <= ARENA_B + STRIP, o
            lam = VT[:, C_LAM:C_LAM + 16]
            yv, wv_, dv, lw = ltmp[:, 0, :], ltmp[:, 1, :], ltmp[:, 2, :], ltmp[:, 3, :]
            S.op("act", lambda e: e.activation(out=yv, in_=lam, func=AF.Exp, scale=-1.0), reads=["VT"], writes=["l_y"])
            S.op("dve", lambda e: e.tensor_scalar(out=wv_, in0=yv, scalar1=1.0, scalar2=None, op0=ALU.add),
                 reads=["l_y"], writes=["l_w"])
            S.op("dve", lambda e: e.tensor_scalar(out=dv, in0=wv_, scalar1=-1.0, scalar2=1e-30, op0=ALU.add, op1=ALU.max),
                 reads=["l_w"], writes=["l_d"])
            S.op("dve", lambda e: e.reciprocal(out=dv, in_=dv), reads=["l_d"], writes=["l_d"])
            S.op("act", lambda e: e.activation(out=lw, in_=wv_, func=AF.Ln), reads=["l_w"], writes=["l_lw"])
            S.op("dve", lambda e: e.tensor_tensor(out=dv, in0=dv, in1=yv, op=ALU.mult), reads=["l_d", "l_y"], writes=["l_d"])
            S.op("dve", lambda e: e.scalar_tensor_tensor(out=nls[:], in0=lw, scalar=-8.0, in1=dv, op0=ALU.mult, op1=ALU.mult),
                 reads=["l_lw", "l_d"], writes=["nls"])
            nls2 = ltmp[:, 0, :]
            S.op("dve", lambda e: e.tensor_scalar(out=nls2, in0=nls[:], scalar1=2.0, scalar2=None, op0=ALU.mult),
                 reads=["nls", "l_y", "l_d"], writes=["nls2", "l_y"])
            S.op("pool", lambda e: e.memset(xrp, 0.0), writes=["xrp"])
            wg = wgb
            for wi_, wd in enumerate((wa_d, wi_d)):
                for d in range(2):
                    S.dma("pool", lambda e, o_=wgb[:, wi_, d, :, :], i_=wd[d].rearrange("n k j -> k n j"):
                          e.dma_start(out=o_, in_=i_), writes=["wgb"])
            SEQ = [(2, 0, 256), (261, 256, 256), (520, 512, 1024)]
            pair = {}

            def stageA(m):
                bf_ = m % 2
                jp, jj = m // 2, m % 2
                if jj == 0:
                    s = w_get("pair")
                    pair["s"] = s
                s = pair["s"]
                wv = WR[:, s, :].rearrange("p (kc t n) -> p kc t n", t=2, n=256)
                for tt in range(3):
                    cs = slice(tt * 512, (tt + 1) * 512)
                    bg = bank()
                    bx = bank()
                    for t, bk in ((0, bg), (1, bx)):
                        def f(e, t=t, bk=bk, jj=jj, cs=cs, wv=wv):
                            ins = None
                            for kc in range(KC):
                                ins = e.matmul(banks[bk][:, :], lhsT=wv[:, kc, t, jj * 128:(jj + 1) * 128],
                                               rhs=xm[:, kc, cs], start=(kc == 0), stop=(kc == KC - 1))
                            return ins

                        S.op("pe", f, reads=[("ws", s)] + [("xm", kc, tt) for kc in range(KC)], writes=[B(bk)])
                    S.op("act", lambda e, bg=bg, cs=cs: e.activation(out=gg[bf_][:, cs], in_=banks[bg][:, :],
                                                                     func=AF.Gelu_apprx_tanh),
                         reads=[B(bg)], writes=[("gg", bf_)])
                    if tt == 0:
                        for sq in range(2):
                            po = SEQ[sq][0]
                            S.op("dve", lambda e, bx=bx, sq=sq, po=po: e.tensor_copy(
                                out=xrp[:, po:po + 256], in_=banks[bx][:, sq * 256:(sq + 1) * 256]),
                                reads=[B(bx)], writes=["xrp"])
                    else:
                        po = 520 + (tt - 1) * 512
                        S.op("dve", lambda e, bx=bx, po=po: e.tensor_copy(out=xrp[:, po:po + 512], in_=banks[bx][:, :]),
                             reads=[B(bx)], writes=["xrp"])
                for (po, co, ln) in SEQ:
                    base = po - 2
                    S.op("dve", lambda e, base=base, co=co, ln=ln: e.tensor_scalar(
                        out=xc[bf_][:, co:co + ln], in0=xrp[:, base:base + ln], scalar1=vt(C_CW + 0 * 8 + m),
                        scalar2=vt(C_CB + m), op0=ALU.mult, op1=ALU.add), reads=["xrp", "VT"], writes=[("xc", bf_)])
                    for j in range(1, 4):
                        S.op("dve", lambda e, base=base, co=co, ln=ln, j=j: e.scalar_tensor_tensor(
                            out=xc[bf_][:, co:co + ln], in0=xrp[:, base + j:base + j + ln], scalar=vt(C_CW + j * 8 + m),
                            in1=xc[bf_][:, co:co + ln], op0=ALU.mult, op1=ALU.add),
                            reads=["xrp", ("xc", bf_), "VT"], writes=[("xc", bf_)])
                S.op("dve", lambda e: e.tensor_copy(out=xcb[bf_], in_=xc[bf_]), reads=[("xc", bf_)], writes=[("xcb", bf_)])

            def stageB(m):
                bf_ = m % 2
                for d in range(2):
                    ra, itb, hs = dirb[d]
                    for tt in range(3):
                        cs = slice(tt * 512, (tt + 1) * 512)
                        ba_ = bank()
                        bi_ = bank()
                        for w_, bk in ((0, ba_), (1, bi_)):
                            S.op("pe", lambda e, w_=w_, bk=bk, cs=cs, d=d: e.matmul(
                                banks[bk][:, :], lhsT=wg[:, w_, d, m, :], rhs=xcb[bf_][:, cs], start=True, stop=True),
                                reads=["wgb", ("xcb", bf_)], writes=[B(bk)])
                        S.op("act", lambda e, ba_=ba_, cs=cs, d=d, ra=ra: e.activation(
                            out=ra[:, cs], in_=banks[ba_][:, :], func=AF.Sigmoid, bias=vt(C_BA + d * 8 + m)),
                            reads=[B(ba_), "VT"], writes=[("ra", d)])
                        S.op("act", lambda e, bi_=bi_, cs=cs, d=d, itb=itb: e.activation(
                            out=itb[:, cs], in_=banks[bi_][:, :], func=AF.Sigmoid, bias=vt(C_BI + d * 8 + m)),
                            reads=[B(bi_), "VT"], writes=[("it", d)])
                    col = d * 8 + m
                    S.op("act", lambda e, hs=hs, ra=ra, col=col: e.activation(out=hs, in_=ra, func=AF.Exp, scale=nls2[:, col:col + 1]),
                         reads=[("ra", d), "nls2"], writes=[("hs", d)])
                    S.op("act", lambda e, ra=ra, col=col: e.activation(out=ra, in_=ra, func=AF.Exp, scale=nls[:, col:col + 1]),
                         reads=[("ra", d), "nls"], writes=[("ra", d)])
                    S.op("act", lambda e, hs=hs: e.activation(out=hs, in_=hs, func=AF.Sqrt, scale=-1.0, bias=1.0),
                         reads=[("hs", d)], writes=[("hs", d)])
                    S.op("pool", lambda e, itb=itb: e.tensor_tensor(out=itb, in0=itb, in1=xc[bf_], op=ALU.mult),
                         reads=[("it", d), ("xc", bf_)], writes=[("it", d)])
                    S.op("pool", lambda e, itb=itb, hs=hs: e.tensor_tensor(out=itb, in0=itb, in1=hs, op=ALU.mult),
                         reads=[("it", d), ("hs", d)], writes=[("it", d)])
                    for sq, (po, co, ln) in enumerate(SEQ):
                        init = 0.0 if sq < 2 else vt(C_H0 + d * 8 + m)
                        if d == 0:
                            S.op("dve", lambda e, co=co, ln=ln, init=init, ra=ra, itb=itb, hs=hs: e.tensor_tensor_scan(
                                out=hs[:, co:co + ln], data0=ra[:, co:co + ln], data1=itb[:, co:co + ln],
                                initial=init, op0=ALU.mult, op1=ALU.add),
                                reads=[("ra", d), ("it", d), "VT"], writes=[("hs", d)])
                        else:
                            lo = co - 1 if co > 0 else None
                            S.op("dve", lambda e, co=co, ln=ln, init=init, lo=lo, ra=ra, itb=itb, hs=hs: e.tensor_tensor_scan(
                                out=hs[:, co + ln - 1:lo:-1], data0=ra[:, co + ln - 1:lo:-1],
                                data1=itb[:, co + ln - 1:lo:-1], initial=init, op0=ALU.mult, op1=ALU.add),
                                reads=[("ra", d), ("it", d), "VT"], writes=[("hs", d)])
                        if sq < 2:
                            c_ = co + ln - 1 if d == 0 else co
                            nhc = sq * 16 + d * 8 + m
                            S.op("dve", lambda e, c_=c_, nhc=nhc, hs=hs: e.tensor_copy(
                                out=NH[:, nhc:nhc + 1], in_=hs[:, c_:c_ + 1]), reads=[("hs", d)], writes=["NH"])
                hf, hb = dirb[0][2], dirb[1][2]
                S.op("dve", lambda e: e.tensor_tensor(out=hf, in0=hf, in1=hb, op=ALU.add),
                     reads=[("hs", 0), ("hs", 1)], writes=[("hs", 0)])
                S.op("dve", lambda e: e.tensor_tensor(out=ybuf[:, m, :], in0=hf, in1=gg[bf_], op=ALU.mult),
                     reads=[("hs", 0), ("gg", bf_)], writes=[("y", m)])

            stageA(0)
            for m in range(8):
                if m + 1 < 8:
                    stageA(m + 1)
                stageB(m)
                if m == 6:
                    do_mod(1, range(4, 6))
                if 2 <= m <= 5:
                    do_mod(1, [4 + m], ahead=2)
            S.barrier()
            S.op("pe", lambda e: e.transpose(out=banks[6][0:32, 0:128], in_=NH[:, :], identity=ident[:]),
                 reads=["NH", "ident"], writes=[B(6)])
            nhs = carve(arena, 110592, F32, [128, 128])
            S.op("dve", lambda e: e.tensor_copy(out=nhs[0:32, :], in_=banks[6][0:32, 0:128]), reads=[B(6)], writes=["nhs"])
            S.dma("sp", lambda e: e.dma_start(out=nh_d, in_=nhs[0:32, :]), reads=["nhs"], out=True)
            slabs = [w_get("k8"), w_get("k8", ahead=2)]
            for tt in range(3):
                cs = slice(tt * 512, (tt + 1) * 512)
                for q in range(2):
                    s = slabs[q]
                    wv = WR[:, s, :].rearrange("p (kc n) -> p kc n", n=512)
                    for mm in range(4):
                        m = q * 4 + mm
                        bk = bank()

                        def f(e, bk=bk, cs=cs, wv=wv, mm=mm):
                            ins = None
                            for kc in range(KC):
                                ins = e.matmul(banks[bk][:, :], lhsT=wv[:, kc, mm * 128:(mm + 1) * 128],
                                               rhs=ybuf[:, kc, cs], start=(kc == 0), stop=(kc == KC - 1))
                            return ins

                        S.op("pe", f, reads=[("ws", s)] + [("y", kc) for kc in range(KC)], writes=[B(bk)])
                        residual(1, 2, bk, m, tt)

        do_mod(0, [3])
        if cfg["attn"]:
            attention()
        else:
            do_mod(0, range(4, 6))
        if not cfg["attn"]:
            do_mod(0, range(6, 10))
        if cfg["ffn0"]:
            ffn(0)
        else:
            do_mod(0, range(10, 12))
        if not cfg["ffn0"]:
            do_mod(1, range(0, 4))
        if cfg["lru"]:
            lru()
        else:
            do_mod(1, range(4, 6))
        if not cfg["lru"]:
            do_mod(1, range(6, 10))
        if cfg["ffn1"]:
            ffn(1)
        else:
            do_mod(1, range(10, 12))
        S.barrier()

        yTs = [carve(arena, 0, F32, [128, KC, 512]), carve(arena, 32768, F32, [128, KC, 512])]
        ost = [carve(arena, 16384 + i * 4096, F32, [128, D]) for i in range(4)]
        fin_r = {0: stats(0), 1: stats(1)}
        for tt in range(3):
            cs = slice(tt * 512, (tt + 1) * 512)
            if tt == 1:
                fin_r[2] = stats(2)
            r = fin_r[tt]
            yb = tt % 2
            yT = yTs[yb]
            for kc in range(KC):
                S.op("dve", lambda e, kc=kc, r=r, cs=cs, yT=yT: e.scalar_tensor_tensor(
                    out=yT[:, kc, :], in0=X[:, kc, cs], scalar=vt(C_FG + kc), in1=rstd[r], op0=ALU.mult, op1=ALU.mult),
                    reads=[("X", kc, tt), ("rstd", r), "VT"], writes=[("yT", yb, kc)])
            for i in range(4):
                g = tt * 4 + i
                ob_ = g % 4
                o_ = ost[ob_]
                for hh in range(2):
                    bk = bank()

                    def f(e, bk=bk, hh=hh, i=i, yT=yT):
                        ins = None
                        for q in range(4):
                            ins = e.transpose(out=banks[bk][:, q * 128:(q + 1) * 128],
                                              in_=yT[:, hh * 4 + q, i * 128:(i + 1) * 128], identity=ident[:])
                        return ins

                    S.op("pe", f, reads=[("yT", yb, hh * 4 + q) for q in range(4)] + ["ident"], writes=[B(bk)])
                    copy_op(evac_eng(), o_[:, hh * 512:(hh + 1) * 512], banks[bk][:, :], [B(bk)], [("ost", ob_, hh)])
                S.dma("sp", lambda e, g=g, o_=o_: e.dma_start(out=y_d[g * 128:(g + 1) * 128, :], in_=o_),
                      reads=[("ost", ob_, 0), ("ost", ob_, 1)], out=True)
        S.finish()

        block = es.enter_context(nc.Block())

        @block.tensor
        def _(e):
            S.emit("pe", e)

        @block.scalar
        def _(e):
            S.emit("act", e)

        @block.vector
        def _(e):
            S.emit("dve", e)

        @block.gpsimd
        def _(e):
            S.emit("pool", e)

        @block.sync
        def _(e):
            S.emit("sp", e)
    return nc


def _colmask():
    cm = np.zeros((128, 64), np.float32)
    for cq in range(64):
        cs = min(max(cq - 8, 0), 48)
        for p in range(128):
            ck = p % 64
            if cs <= ck < cs + 16:
                cm[p, 63 - cq] = 1.0
    return cm


def make_in_maps(inp):
    f = lambda a: np.ascontiguousarray(np.asarray(a, dtype=np.float32))
    x_prompt, x_sample = f(inp["x_prompt"]), f(inp["x_sample"])
    c, c_ctx = f(inp["c"]), f(inp["c_ctx"])
    shared = {
        "ident": np.eye(128, dtype=np.float32),
        "cm": _colmask(),
        "rpb": f(inp["attn_rpb"])[0],
        "w_mod": f(inp["w_mod"]),
        "w_qkv": f(inp["attn_w_qkv"])[0],
        "w_o": f(inp["attn_w_o"])[0],
        "w_in": f(inp["lru_w_in"])[0],
        "w_a": f(inp["lru_w_a"])[0],
        "w_i": f(inp["lru_w_i"])[0],
        "w_out": f(inp["lru_w_out"])[0],
        "w_gu": f(inp["ffn_w_gu"]),
        "w_down": f(inp["ffn_w_down"]),
    }
    common_rows = [
        f(inp["b_mod"]).reshape(96, 128),
        f(inp["norm_g"]).reshape(32, 128),
        f(inp["final_g"]).reshape(8, 128),
        f(inp["lru_conv_w"])[0].reshape(32, 128),
        f(inp["lru_conv_b"])[0].reshape(8, 128),
        f(inp["lru_b_a"])[0].reshape(16, 128),
        f(inp["lru_b_i"])[0].reshape(16, 128),
        f(inp["lru_lam"])[0].reshape(16, 128),
    ]
    maps = []
    for i in range(NCORES):
        smalls = np.concatenate(common_rows + [c_ctx.reshape(8, 128), c[i].reshape(8, 128),
                                               f(inp["state_h"])[i, 0].reshape(16, 128)], axis=0)
        assert smalls.shape == (256, 128)
        m = dict(shared)
        m["x"] = np.concatenate([x_prompt[2 * i], x_prompt[2 * i + 1], x_sample[i]], axis=0)
        m["smalls"] = np.ascontiguousarray(smalls)
        m["ck"] = f(inp["cache_k"])[i, 0].reshape(512, D)
        m["cv"] = f(inp["cache_v"])[i, 0].reshape(512, D)
        maps.append(m)
    return maps


_NC_CACHE = {}


def run(inp, cfg=FULL):
    key = tuple(sorted(cfg.items()))
    if key not in _NC_CACHE:
        _NC_CACHE[key] = build(cfg)
    nc = _NC_CACHE[key]
    import os
    if os.environ.get("DBG1CORE"):
        res = run_bass_kernel_spmd(nc, make_in_maps(inp)[:1], core_ids=[0])
        rs = [res.results[0]] * NCORES
    else:
        res = run_bass_kernel_spmd(nc, make_in_maps(inp), core_ids=list(range(NCORES)))
        rs = res.results
    y_prompt = np.stack([rs[i // 2]["y"][(i % 2) * 256:(i % 2) * 256 + 256] for i in range(16)], axis=0)
    y_sample = np.stack([rs[i]["y"][512:1536] for i in range(8)], axis=0)
    nk = np.stack([rs[i // 2]["nk"][(i % 2) * 256:(i % 2) * 256 + 256] for i in range(16)], axis=0)
    nv = np.stack([rs[i // 2]["nv"][(i % 2) * 256:(i % 2) * 256 + 256] for i in range(16)], axis=0)
    nk = nk.reshape(16, 1, 256, 16, 64)
    nv = nv.reshape(16, 1, 256, 16, 64)
    nh = np.stack([rs[i // 2]["nh"].reshape(2, 2, 1024)[i % 2] for i in range(16)], axis=0).reshape(16, 1, 2, 1024)
    return (y_prompt.astype(np.float32), y_sample.astype(np.float32), nk.astype(np.float32),
            nv.astype(np.float32), nh.astype(np.float32))


def kernel(**inputs):
    return run(inputs, FULL)
```

```python
import numpy as np
from contextlib import ExitStack
import concourse.bass as bass
import concourse.mybir as mybir
from concourse.bass_utils import run_bass_kernel_spmd

F32 = mybir.dt.float32
BF16 = mybir.dt.bfloat16
AF = mybir.ActivationFunctionType
ALU = mybir.AluOpType

D = 1024
KC = 8
NT = 1536
DFF = 2816
FC = 22
NCORES = 8
EPS = 1e-6


class Sched:
    ENG = ("pe", "act", "dve", "pool", "sp")

    def __init__(self, nc, es):
        self.nc = nc
        self.ops = {e: [] for e in self.ENG}
        self.cnt = {e: 0 for e in self.ENG}
        self.esem = {e: es.enter_context(nc.semaphore("s_" + e)) for e in self.ENG}
        self.dsem = {q: [es.enter_context(nc.semaphore("d_%s%d" % (q, i))) for i in range(n)]
                     for q, n in (("sp", 24), ("pool", 12))}
        self.dtgt = {}
        self.drr = {"sp": 0, "pool": 0}
        self.lastw = {}
        self.rd = {}
        self.seen = {e: {} for e in self.ENG}
        self.pending = {e: {} for e in self.ENG}
        self.out_tokens = []

    def _deps(self, eng, idx, reads, writes, strict=False):
        waits = dict(self.pending[eng])
        self.pending[eng] = {}
        for sk in list(waits):
            if self.seen[eng].get(sk, 0) >= waits[sk]:
                del waits[sk]

        def need(tok, raw):
            sk, v, pe, pidx = tok
            if pe == eng and eng != "pool":
                if eng == "pe":
                    return
                if raw == "war":
                    return
            if self.seen[eng].get(sk, 0) >= v:
                return
            if waits.get(sk, 0) < v:
                waits[sk] = v

        for r in reads:
            t = self.lastw.get(r)
            if t:
                need(t, True)
        for w in writes:
            t = self.lastw.get(w)
            if t:
                need(t, "waw")
            for t in self.rd.get(w, {}).values():
                need(t, "war")
        for sk, v in waits.items():
            self.seen[eng][sk] = v
        return waits

    def _commit(self, tok, reads, writes):
        key = tok[2] if tok[2] else tok[0]
        for r in reads:
            self.rd.setdefault(r, {})[key] = tok
        for w in writes:
            self.lastw[w] = tok
            self.rd[w] = {}

    def op(self, eng, fn, reads=(), writes=()):
        idx = self.cnt[eng]
        waits = self._deps(eng, idx, reads, writes)
        self.cnt[eng] += 1
        tok = (("e", eng), idx + 1, eng, idx)
        self._commit(tok, reads, writes)
        self.ops[eng].append((list(waits.items()), fn, None))
        return tok

    def dma(self, q, fn, reads=(), writes=(), out=False):
        idx = self.cnt[q]
        waits = self._deps(q, idx, reads, writes, strict=True)
        k = self.drr[q]
        self.drr[q] += 1
        sk = ("d", q, k % len(self.dsem[q]))
        prev = self.dtgt.get(sk, 0)
        if prev and self.seen[q].get(sk, 0) < prev:
            waits[sk] = prev
            self.seen[q][sk] = prev
        self.dtgt[sk] = prev + 16
        tok = (sk, prev + 16, None, None)
        self._commit(tok, reads, writes)
        self.ops[q].append((list(waits.items()), fn, sk))
        if out:
            self.out_tokens.append(tok)
        return tok

    def barrier(self):
        for e in self.ENG:
            p = self.pending[e]
            for f in self.ENG:
                if f != e and self.cnt[f] > 0:
                    sk = ("e", f)
                    p[sk] = max(p.get(sk, 0), self.cnt[f])
            for sk, v in self.dtgt.items():
                p[sk] = max(p.get(sk, 0), v)

    def finish(self):
        waits = {}
        for sk, v in self.dtgt.items():
            waits[sk] = v
        for f in self.ENG:
            if f != "sp" and self.cnt[f] > 0:
                waits[("e", f)] = self.cnt[f]
        self.ops["sp"].append((list(waits.items()), None, None))

    def sem(self, sk):
        return self.esem[sk[1]] if sk[0] == "e" else self.dsem[sk[1]][sk[2]]

    def emit(self, eng, e):
        for waits, fn, dsk in self.ops[eng]:
            attach = None
            if eng != "pe" and fn is not None and waits:
                attach = waits[-1]
                waits = waits[:-1]
            for sk, v in waits:
                e.wait_ge(self.sem(sk), v)
            if fn is None:
                continue
            if eng == "pe":
                px = _PEProxy(e, self.sem(attach[0]), attach[1]) if attach is not None else e
                ins = fn(px)
            else:
                ins = fn(e)
                if attach is not None:
                    ins._wait_ge(self.sem(attach[0]), attach[1])
            if dsk is None:
                ins.then_inc(self.esem[eng], 1)
            else:
                ins.then_inc(self.sem(dsk), 16)


class _PEProxy:
    def __init__(self, e, sem, val):
        self.e, self.sem, self.val = e, sem, val

    def _first(self, ins):
        if self.sem is not None:
            ins._wait_ge(self.sem, self.val)
            self.sem = None
        return ins

    def matmul(self, *a, **k):
        return self._first(self.e.matmul(*a, **k))

    def transpose(self, *a, **k):
        return self._first(self.e.transpose(*a, **k))


def _prod(s):
    r = 1
    for v in s:
        r *= v
    return r


def carve(base, off, dtype, shape):
    esz = 4 if dtype == F32 else 2
    nb = _prod(shape[1:]) * esz
    assert off % 4 == 0 and nb % 4 == 0
    sl = base[:, off // 4:(off + nb) // 4]
    if dtype != F32:
        sl = sl.bitcast(dtype)
    if len(shape) == 2:
        return sl
    names = "abcd"[:len(shape) - 1]
    pat = "p (%s) -> p %s" % (" ".join(names), " ".join(names))
    kw = {names[i]: shape[i + 1] for i in range(1, len(names))}
    return sl.rearrange(pat, **kw)


FULL = dict(attn=True, ffn0=True, lru=True, ffn1=True)


def build(cfg=FULL):
    nc = bass.Bass("TRN2", target_bir_lowering=False)

    def din(name, shape, dt=F32):
        return nc.dram_tensor(name, shape, dt, kind="ExternalInput").ap()

    def dout(name, shape):
        return nc.dram_tensor(name, shape, F32, kind="ExternalOutput").ap()

    x_d = din("x", [NT, D])
    smalls_d = din("smalls", [256, 128])
    ident_d = din("ident", [128, 128])
    cm_d = din("cm", [128, 64])
    ck_d = din("ck", [512, D])
    cv_d = din("cv", [512, D])
    rpb_d = din("rpb", [16, 15, 31])
    wmod_d = din("w_mod", [2, D, 6144])
    wqkv_d = din("w_qkv", [D, 3072])
    wo_d = din("w_o", [D, D])
    win_d = din("w_in", [D, 2048])
    wa_d = din("w_a", [2, 8, 128, 128])
    wi_d = din("w_i", [2, 8, 128, 128])
    wout_d = din("w_out", [D, D])
    wgu_d = din("w_gu", [2, D, 2 * DFF])
    wdown_d = din("w_down", [2, DFF, D])
    y_d = dout("y", [NT, D])
    nk_d = dout("nk", [512, D])
    nv_d = dout("nv", [512, D])
    nh_d = dout("nh", [32, 128])
    epd_t = nc.dram_tensor("epd", [16 * 15 * 127 + 256], BF16, kind="Internal")
    epd = epd_t.ap()

    with ExitStack() as es:
        S = Sched(nc, es)

        def sb(name, shape, dt):
            return es.enter_context(nc.sbuf_tensor("sb_" + name, shape, dt))

        X = sb("X", [128, KC, NT], F32)
        WR = sb("WR", [128, 3, 4096], BF16)
        VT = sb("VT", [128, 256], F32)
        MOD = sb("MOD", [128, 2, 48, 2], F32)
        GS = sb("GS", [128, 2, 2, KC, 2], F32)
        ident = sb("ident", [128, 128], F32)
        ones = sb("ones", [128, 128], BF16)
        cmb = sb("cmb", [128, 64], BF16)
        scT = sb("scT", [128, KC, 2], BF16)
        nls = sb("nls", [128, 16], F32)
        ltmp = sb("ltmp", [128, 4, 16], F32)
        NH = sb("NH", [128, 32], F32)
        ARENA_B = 122880
        STRIP = 12288
        arena = sb("arena", [128, (ARENA_B + STRIP) // 4], F32)
        rstd = [carve(arena, ARENA_B + i * 2048, F32, [128, 512]) for i in range(3)]
        tmpf = [carve(arena, ARENA_B + 6144 + i * 2048, F32, [128, 512]) for i in range(3)]
        xsq8 = carve(arena, 98304, BF16, [128, KC, 512])
        banks = [es.enter_context(nc.psum_tensor("bk%d" % i, [128, 512], F32)) for i in range(8)]

        state = {"bank": 0, "ev": 0, "xsq": 0, "tmp": 0, "kst": 0, "pt": 0}

        def bank(pool=(0, 1, 2, 3, 4, 5)):
            i = pool[state["bank"] % len(pool)]
            state["bank"] += 1
            return i

        def evac_eng():
            state["ev"] += 1
            return "act" if state["ev"] % 2 else "dve"

        def copy_op(eng, out, in_, reads, writes, scale=None):
            if eng == "act":
                if scale is None:
                    S.op("act", lambda e, o=out, i=in_: e.activation(out=o, in_=i, func=AF.Copy), reads, writes)
                else:
                    S.op("act", lambda e, o=out, i=in_, s=scale: e.activation(out=o, in_=i, func=AF.Copy, scale=s),
                         reads, writes)
            else:
                if scale is None:
                    S.op(eng, lambda e, o=out, i=in_: e.tensor_copy(out=o, in_=i), reads, writes)
                else:
                    S.op(eng, lambda e, o=out, i=in_, s=scale: e.tensor_scalar(out=o, in0=i, scalar1=s, scalar2=None,
                                                                               op0=ALU.mult), reads, writes)

        def B(i):
            return ("bank", i)

        def vt(col, n=1):
            return VT[:, col:col + n]

        COND = [0, 1, 1]

        plan = []

        def add_k8(w2d, c0):
            plan.append(("k8", (w2d, c0)))

        for_layers = []
        def mod_slabs(l, qs):
            for q in qs:
                plan.append(("k8", (wmod_d[l], q * 512)))

        mod_slabs(0, range(0, 4))
        if cfg["attn"]:
            for q in range(6):
                plan.append(("k8", (wqkv_d, q * 512)))
            mod_slabs(0, range(4, 6))
            mod_slabs(0, range(6, 10))
            for q in range(2):
                plan.append(("k8", (wo_d, q * 512)))
        else:
            mod_slabs(0, range(4, 6))
            mod_slabs(0, range(6, 10))
        if cfg["ffn0"]:
            for j in range(11):
                plan.append(("pair", (wgu_d[0], j * 256, DFF)))
                if j in (2, 4, 6, 8):
                    mod_slabs(1, [j // 2 - 1])
            mod_slabs(0, range(10, 12))
            for m in range(8):
                plan.append(("down", (wdown_d[0], m * 128)))
        else:
            mod_slabs(0, range(10, 12))
            mod_slabs(1, range(0, 4))
        if cfg["lru"]:
            plan.append(("pair", (win_d, 0 * 256, 1024)))
            plan.append(("pair", (win_d, 1 * 256, 1024)))
            mod_slabs(1, [6])
            plan.append(("pair", (win_d, 2 * 256, 1024)))
            mod_slabs(1, [7, 8])
            plan.append(("pair", (win_d, 3 * 256, 1024)))
            mod_slabs(1, [9])
            mod_slabs(1, range(4, 6))
            for q in range(2):
                plan.append(("k8", (wout_d, q * 512)))
        else:
            mod_slabs(1, range(4, 6))
            mod_slabs(1, range(6, 10))
        if cfg["ffn1"]:
            for j in range(11):
                plan.append(("pair", (wgu_d[1], j * 256, DFF)))
            mod_slabs(1, range(10, 12))
            for m in range(8):
                plan.append(("down", (wdown_d[1], m * 128)))
        else:
            mod_slabs(1, range(10, 12))

        wstate = {"loaded": 0, "next": 0}

        def w_load(j):
            kind, a = plan[j]
            s = j % 3
            res = [("ws", s)]
            if kind == "k8":
                w2d, c0 = a
                src = w2d[:, c0:c0 + 512].rearrange("(kc p) n -> p kc n", p=128)
                dst = WR[:, s, :].rearrange("p (kc n) -> p kc n", n=512)
                S.dma("pool", lambda e, o=dst, i=src: e.dma_start(out=o, in_=i), writes=res)
            elif kind == "pair":
                w2d, c0, off = a
                dst = WR[:, s, :].rearrange("p (kc t n) -> p kc t n", t=2, n=256)
                for t in range(2):
                    src = w2d[:, t * off + c0:t * off + c0 + 256].rearrange("(kc p) n -> p kc n", p=128)
                    S.dma("pool", lambda e, o=dst[:, :, t, :], i=src: e.dma_start(out=o, in_=i), writes=res)
            elif kind == "down":
                w2d, c0 = a
                src = w2d[:, c0:c0 + 128].rearrange("(kc p) n -> p kc n", p=128)
                dst = WR[:, s, 0:FC * 128].rearrange("p (kc n) -> p kc n", n=128)
                S.dma("pool", lambda e, o=dst, i=src: e.dma_start(out=o, in_=i), writes=res)
            elif kind == "lrug":
                dst = WR[:, s, :].rearrange("p (w d n j) -> p w d n j", w=2, d=2, n=8)
                for wi_, wd in enumerate((wa_d, wi_d)):
                    for d in range(2):
                        src = wd[d].rearrange("n k j -> k n j")
                        S.dma("pool", lambda e, o=dst[:, wi_, d, :, :], i=src: e.dma_start(out=o, in_=i), writes=res)

        def w_get(kind, ahead=3):
            i = wstate["next"]
            wstate["next"] += 1
            assert plan[i][0] == kind, (i, plan[i][0], kind)
            while wstate["loaded"] < min(i + ahead, len(plan)):
                w_load(wstate["loaded"])
                wstate["loaded"] += 1
            return i % 3

        while wstate["loaded"] < min(3, len(plan)):
            w_load(wstate["loaded"])
            wstate["loaded"] += 1

        S0 = carve(arena, 0, F32, [128, 2, 128])
        cmf = carve(arena, 1024, F32, [128, 64])
        S.dma("sp", lambda e: e.dma_start(out=ident[:], in_=ident_d), writes=["ident"])
        S.dma("sp", lambda e: e.dma_start(out=S0, in_=smalls_d.rearrange("(t r) c -> r t c", t=2)), writes=["S0"])
        S.op("pool", lambda e: e.memset(ones[:], 1.0), writes=["ones"])
        S.dma("sp", lambda e: e.dma_start(out=cmf, in_=cm_d), writes=["cmf"])
        S.op("pool", lambda e: e.tensor_copy(out=cmb[:], in_=cmf), reads=["cmf"], writes=["cmb"])

        def f_sm(e):
            e.transpose(out=banks[6][:, 0:128], in_=S0[:, 0, :], identity=ident[:])
            return e.transpose(out=banks[6][:, 128:256], in_=S0[:, 1, :], identity=ident[:])

        S.op("pe", f_sm, reads=["ident", "S0"], writes=[B(6)])
        S.op("dve", lambda e: e.tensor_copy(out=VT[:], in_=banks[6][:, 0:256]), reads=[B(6)], writes=["VT"])
        C_BMOD, C_NG, C_FG, C_CW, C_CB, C_BA, C_BI, C_LAM, C_CP, C_H0 = 0, 96, 128, 136, 168, 176, 192, 208, 224, 240
        for cnd in range(2):
            S.op("act", lambda e, c=cnd: e.activation(out=scT[:, :, c], in_=VT[:, C_CP + c * 8:C_CP + c * 8 + 8],
                                                      func=AF.Silu), reads=["VT"], writes=["scT"])

        def do_mod(l, qs, ahead=3):
            for q in qs:
                s = w_get("k8", ahead=ahead)
                wv = WR[:, s, :].rearrange("p (kc n) -> p kc n", n=512)

                def f(e, wv=wv, q=q):
                    ins = None
                    for jj in range(4):
                        j = 4 * q + jj
                        for kc in range(KC):
                            ins = e.matmul(banks[7][:, 2 * j:2 * j + 2], lhsT=wv[:, kc, jj * 128:(jj + 1) * 128],
                                           rhs=scT[:, kc, :], start=(kc == 0), stop=(kc == KC - 1))
                    return ins

                S.op("pe", f, reads=[("ws", s), "scT"], writes=[B(7)])
                pv = banks[7][:, 0:96].rearrange("p (j c) -> p j c", c=2)
                for cnd in range(2):
                    S.op("dve", lambda e, q=q, c=cnd, l=l: e.tensor_tensor(
                        out=MOD[:, l, 4 * q:4 * q + 4, c], in0=pv[:, 4 * q:4 * q + 4, c],
                        in1=VT[:, C_BMOD + l * 48 + 4 * q:C_BMOD + l * 48 + 4 * q + 4], op=ALU.add),
                        reads=[B(7), "VT"], writes=[("mod", l, q)])

        def mod_vec(l, which, kc, cnd):
            return MOD[:, l, which * 8 + kc, cnd:cnd + 1]

        def mod_res(l, which):
            return [("mod", l, 2 * which), ("mod", l, 2 * which + 1)]

        def make_gs(l, s):
            which = 1 if s == 0 else 4
            for cnd in range(2):
                S.op("dve", lambda e, c=cnd: e.tensor_scalar(out=GS[:, l, s, :, c], in0=MOD[:, l, which * 8:which * 8 + 8, c],
                                                             scalar1=1.0, scalar2=None, op0=ALU.add),
                     reads=mod_res(l, which), writes=[("gs", l, s, cnd)])
                S.op("dve", lambda e, c=cnd: e.tensor_tensor(out=GS[:, l, s, :, c], in0=GS[:, l, s, :, c],
                                                             in1=VT[:, C_NG + l * 16 + s * 8:C_NG + l * 16 + s * 8 + 8],
                                                             op=ALU.mult),
                     reads=[("gs", l, s, cnd), "VT"], writes=[("gs", l, s, cnd)])

        def stats(tt):
            cs = slice(tt * 512, (tt + 1) * 512)
            S.op("act", lambda e: e.activation(out=xsq8[:, 0:4, :], in_=X[:, 0:4, cs], func=AF.Square),
                 reads=[("X", kc, tt) for kc in range(4)], writes=[("xsq", 0)])
            S.op("dve", lambda e: e.tensor_tensor(out=xsq8[:, 4:8, :], in0=X[:, 4:8, cs], in1=X[:, 4:8, cs], op=ALU.mult),
                 reads=[("X", kc, tt) for kc in range(4, 8)], writes=[("xsq", 1)])

            def f(e):
                ins = None
                for kc in range(KC):
                    ins = e.matmul(banks[6][:, :], lhsT=ones[:], rhs=xsq8[:, kc, :], start=(kc == 0), stop=(kc == KC - 1))
                return ins

            S.op("pe", f, reads=[("xsq", 0), ("xsq", 1), "ones"], writes=[B(6)])
            r = tt % 3
            S.op("act", lambda e, r=r: e.activation(out=rstd[r], in_=banks[6][:, :], func=AF.Ln, scale=1.0 / D, bias=EPS),
                 reads=[B(6)], writes=[("rstd", r)])
            S.op("act", lambda e, r=r: e.activation(out=rstd[r], in_=rstd[r], func=AF.Exp, scale=-0.5),
                 reads=[("rstd", r)], writes=[("rstd", r)])
            return r

        xm = carve(arena, 0, BF16, [128, KC, NT])

        def norm_mod(l, s):
            sh = 0 if s == 0 else 3
            make_gs(l, s)

            def modulate(tt, r):
                cnd = COND[tt]
                cs = slice(tt * 512, (tt + 1) * 512)
                for kc in range(KC):
                    b = state["tmp"] % 3
                    state["tmp"] += 1
                    S.op("dve", lambda e, b=b, kc=kc, r=r, cs=cs: e.tensor_tensor(out=tmpf[b], in0=X[:, kc, cs],
                                                                                 in1=rstd[r], op=ALU.mult),
                         reads=[("X", kc, tt), ("rstd", r)], writes=[("tmpf", b)])
                    S.op("act", lambda e, b=b, kc=kc, cs=cs, cnd=cnd: e.activation(
                        out=xm[:, kc, cs], in_=tmpf[b], func=AF.Identity,
                        scale=GS[:, l, s, kc, cnd:cnd + 1], bias=mod_vec(l, sh, kc, cnd)),
                        reads=[("tmpf", b), ("gs", l, s, cnd)] + mod_res(l, sh), writes=[("xm", kc, tt)])

            r0 = stats(0)
            r1 = stats(1)
            modulate(0, r0)
            r2 = stats(2)
            modulate(1, r1)
            modulate(2, r2)

        def residual(l, which, bk, m, tt):
            cnd = COND[tt]
            cs = slice(tt * 512, (tt + 1) * 512)
            S.op("dve", lambda e: e.scalar_tensor_tensor(out=X[:, m, cs], in0=banks[bk][:, :],
                                                         scalar=mod_vec(l, which, m, cnd), in1=X[:, m, cs],
                                                         op0=ALU.mult, op1=ALU.add),
                 reads=[B(bk), ("X", m, tt)] + mod_res(l, which), writes=[("X", m, tt)])

        do_mod(0, [0, 1, 2])

        xst = [carve(arena, 8192 + i * 16384, F32, [128, 4, D]) for i in range(2)]
        for tt in range(3):
            st = xst[tt % 2]
            for i in range(4):
                g = tt * 4 + i
                S.dma("sp", lambda e, st=st, i=i, g=g: e.dma_start(out=st[:, i, :], in_=x_d[g * 128:(g + 1) * 128, :]),
                      writes=[("xst", tt % 2, i)])
            for kc in range(KC):
                bk = bank()

                def f(e, st=st, kc=kc, bk=bk):
                    ins = None
                    for i in range(4):
                        ins = e.transpose(out=banks[bk][:, i * 128:(i + 1) * 128], in_=st[:, i, kc * 128:(kc + 1) * 128],
                                          identity=ident[:])
                    return ins

                S.op("pe", f, reads=[("xst", tt % 2, i) for i in range(4)] + ["ident"], writes=[B(bk)])
                copy_op(evac_eng(), X[:, kc, tt * 512:(tt + 1) * 512], banks[bk][:, :], [B(bk)], [("X", kc, tt)])
        S.barrier()

        def ffn(l):
            norm_mod(l, 1)
            hbuf = carve(arena, 24576, BF16, [128, FC, NT])
            sgs = [carve(arena, 24576 + 67584 + i * 2048, F32, [128, 512]) for i in range(3)]
            sgi = [0]
            for jp in range(11):
                s = w_get("pair")
                wv = WR[:, s, :].rearrange("p (kc t n) -> p kc t n", t=2, n=256)
                for jj in range(2):
                    j = 2 * jp + jj
                    for tt in range(3):
                        cs = slice(tt * 512, (tt + 1) * 512)
                        bg = bank()
                        bu = bank()
                        for t, bk in ((0, bg), (1, bu)):
                            def f(e, t=t, bk=bk, jj=jj, cs=cs, wv=wv):
                                ins = None
                                for kc in range(KC):
                                    ins = e.matmul(banks[bk][:, :], lhsT=wv[:, kc, t, jj * 128:(jj + 1) * 128],
                                                   rhs=xm[:, kc, cs], start=(kc == 0), stop=(kc == KC - 1))
                                return ins

                            S.op("pe", f, reads=[("ws", s)] + [("xm", kc, tt) for kc in range(KC)], writes=[B(bk)])
                        g = sgi[0] % 3
                        sgi[0] += 1
                        S.op("act", lambda e, g=g, bg=bg: e.activation(out=sgs[g], in_=banks[bg][:, :], func=AF.Silu),
                             reads=[B(bg)], writes=[("sg", g)])
                        S.op("dve", lambda e, g=g, bu=bu, j=j, cs=cs: e.tensor_tensor(out=hbuf[:, j, cs], in0=banks[bu][:, :],
                                                                                     in1=sgs[g], op=ALU.mult),
                             reads=[B(bu), ("sg", g)], writes=[("h", j, tt)])
                if l == 0 and jp in (2, 4, 6, 8):
                    do_mod(1, [jp // 2 - 1])
            if (l == 0 and True) or l == 1:
                do_mod(l, range(10, 12))
            for m in range(KC):
                s = w_get("down")
                wv = WR[:, s, 0:FC * 128].rearrange("p (kc n) -> p kc n", n=128)
                for tt in range(3):
                    cs = slice(tt * 512, (tt + 1) * 512)
                    bk = bank()

                    def f(e, bk=bk, cs=cs, wv=wv):
                        ins = None
                        for j in range(FC):
                            ins = e.matmul(banks[bk][:, :], lhsT=wv[:, j, :], rhs=hbuf[:, j, cs],
                                           start=(j == 0), stop=(j == FC - 1))
                        return ins

                    S.op("pe", f, reads=[("ws", s)] + [("h", j, tt) for j in range(FC)], writes=[B(bk)])
                    residual(l, 5, bk, m, tt)
            S.barrier()

        def attention():
            l = 0
            norm_mod(0, 0)
            QA = carve(arena, 24576, BF16, [128, KC, NT])
            KT = carve(arena, 49152, BF16, [128, KC, NT])
            VA = carve(arena, 73728, BF16, [128, 12, 8, 192])
            A0 = 0
            Pt = [carve(arena, A0 + 18432 + i * 1024, BF16, [128, 512]) for i in range(6)]
            Ct = [carve(arena, A0 + 4096 + i * 2816, BF16, [128, 22, 64]) for i in range(2)]
            rden = [carve(arena, A0 + 9728 + i * 2048, F32, [128, 512]) for i in range(2)]
            QZ = [[carve(arena, A0 + 13824 + (hp_ * 2 + i) * 1024, BF16, [128, 512]) for i in range(2)] for hp_ in range(2)]
            R1 = carve(arena, 110592 + 128, F32, [128, 2, 31])
            EPs = carve(arena, 110592 + 128 + 248, BF16, [128, 2, 128])
            kst = carve(arena, 110592 + 1024, F32, [128, 2816])
            if cfg.get("a_vam", True):
                S.op("pool", lambda e: e.memset(VA[:, :, :, 64:128], 1.0),
                     writes=[("vaug", t) for t in range(12)] + [("xsq", 0), ("xsq", 1)])
            ETAB = cfg.get("a_etab", True)
            if ETAB:
                S.dma("sp", lambda e: e.dma_start(out=R1[0:120], in_=rpb_d.rearrange("(hg h) r c -> (h r) hg c", hg=2)),
                      writes=["R1"])
                S.op("pool", lambda e: e.memset(EPs[0:120], 0.0), writes=["EPs"])
                S.op("act", lambda e: e.activation(out=EPs[0:120, :, 48:79], in_=R1[0:120, :, :], func=AF.Exp),
                     reads=["R1", "EPs"], writes=["EPs"])
                epd_v = epd[0:16 * 15 * 127].rearrange("(hg q j) -> q hg j", hg=2, j=127)
                S.dma("sp", lambda e: e.dma_start(out=epd_v, in_=EPs[0:120, :, 0:127]), reads=["EPs"], writes=["epd"])
            def prep_ct(h):
                i = h % 2
                if not ETAB:
                    return
                for a, s0 in ((0, 4), (1, 3)):
                    src = bass.AP(tensor=epd_t, offset=h * 15 * 127, ap=[[1, 64], [127, 15], [1, 64]])
                    S.dma("sp", lambda e, i=i, a=a, s0=s0, src=src: e.dma_start(
                        out=Ct[i][a * 64:(a + 1) * 64, s0:s0 + 15, :], in_=src), reads=["epd"], writes=[("ct", i)])
                S.op("pool", lambda e, i=i: e.tensor_tensor(out=Ct[i][:, 3:19, :], in0=Ct[i][:, 3:19, :],
                                                            in1=cmb[:].unsqueeze(1).to_broadcast([128, 16, 64]),
                                                            op=ALU.mult),
                     reads=[("ct", i), "cmb"], writes=[("ct", i)])

            for q in range(6):
                s = w_get("k8")
                wv = WR[:, s, :].rearrange("p (kc n) -> p kc n", n=512)
                xr_ = [("xm", kc, tt) for kc in range(KC) for tt in range(3)]
                if q < 4:
                    dst, nm, scl = (QA, "qa", 0.125) if q < 2 else (KT, "kt", None)
                    for mm in range(4):
                        c = (q % 2) * 4 + mm
                        for tt in range(3):
                            cs = slice(tt * 512, (tt + 1) * 512)
                            bk = bank()

                            def f(e, bk=bk, cs=cs, wv=wv, mm=mm):
                                ins = None
                                for kc in range(KC):
                                    ins = e.matmul(banks[bk][:, :], lhsT=wv[:, kc, mm * 128:(mm + 1) * 128],
                                                   rhs=xm[:, kc, cs], start=(kc == 0), stop=(kc == KC - 1))
                                return ins

                            S.op("pe", f, reads=[("ws", s)] + [("xm", kc, tt) for kc in range(KC)], writes=[B(bk)])
                            if nm == "qa":
                                wr = [("qa", c, 0, tt), ("qa", c, 1, tt)]
                            else:
                                wr = [("kt", c, tt)]
                            copy_op(evac_eng(), dst[:, c, cs], banks[bk][:, :], [B(bk)], wr, scale=scl)
                if q in (2, 3, 4, 5) and cfg.get("a_tok", True):
                    isv = q >= 4
                    half = q % 2
                    for g in range(12 if isv else 4):
                        bk = bank()
                        ts_ = slice(g * 128, (g + 1) * 128)
                        tt = g // 4

                        def f(e, bk=bk, ts_=ts_, wv=wv):
                            ins = None
                            for kc in range(KC):
                                ins = e.matmul(banks[bk][:, :], lhsT=xm[:, kc, ts_], rhs=wv[:, kc, :],
                                               start=(kc == 0), stop=(kc == KC - 1))
                            return ins

                        S.op("pe", f, reads=[("ws", s)] + [("xm", kc, tt) for kc in range(KC)], writes=[B(bk)])
                        TM = cfg.get("a_tokm", 3)
                        eng = evac_eng()
                        if isv and TM >= 2:
                            vtile = 8 + g if g < 4 else g - 4
                            pv4 = banks[bk][:, :].rearrange("p (a t d) -> p a t d", t=2, d=64)
                            for t in range(2):
                                copy_op(eng, VA[:, vtile, half * 4:half * 4 + 4, t * 128:t * 128 + 64], pv4[:, :, t, :],
                                        [B(bk)], [("vaug", vtile)])
                        if g < 4 and TM >= 3:
                            so = (state["kst"] % 5) * 512
                            state["kst"] += 1
                            stg = kst[:, so:so + 512]
                            copy_op(eng, stg, banks[bk][:, :], [B(bk)], [("kst", so)])
                            od = nv_d if isv else nk_d
                            if TM >= 4 or TM == 3 and cfg.get("a_tokm", 3) == 3 and not cfg.get("a_nodma", False):
                              S.dma("sp", lambda e, od=od, g=g, half=half, stg=stg: e.dma_start(
                                out=od[g * 128:(g + 1) * 128, half * 512:(half + 1) * 512], in_=stg),
                                reads=[("kst", so)], out=True)
            S.barrier()
            for i in range(2):
                S.op("pool", lambda e, i=i: e.memset(Ct[i], 0.0), writes=[("ct", i)])
            for hp_ in range(2):
                for i in range(2):
                    S.op("pool", lambda e, hp_=hp_, i=i: e.memset(QZ[hp_][i], 0.0), writes=[("qz", hp_, i)])

            def prep_qz(c, hp, i, tt, q0, n, eng="dve"):
                ps_ = slice(hp * 64, hp * 64 + 64)
                S.op(eng, lambda e: e.tensor_copy(out=QZ[hp][i][ps_, 0:n], in_=QA[ps_, c, q0:q0 + n]),
                     reads=[("qa", c, hp, tt)], writes=[("qz", hp, i)])
            do_mod(0, range(4, 6))

            def run_items(items):
                n_it = len(items)
                import os
                DEPTH = int(os.environ.get('ADEPTH', '4'))
                for n in range(n_it + DEPTH):
                    if n < n_it:
                        items[n]["S"]()
                    if n >= DEPTH:
                        items[n - DEPTH]["PV"]()

            obank = [0]

            items = []
            for s_ in range(2):
                for h in range(16):
                    c, hp = h // 2, h % 2
                    ps = slice(hp * 64, hp * 64 + 64)
                    dps = slice(64, 128) if hp == 0 else slice(0, 64)
                    q0 = s_ * 256
                    it = {}

                    def fS(c=c, ps=ps, q0=q0, s_=s_, it=it):
                        bk = bank((0, 1, 2, 3, 6))
                        p = state["pt"] % 6
                        state["pt"] += 1
                        it["p"] = p

                        hp_ = 0 if ps.start == 0 else 1
                        prep_qz(c, hp_, s_, 0, q0, 256, eng="pool")

                        def f(e):
                            ins = None
                            for j in range(2):
                                ins = e.matmul(banks[bk][:, j * 256:(j + 1) * 256],
                                               lhsT=KT[:, c, q0 + j * 128:q0 + (j + 1) * 128],
                                               rhs=QZ[hp_][s_][:, 0:256], start=True, stop=True)
                            return ins

                        S.op("pe", f, reads=[("kt", c, 0), ("qz", hp_, s_)], writes=[B(bk)])
                        S.op("act", lambda e: e.activation(out=Pt[p], in_=banks[bk][:, :], func=AF.Exp),
                             reads=[B(bk)], writes=[("pt", p)])

                    def fPV(c=c, hp=hp, ps=ps, dps=dps, q0=q0, s_=s_, it=it):
                        p = it["p"]
                        ob = 4 + obank[0] % 2
                        obank[0] += 1

                        def f(e):
                            ins = None
                            for j in range(2):
                                ins = e.matmul(banks[ob][:, 0:256], lhsT=VA[:, 8 + s_ * 2 + j, c, hp * 64:hp * 64 + 128],
                                               rhs=Pt[p][:, j * 256:(j + 1) * 256], start=(j == 0), stop=(j == 1))
                            return ins

                        S.op("pe", f, reads=[("pt", p), ("vaug", 8 + s_ * 2), ("vaug", 9 + s_ * 2)], writes=[B(ob)])
                        r = ob - 4
                        S.op("act", lambda e: e.activation(out=rden[r][dps, 0:256], in_=banks[ob][dps, 0:256], func=AF.Ln),
                             reads=[B(ob)], writes=[("rden", r)])
                        S.op("act", lambda e: e.activation(out=rden[r][dps, 0:256], in_=rden[r][dps, 0:256], func=AF.Exp,
                                                           scale=-1.0),
                             reads=[("rden", r)], writes=[("rden", r)])
                        S.op("dve", lambda e: e.tensor_tensor(out=QA[ps, c, q0:q0 + 256], in0=banks[ob][ps, 0:256],
                                                              in1=rden[r][dps, 0:256], op=ALU.mult),
                             reads=[B(ob), ("rden", r)], writes=[("qa", c, hp, 0)])

                    it["S"] = fS
                    it["PV"] = fPV
                    items.append(it)
            if cfg.get("a_ctx", True):
                run_items(items)

            for ct in range(4 if cfg.get("a_cache", True) else 0):
                for t in range(2):
                    src = cv_d[ct * 128:(ct + 1) * 128, :].rearrange("p (a t d) -> p a t d", t=2, d=64)[:, :, t, :]
                    S.dma("pool", lambda e, ct=ct, t=t, src=src: e.dma_start(
                        out=VA[:, 8 + ct, :, t * 128:t * 128 + 64], in_=src), writes=[("vaug", 8 + ct)])
            kstg = kst[:, 0:2048].rearrange("p (a b) -> p a b", a=2)
            for ct in range(4 if cfg.get("a_cache", True) else 0):
                buf = ct % 2
                S.dma("sp", lambda e, ct=ct, buf=buf: e.dma_start(out=kstg[:, buf, :], in_=ck_d[ct * 128:(ct + 1) * 128, :]),
                      writes=[("kstg", buf)] + [("kst", so) for so in range(0, 2560, 512)])
                for hh in range(2):
                    bk = bank((0, 1, 2, 3))

                    def f(e, bk=bk, hh=hh, buf=buf):
                        ins = None
                        for q in range(4):
                            c = hh * 4 + q
                            ins = e.transpose(out=banks[bk][:, q * 128:(q + 1) * 128],
                                              in_=kstg[:, buf, c * 128:(c + 1) * 128], identity=ident[:])
                        return ins

                    S.op("pe", f, reads=[("kstg", buf), "ident"], writes=[B(bk)])
                    copy_op(evac_eng(), KT[:, hh * 4:hh * 4 + 4, ct * 128:(ct + 1) * 128],
                            banks[bk][:, :].rearrange("p (a b) -> p a b", b=128), [B(bk)],
                            [("kt", hh * 4 + q, 0) for q in range(4)])

            PB = [carve(arena, 112640 + i * 1024, BF16, [128, 512]) for i in range(10)] + \
                 [carve(arena, A0 + i * 1024, BF16, [128, 512]) for i in range(2)]
            for i in range(12):
                S.op("pool", lambda e, i=i: e.memset(PB[i], 0.0),
                     writes=[("pb", i), ("kstg", 0), ("kstg", 1)] + [("kst", so) for so in range(0, 2560, 512)])

            def rs_(r):
                return min(max(r - 4, 0), 8)

            def valid(rk, r):
                return rs_(r) <= rk <= rs_(r) + 7

            lat_tiles = {0: list(range(0, 6)), 1: list(range(2, 8))}
            items = []
            prep_ct(0)
            prep_qz(0, 0, 0, 1, 512, 512)
            for h in range(16):
                c, hp = h // 2, h % 2
                ps = slice(hp * 64, hp * 64 + 64)
                dps = slice(64, 128) if hp == 0 else slice(0, 64)
                for qt in range(2):
                    tt = 1 + qt
                    q0 = 512 + qt * 512
                    tiles = [("c", ct) for ct in range(4)] + [("l", kt) for kt in lat_tiles[qt]]
                    for n, (kind, t) in enumerate(tiles):
                        it = {}
                        first = (n == 0)
                        last = (n == len(tiles) - 1)

                        def fS(c=c, hp=hp, ps=ps, q0=q0, tt=tt, qt=qt, kind=kind, t=t, it=it, h=h, first=first):
                            if first:
                                nh_, nq_ = (h, 1) if qt == 0 else (h + 1, 0)
                                if nh_ < 16:
                                    prep_qz(nh_ // 2, nh_ % 2, nq_, 1 + nq_, 512 + nq_ * 512, 512)
                                if qt == 1 and h + 1 < 16:
                                    prep_ct(h + 1)
                                if qt == 0 and h in (2, 6, 10, 14):
                                    do_mod(0, [6 + h // 4])
                            bk = bank((0, 1, 2, 3, 6))
                            p = state["pt"] % 6
                            state["pt"] += 1
                            it["p"] = p
                            k0 = t * 128 if kind == "c" else 512 + t * 128
                            kres = ("kt", c, 0) if kind == "c" else ("kt", c, 1 + t // 4)
                            if kind == "l":
                                vu = [b for b in range(8) if valid(2 * t, 8 * qt + b) or valid(2 * t + 1, 8 * qt + b)]
                                cols = slice(vu[0] * 64, (vu[-1] + 1) * 64)
                                assert vu == list(range(vu[0], vu[-1] + 1))
                            else:
                                cols = slice(0, 512)
                            it["cols"] = cols
                            S.op("pe", lambda e: e.matmul(banks[bk][:, cols], lhsT=KT[:, c, k0:k0 + 128],
                                                          rhs=QZ[hp][qt][:, cols], start=True, stop=True),
                                 reads=[kres, ("qz", hp, qt)], writes=[B(bk)])
                            S.op("act", lambda e: e.activation(out=Pt[p][:, cols], in_=banks[bk][:, cols], func=AF.Exp),
                                 reads=[B(bk)], writes=[("pt", p)])
                            if kind == "l":
                                Dd = 2 * t - 8 * qt
                                s_hi = Dd + 11
                                ci = h % 2
                                pbi = qt * 6 + lat_tiles[qt].index(t)
                                it["pb"] = pbi
                                vb = [[b for b in range(8) if valid(2 * t + a, 8 * qt + b)] for a in range(2)]
                                if vb[0] == vb[1]:
                                    jobs = [(slice(0, 128), vb[0])]
                                else:
                                    jobs = [(slice(a * 64, (a + 1) * 64), vb[a]) for a in range(2)]
                                for psl, vbl in jobs:
                                    if not vbl:
                                        continue
                                    b_lo, b_hi = vbl[0], vbl[-1] + 1
                                    assert vbl == list(range(b_lo, b_hi))
                                    hi_ = s_hi - b_lo
                                    lo_ = s_hi - b_hi
                                    ev_ = Ct[ci][psl, hi_:(lo_ if lo_ >= 0 else None):-1, ::-1]
                                    o3 = PB[pbi][psl, b_lo * 64:b_hi * 64].rearrange("p (b c) -> p b c", c=64)
                                    i3 = Pt[p][psl, b_lo * 64:b_hi * 64].rearrange("p (b c) -> p b c", c=64)
                                    S.op("dve", lambda e, o3=o3, i3=i3, ev_=ev_: e.tensor_tensor(out=o3, in0=i3, in1=ev_, op=ALU.mult),
                                         reads=[("pt", p), ("ct", ci)], writes=[("pb", pbi)])

                        def fPV(c=c, hp=hp, ps=ps, dps=dps, q0=q0, tt=tt, kind=kind, t=t, it=it, first=first, last=last):
                            p = it["p"]
                            if first:
                                obank[0] += 1
                            ob = 4 + obank[0] % 2
                            vtile = 8 + t if kind == "c" else t
                            cols = it["cols"]
                            if kind == "l":
                                rhs_, rres = PB[it["pb"]][:, cols], ("pb", it["pb"])
                            else:
                                rhs_, rres = Pt[p], ("pt", p)
                            S.op("pe", lambda e: e.matmul(banks[ob][:, cols], lhsT=VA[:, vtile, c, hp * 64:hp * 64 + 128],
                                                          rhs=rhs_, start=first, stop=last),
                                 reads=[rres, ("vaug", vtile)], writes=[B(ob)])
                            if last:
                                r = ob - 4
                                S.op("act", lambda e: e.activation(out=rden[r][dps, :], in_=banks[ob][dps, :], func=AF.Ln),
                                     reads=[B(ob)], writes=[("rden", r)])
                                S.op("act", lambda e: e.activation(out=rden[r][dps, :], in_=rden[r][dps, :], func=AF.Exp,
                                                                   scale=-1.0),
                                     reads=[("rden", r)], writes=[("rden", r)])
                                S.op("dve", lambda e: e.tensor_tensor(out=QA[ps, c, q0:q0 + 512], in0=banks[ob][ps, :],
                                                                      in1=rden[r][dps, :], op=ALU.mult),
                                     reads=[B(ob), ("rden", r)], writes=[("qa", c, hp, tt)])

                        it["S"] = fS
                        it["PV"] = fPV
                        items.append(it)
            if cfg.get("a_lat", True):
                run_items(items)

            S.barrier()
            slabs = [w_get("k8"), w_get("k8", ahead=2)]
            for tt in range(3):
                cs = slice(tt * 512, (tt + 1) * 512)
                for q in range(2):
                    s = slabs[q]
                    wv = WR[:, s, :].rearrange("p (kc n) -> p kc n", n=512)
                    for mm in range(4):
                        m = q * 4 + mm
                        bk = bank((0, 1, 2, 3))

                        def f(e, bk=bk, cs=cs, wv=wv, mm=mm):
                            ins = None
                            for kc in range(KC):
                                ins = e.matmul(banks[bk][:, :], lhsT=wv[:, kc, mm * 128:(mm + 1) * 128],
                                               rhs=QA[:, kc, cs], start=(kc == 0), stop=(kc == KC - 1))
                            return ins

                        S.op("pe", f, reads=[("ws", s)] + [("qa", kc, hp, tt) for kc in range(KC) for hp in range(2)],
                             writes=[B(bk)])
                        residual(0, 2, bk, m, tt)

        def lru():
            l = 1
            norm_mod(1, 0)
            S.barrier()
            ybuf = carve(arena, 24576, BF16, [128, KC, NT])
            o = 49152
            xrp = carve(arena, o, F32, [128, 1548]); o += 6192
            gg = []; xc = []; xcb = []
            for i in range(2):
                gg.append(carve(arena, o, F32, [128, NT])); o += 6144
                xc.append(carve(arena, o, F32, [128, NT])); o += 6144
                xcb.append(carve(arena, o, BF16, [128, NT])); o += 3072
            dirb = []
            for d in range(2):
                ra = carve(arena, o, F32, [128, NT]); o += 6144
                itb = carve(arena, o, F32, [128, NT]); o += 6144
                hs = carve(arena, o, F32, [128, NT]); o += 6144
                dirb.append((ra, itb, hs))
            wgb = carve(arena, o, BF16, [128, 2, 2, 8, 128]); o += 8192
            assert o <= ARENA_B + STRIP, o
            lam = VT[:, C_LAM:C_LAM + 16]
            yv, wv_, dv, lw = ltmp[:, 0, :], ltmp[:, 1, :], ltmp[:, 2, :], ltmp[:, 3, :]
            S.op("act", lambda e: e.activation(out=yv, in_=lam, func=AF.Exp, scale=-1.0), reads=["VT"], writes=["l_y"])
            S.op("dve", lambda e: e.tensor_scalar(out=wv_, in0=yv, scalar1=1.0, scalar2=None, op0=ALU.add),
                 reads=["l_y"], writes=["l_w"])
            S.op("dve", lambda e: e.tensor_scalar(out=dv, in0=wv_, scalar1=-1.0, scalar2=1e-30, op0=ALU.add, op1=ALU.max),
                 reads=["l_w"], writes=["l_d"])
            S.op("dve", lambda e: e.reciprocal(out=dv, in_=dv), reads=["l_d"], writes=["l_d"])
            S.op("act", lambda e: e.activation(out=lw, in_=wv_, func=AF.Ln), reads=["l_w"], writes=["l_lw"])
            S.op("dve", lambda e: e.tensor_tensor(out=dv, in0=dv, in1=yv, op=ALU.mult), reads=["l_d", "l_y"], writes=["l_d"])
            S.op("dve", lambda e: e.scalar_tensor_tensor(out=nls[:], in0=lw, scalar=-8.0, in1=dv, op0=ALU.mult, op1=ALU.mult),
                 reads=["l_lw", "l_d"], writes=["nls"])
            nls2 = ltmp[:, 0, :]
            S.op("dve", lambda e: e.tensor_scalar(out=nls2, in0=nls[:], scalar1=2.0, scalar2=None, op0=ALU.mult),
                 reads=["nls", "l_y", "l_d"], writes=["nls2", "l_y"])
            S.op("pool", lambda e: e.memset(xrp, 0.0), writes=["xrp"])
            wg = wgb
            for wi_, wd in enumerate((wa_d, wi_d)):
                for d in range(2):
                    S.dma("pool", lambda e, o_=wgb[:, wi_, d, :, :], i_=wd[d].rearrange("n k j -> k n j"):
                          e.dma_start(out=o_, in_=i_), writes=["wgb"])
            SEQ = [(2, 0, 256), (261, 256, 256), (520, 512, 1024)]
            pair = {}

            def stageA(m):
                bf_ = m % 2
                jp, jj = m // 2, m % 2
                if jj == 0:
                    s = w_get("pair")
                    pair["s"] = s
                s = pair["s"]
                wv = WR[:, s, :].rearrange("p (kc t n) -> p kc t n", t=2, n=256)
                for tt in range(3):
                    cs = slice(tt * 512, (tt + 1) * 512)
                    bg = bank()
                    bx = bank()
                    for t, bk in ((0, bg), (1, bx)):
                        def f(e, t=t, bk=bk, jj=jj, cs=cs, wv=wv):
                            ins = None
                            for kc in range(KC):
                                ins = e.matmul(banks[bk][:, :], lhsT=wv[:, kc, t, jj * 128:(jj + 1) * 128],
                                               rhs=xm[:, kc, cs], start=(kc == 0), stop=(kc == KC - 1))
                            return ins

                        S.op("pe", f, reads=[("ws", s)] + [("xm", kc, tt) for kc in range(KC)], writes=[B(bk)])
                    S.op("act", lambda e, bg=bg, cs=cs: e.activation(out=gg[bf_][:, cs], in_=banks[bg][:, :],
                                                                     func=AF.Gelu_apprx_tanh),
                         reads=[B(bg)], writes=[("gg", bf_)])
                    if tt == 0:
                        for sq in range(2):
                            po = SEQ[sq][0]
                            S.op("dve", lambda e, bx=bx, sq=sq, po=po: e.tensor_copy(
                                out=xrp[:, po:po + 256], in_=banks[bx][:, sq * 256:(sq + 1) * 256]),
                                reads=[B(bx)], writes=["xrp"])
                    else:
                        po = 520 + (tt - 1) * 512
                        S.op("dve", lambda e, bx=bx, po=po: e.tensor_copy(out=xrp[:, po:po + 512], in_=banks[bx][:, :]),
                             reads=[B(bx)], writes=["xrp"])
                for (po, co, ln) in SEQ:
                    base = po - 2
                    S.op("dve", lambda e, base=base, co=co, ln=ln: e.tensor_scalar(
                        out=xc[bf_][:, co:co + ln], in0=xrp[:, base:base + ln], scalar1=vt(C_CW + 0 * 8 + m),
                        scalar2=vt(C_CB + m), op0=ALU.mult, op1=ALU.add), reads=["xrp", "VT"], writes=[("xc", bf_)])
                    for j in range(1, 4):
                        S.op("dve", lambda e, base=base, co=co, ln=ln, j=j: e.scalar_tensor_tensor(
                            out=xc[bf_][:, co:co + ln], in0=xrp[:, base + j:base + j + ln], scalar=vt(C_CW + j * 8 + m),
                            in1=xc[bf_][:, co:co + ln], op0=ALU.mult, op1=ALU.add),
                            reads=["xrp", ("xc", bf_), "VT"], writes=[("xc", bf_)])
                S.op("dve", lambda e: e.tensor_copy(out=xcb[bf_], in_=xc[bf_]), reads=[("xc", bf_)], writes=[("xcb", bf_)])

            def stageB(m):
                bf_ = m % 2
                for d in range(2):
                    ra, itb, hs = dirb[d]
                    for tt in range(3):
                        cs = slice(tt * 512, (tt + 1) * 512)
                        ba_ = bank()
                        bi_ = bank()
                        for w_, bk in ((0, ba_), (1, bi_)):
                            S.op("pe", lambda e, w_=w_, bk=bk, cs=cs, d=d: e.matmul(
                                banks[bk][:, :], lhsT=wg[:, w_, d, m, :], rhs=xcb[bf_][:, cs], start=True, stop=True),
                                reads=["wgb", ("xcb", bf_)], writes=[B(bk)])
                        S.op("act", lambda e, ba_=ba_, cs=cs, d=d, ra=ra: e.activation(
                            out=ra[:, cs], in_=banks[ba_][:, :], func=AF.Sigmoid, bias=vt(C_BA + d * 8 + m)),
                            reads=[B(ba_), "VT"], writes=[("ra", d)])
                        S.op("act", lambda e, bi_=bi_, cs=cs, d=d, itb=itb: e.activation(
                            out=itb[:, cs], in_=banks[bi_][:, :], func=AF.Sigmoid, bias=vt(C_BI + d * 8 + m)),
                            reads=[B(bi_), "VT"], writes=[("it", d)])
                    col = d * 8 + m
                    S.op("act", lambda e, hs=hs, ra=ra, col=col: e.activation(out=hs, in_=ra, func=AF.Exp, scale=nls2[:, col:col + 1]),
                         reads=[("ra", d), "nls2"], writes=[("hs", d)])
                    S.op("act", lambda e, ra=ra, col=col: e.activation(out=ra, in_=ra, func=AF.Exp, scale=nls[:, col:col + 1]),
                         reads=[("ra", d), "nls"], writes=[("ra", d)])
                    S.op("act", lambda e, hs=hs: e.activation(out=hs, in_=hs, func=AF.Sqrt, scale=-1.0, bias=1.0),
                         reads=[("hs", d)], writes=[("hs", d)])
                    S.op("pool", lambda e, itb=itb: e.tensor_tensor(out=itb, in0=itb, in1=xc[bf_], op=ALU.mult),
                         reads=[("it", d), ("xc", bf_)], writes=[("it", d)])
                    S.op("pool", lambda e, itb=itb, hs=hs: e.tensor_tensor(out=itb, in0=itb, in1=hs, op=ALU.mult),
                         reads=[("it", d), ("hs", d)], writes=[("it", d)])
                    for sq, (po, co, ln) in enumerate(SEQ):
                        init = 0.0 if sq < 2 else vt(C_H0 + d * 8 + m)
                        if d == 0:
                            S.op("dve", lambda e, co=co, ln=ln, init=init, ra=ra, itb=itb, hs=hs: e.tensor_tensor_scan(
                                out=hs[:, co:co + ln], data0=ra[:, co:co + ln], data1=itb[:, co:co + ln],
                                initial=init, op0=ALU.mult, op1=ALU.add),
                                reads=[("ra", d), ("it", d), "VT"], writes=[("hs", d)])
                        else:
                            lo = co - 1 if co > 0 else None
                            S.op("dve", lambda e, co=co, ln=ln, init=init, lo=lo, ra=ra, itb=itb, hs=hs: e.tensor_tensor_scan(
                                out=hs[:, co + ln - 1:lo:-1], data0=ra[:, co + ln - 1:lo:-1],
                                data1=itb[:, co + ln - 1:lo:-1], initial=init, op0=ALU.mult, op1=ALU.add),
                                reads=[("ra", d), ("it", d), "VT"], writes=[("hs", d)])
                        if sq < 2:
                            c_ = co + ln - 1 if d == 0 else co
                            nhc = sq * 16 + d * 8 + m
                            S.op("dve", lambda e, c_=c_, nhc=nhc, hs=hs: e.tensor_copy(
                                out=NH[:, nhc:nhc + 1], in_=hs[:, c_:c_ + 1]), reads=[("hs", d)], writes=["NH"])
                hf, hb = dirb[0][2], dirb[1][2]
                S.op("dve", lambda e: e.tensor_tensor(out=hf, in0=hf, in1=hb, op=ALU.add),
                     reads=[("hs", 0), ("hs", 1)], writes=[("hs", 0)])
                S.op("dve", lambda e: e.tensor_tensor(out=ybuf[:, m, :], in0=hf, in1=gg[bf_], op=ALU.mult),
                     reads=[("hs", 0), ("gg", bf_)], writes=[("y", m)])

            stageA(0)
            for m in range(8):
                if m + 1 < 8:
                    stageA(m + 1)
                    if (m + 1) % 2 == 1:
                        while wstate["loaded"] < min(wstate["next"] + 2, len(plan)):
                            w_load(wstate["loaded"])
                            wstate["loaded"] += 1
                stageB(m)
                if m == 6:
                    do_mod(1, range(4, 6))
                if 2 <= m <= 5:
                    do_mod(1, [4 + m], ahead=2)
            S.barrier()
            S.op("pe", lambda e: e.transpose(out=banks[6][0:32, 0:128], in_=NH[:, :], identity=ident[:]),
                 reads=["NH", "ident"], writes=[B(6)])
            nhs = carve(arena, 110592, F32, [128, 128])
            S.op("dve", lambda e: e.tensor_copy(out=nhs[0:32, :], in_=banks[6][0:32, 0:128]), reads=[B(6)], writes=["nhs"])
            S.dma("sp", lambda e: e.dma_start(out=nh_d, in_=nhs[0:32, :]), reads=["nhs"], out=True)
            slabs = [w_get("k8"), w_get("k8", ahead=2)]
            for tt in range(3):
                cs = slice(tt * 512, (tt + 1) * 512)
                for q in range(2):
                    s = slabs[q]
                    wv = WR[:, s, :].rearrange("p (kc n) -> p kc n", n=512)
                    for mm in range(4):
                        m = q * 4 + mm
                        bk = bank()

                        def f(e, bk=bk, cs=cs, wv=wv, mm=mm):
                            ins = None
                            for kc in range(KC):
                                ins = e.matmul(banks[bk][:, :], lhsT=wv[:, kc, mm * 128:(mm + 1) * 128],
                                               rhs=ybuf[:, kc, cs], start=(kc == 0), stop=(kc == KC - 1))
                            return ins

                        S.op("pe", f, reads=[("ws", s)] + [("y", kc) for kc in range(KC)], writes=[B(bk)])
                        residual(1, 2, bk, m, tt)

        do_mod(0, [3])
        if cfg["attn"]:
            attention()
        else:
            do_mod(0, range(4, 6))
        if not cfg["attn"]:
            do_mod(0, range(6, 10))
        if cfg["ffn0"]:
            ffn(0)
        else:
            do_mod(0, range(10, 12))
        if not cfg["ffn0"]:
            do_mod(1, range(0, 4))
        if cfg["lru"]:
            lru()
        else:
            do_mod(1, range(4, 6))
        if not cfg["lru"]:
            do_mod(1, range(6, 10))
        if cfg["ffn1"]:
            ffn(1)
        else:
            do_mod(1, range(10, 12))
        S.barrier()

        yTs = [carve(arena, 0, F32, [128, KC, 512]), carve(arena, 32768, F32, [128, KC, 512])]
        ost = [carve(arena, 16384 + i * 4096, F32, [128, D]) for i in range(4)]
        fin_r = {0: stats(0), 1: stats(1)}
        for tt in range(3):
            cs = slice(tt * 512, (tt + 1) * 512)
            if tt == 1:
                fin_r[2] = stats(2)
            r = fin_r[tt]
            yb = tt % 2
            yT = yTs[yb]
            for kc in range(KC):
                S.op("dve", lambda e, kc=kc, r=r, cs=cs, yT=yT: e.scalar_tensor_tensor(
                    out=yT[:, kc, :], in0=X[:, kc, cs], scalar=vt(C_FG + kc), in1=rstd[r], op0=ALU.mult, op1=ALU.mult),
                    reads=[("X", kc, tt), ("rstd", r), "VT"], writes=[("yT", yb, kc)])
            for i in range(4):
                g = tt * 4 + i
                ob_ = g % 4
                o_ = ost[ob_]
                for hh in range(2):
                    bk = bank()

                    def f(e, bk=bk, hh=hh, i=i, yT=yT):
                        ins = None
                        for q in range(4):
                            ins = e.transpose(out=banks[bk][:, q * 128:(q + 1) * 128],
                                              in_=yT[:, hh * 4 + q, i * 128:(i + 1) * 128], identity=ident[:])
                        return ins

                    S.op("pe", f, reads=[("yT", yb, hh * 4 + q) for q in range(4)] + ["ident"], writes=[B(bk)])
                    copy_op(evac_eng(), o_[:, hh * 512:(hh + 1) * 512], banks[bk][:, :], [B(bk)], [("ost", ob_, hh)])
                S.dma("sp", lambda e, g=g, o_=o_: e.dma_start(out=y_d[g * 128:(g + 1) * 128, :], in_=o_),
                      reads=[("ost", ob_, 0), ("ost", ob_, 1)], out=True)
        S.finish()

        block = es.enter_context(nc.Block())

        @block.tensor
        def _(e):
            S.emit("pe", e)

        @block.scalar
        def _(e):
            S.emit("act", e)

        @block.vector
        def _(e):
            S.emit("dve", e)

        @block.gpsimd
        def _(e):
            S.emit("pool", e)

        @block.sync
        def _(e):
            S.emit("sp", e)
    return nc


def _colmask():
    cm = np.zeros((128, 64), np.float32)
    for cq in range(64):
        cs = min(max(cq - 8, 0), 48)
        for p in range(128):
            ck = p % 64
            if cs <= ck < cs + 16:
                cm[p, 63 - cq] = 1.0
    return cm


def make_in_maps(inp):
    f = lambda a: np.ascontiguousarray(np.asarray(a, dtype=np.float32))
    x_prompt, x_sample = f(inp["x_prompt"]), f(inp["x_sample"])
    c, c_ctx = f(inp["c"]), f(inp["c_ctx"])
    shared = {
        "ident": np.eye(128, dtype=np.float32),
        "cm": _colmask(),
        "rpb": f(inp["attn_rpb"])[0],
        "w_mod": f(inp["w_mod"]),
        "w_qkv": f(inp["attn_w_qkv"])[0],
        "w_o": f(inp["attn_w_o"])[0],
        "w_in": f(inp["lru_w_in"])[0],
        "w_a": f(inp["lru_w_a"])[0],
        "w_i": f(inp["lru_w_i"])[0],
        "w_out": f(inp["lru_w_out"])[0],
        "w_gu": f(inp["ffn_w_gu"]),
        "w_down": f(inp["ffn_w_down"]),
    }
    common_rows = [
        f(inp["b_mod"]).reshape(96, 128),
        f(inp["norm_g"]).reshape(32, 128),
        f(inp["final_g"]).reshape(8, 128),
        f(inp["lru_conv_w"])[0].reshape(32, 128),
        f(inp["lru_conv_b"])[0].reshape(8, 128),
        f(inp["lru_b_a"])[0].reshape(16, 128),
        f(inp["lru_b_i"])[0].reshape(16, 128),
        f(inp["lru_lam"])[0].reshape(16, 128),
    ]
    maps = []
    for i in range(NCORES):
        smalls = np.concatenate(common_rows + [c_ctx.reshape(8, 128), c[i].reshape(8, 128),
                                               f(inp["state_h"])[i, 0].reshape(16, 128)], axis=0)
        assert smalls.shape == (256, 128)
        m = dict(shared)
        m["x"] = np.concatenate([x_prompt[2 * i], x_prompt[2 * i + 1], x_sample[i]], axis=0)
        m["smalls"] = np.ascontiguousarray(smalls)
        m["ck"] = f(inp["cache_k"])[i, 0].reshape(512, D)
        m["cv"] = f(inp["cache_v"])[i, 0].reshape(512, D)
        maps.append(m)
    return maps


_NC_CACHE = {}


def run(inp, cfg=FULL):
    key = tuple(sorted(cfg.items()))
    if key not in _NC_CACHE:
        _NC_CACHE[key] = build(cfg)
    nc = _NC_CACHE[key]
    import os
    if os.environ.get("DBG1CORE"):
        res = run_bass_kernel_spmd(nc, make_in_maps(inp)[:1], core_ids=[0])
        rs = [res.results[0]] * NCORES
    else:
        res = run_bass_kernel_spmd(nc, make_in_maps(inp), core_ids=list(range(NCORES)))
        rs = res.results
    y_prompt = np.stack([rs[i // 2]["y"][(i % 2) * 256:(i % 2) * 256 + 256] for i in range(16)], axis=0)
    y_sample = np.stack([rs[i]["y"][512:1536] for i in range(8)], axis=0)
    nk = np.stack([rs[i // 2]["nk"][(i % 2) * 256:(i % 2) * 256 + 256] for i in range(16)], axis=0)
    nv = np.stack([rs[i // 2]["nv"][(i % 2) * 256:(i % 2) * 256 + 256] for i in range(16)], axis=0)
    nk = nk.reshape(16, 1, 256, 16, 64)
    nv = nv.reshape(16, 1, 256, 16, 64)
    nh = np.stack([rs[i // 2]["nh"].reshape(2, 2, 1024)[i % 2] for i in range(16)], axis=0).reshape(16, 1, 2, 1024)
    return (y_prompt.astype(np.float32), y_sample.astype(np.float32), nk.astype(np.float32),
            nv.astype(np.float32), nh.astype(np.float32))


def kernel(**inputs):
    return run(inputs, FULL)
```
